# Optimizing a Trainium2 kernel written in Bass

```python
import math
import jax, jax.numpy as jnp
from jax import lax
import numpy as np

D_MODEL = 4096
BATCH = 4
SEQ = 4096
DEPTH = 1

ATTN_HEADS = 8
ATTN_HEAD_DIM = 128
ATTN_WIDTH = ATTN_HEADS * 2 * ATTN_HEAD_DIM
CONV_WIDTH = D_MODEL - ATTN_WIDTH
CONV_GROUPS = 16
CONV_K = 3
IN_WIDTH = 3 * ATTN_WIDTH + 3 * CONV_WIDTH
D_FF = 11008
ROPE_THETA = 10000.0
Q_BLOCK = 128
EPS = 1e-6
N_MOD = 6

kernel_name = "hybrid_diffattn_shortconv_convffn_block"


def rmsnorm(x, g):
    xf = x.astype(jnp.float32)
    y = xf * lax.rsqrt(jnp.mean(xf * xf, axis=-1, keepdims=True) + EPS) * g.astype(jnp.float32)
    return y.astype(x.dtype)


def causal_dwconv(x, w):
    k = w.shape[0]
    s = x.shape[1]
    xp = jnp.pad(x, ((0, 0), (k - 1, 0), (0, 0)))
    y = xp[:, 0:s] * w[0]
    for j in range(1, k):
        y = y + xp[:, j:j + s] * w[j]
    return y


def rope_tables(positions, dim):
    inv_freq = ROPE_THETA ** (-jnp.arange(0, dim, 2, dtype=jnp.float32) / dim)
    ang = positions.astype(jnp.float32)[..., None] * inv_freq
    ang = jnp.concatenate([ang, ang], axis=-1)
    return jnp.cos(ang), jnp.sin(ang)


def apply_rope(t, cos, sin):
    cos = cos[:, :, None, None, :].astype(t.dtype)
    sin = sin[:, :, None, None, :].astype(t.dtype)
    half = t.shape[-1] // 2
    rot = jnp.concatenate([-t[..., half:], t[..., :half]], axis=-1)
    return t * cos + rot * sin


def diff_attention(q, k, v, lam):
    b, s, h, _, d = q.shape
    nb = s // Q_BLOCK
    scale = d ** -0.5
    qb = q.reshape(b, nb, Q_BLOCK, h, 2, d).transpose(1, 0, 2, 3, 4, 5)
    kpos = jnp.arange(s)

    def one_block(args):
        qi, i = args
        sc = jnp.einsum('bqhcd,bkhcd->bchqk', qi, k).astype(jnp.float32) * scale
        qpos = i * Q_BLOCK + jnp.arange(Q_BLOCK)
        mask = kpos[None, :] <= qpos[:, None]
        sc = jnp.where(mask, sc, jnp.finfo(jnp.float32).min)
        p = jax.nn.softmax(sc, axis=-1)
        a = p[:, 0] - lam * p[:, 1]
        return jnp.einsum('bhqk,bkhe->bqhe', a.astype(v.dtype), v)

    out = lax.map(one_block, (qb, jnp.arange(nb)))
    return out.transpose(1, 0, 2, 3, 4).reshape(b, s, h, 2 * d)


def setup_inputs(seed: int = 0) -> dict:
    key = jax.random.key(seed)
    ks = jax.random.split(key, 20)
    f32 = jnp.float32
    nrm = lambda k, shape, s: jax.random.normal(k, shape, f32) * s
    x = jax.random.normal(ks[0], (BATCH, SEQ, D_MODEL), f32)
    c = jax.random.normal(ks[1], (BATCH, D_MODEL), f32)
    offset = jax.random.randint(ks[2], (BATCH,), 0, 1024, dtype=jnp.int32)
    positions = offset[:, None] + jnp.arange(SEQ, dtype=jnp.int32)[None, :]
    return {
        "x": x,
        "c": c,
        "positions": positions,
        "w_ada": nrm(ks[3], (DEPTH, D_MODEL, N_MOD * D_MODEL), D_MODEL ** -0.5),
        "b_ada": nrm(ks[4], (DEPTH, N_MOD * D_MODEL), 0.02),
        "g_pre_mix": 1.0 + nrm(ks[5], (DEPTH, D_MODEL), 0.02),
        "g_post_mix": 1.0 + nrm(ks[6], (DEPTH, D_MODEL), 0.02),
        "w_in": nrm(ks[7], (DEPTH, D_MODEL, IN_WIDTH), D_MODEL ** -0.5),
        "lambda_q1": nrm(ks[8], (DEPTH, ATTN_HEAD_DIM), 0.1),
        "lambda_k1": nrm(ks[9], (DEPTH, ATTN_HEAD_DIM), 0.1),
        "lambda_q2": nrm(ks[10], (DEPTH, ATTN_HEAD_DIM), 0.1),
        "lambda_k2": nrm(ks[11], (DEPTH, ATTN_HEAD_DIM), 0.1),
        "g_subln": 1.0 + nrm(ks[12], (DEPTH, 2 * ATTN_HEAD_DIM), 0.02),
        "w_conv_mix": nrm(ks[13], (DEPTH, CONV_K, CONV_WIDTH), CONV_K ** -0.5),
        "w_out": nrm(ks[14], (DEPTH, D_MODEL, D_MODEL), D_MODEL ** -0.5),
        "g_pre_ffn": 1.0 + nrm(ks[15], (DEPTH, D_MODEL), 0.02),
        "g_post_ffn": 1.0 + nrm(ks[16], (DEPTH, D_MODEL), 0.02),
        "w_up": nrm(ks[17], (DEPTH, D_MODEL, 2 * D_FF), D_MODEL ** -0.5),
        "w_conv_ffn": nrm(ks[18], (DEPTH, CONV_K, 2 * D_FF), CONV_K ** -0.5),
        "w_down": nrm(ks[19], (DEPTH, D_FF, D_MODEL), D_FF ** -0.5),
    }


def reference(x, c, positions, w_ada, b_ada, g_pre_mix, g_post_mix, w_in, lambda_q1, lambda_k1,
              lambda_q2, lambda_k2, g_subln, w_conv_mix, w_out, g_pre_ffn, g_post_ffn, w_up,
              w_conv_ffn, w_down):
    b, s, _ = x.shape
    cos, sin = rope_tables(positions, ATTN_HEAD_DIM)
    cond = jax.nn.silu(c)
    a_w, c_w = ATTN_WIDTH, CONV_WIDTH
    splits = [a_w, 2 * a_w, 3 * a_w, 3 * a_w + c_w, 3 * a_w + 2 * c_w]
    for l in range(DEPTH):
        lam_init = 0.8 - 0.6 * math.exp(-0.3 * l)
        mod = jnp.einsum('bd,de->be', cond, w_ada[l]) + b_ada[l]
        sh1, sc1, gt1, sh2, sc2, gt2 = [m[:, None, :] for m in jnp.split(mod, N_MOD, axis=-1)]

        h = rmsnorm(x, g_pre_mix[l]) * (1.0 + sc1) + sh1
        proj = jnp.einsum('bsd,de->bse', h, w_in[l])
        q, k, v, gate_b, gate_c, hc = jnp.split(proj, splits, axis=-1)

        q = apply_rope(q.reshape(b, s, ATTN_HEADS, 2, ATTN_HEAD_DIM), cos, sin)
        k = apply_rope(k.reshape(b, s, ATTN_HEADS, 2, ATTN_HEAD_DIM), cos, sin)
        v = v.reshape(b, s, ATTN_HEADS, 2 * ATTN_HEAD_DIM)
        lam = (jnp.exp(jnp.sum(lambda_q1[l].astype(jnp.float32) * lambda_k1[l].astype(jnp.float32)))
               - jnp.exp(jnp.sum(lambda_q2[l].astype(jnp.float32) * lambda_k2[l].astype(jnp.float32)))
               + lam_init)
        attn = diff_attention(q, k, v, lam)
        attn = (rmsnorm(attn, g_subln[l]) * (1.0 - lam_init)).reshape(b, s, a_w)

        conv = gate_b * causal_dwconv(gate_c * hc, w_conv_mix[l])

        mixed = jnp.concatenate([attn.astype(conv.dtype), conv], axis=-1)
        y = jnp.einsum('bse,ed->bsd', mixed, w_out[l])
        x = x + gt1 * rmsnorm(y, g_post_mix[l])

        h2 = rmsnorm(x, g_pre_ffn[l]) * (1.0 + sc2) + sh2
        u = causal_dwconv(jnp.einsum('bsd,df->bsf', h2, w_up[l]), w_conv_ffn[l])
        u_gate, u_val = jnp.split(u, 2, axis=-1)
        f = jnp.einsum('bsf,fd->bsd', jax.nn.silu(u_gate) * u_val, w_down[l])
        x = x + gt2 * rmsnorm(f, g_post_ffn[l])
    return x
```

```python
import math
from contextlib import ExitStack
import numpy as np
import concourse.bass as bass
import concourse.mybir as mybir
from concourse.bass_utils import run_bass_kernel_spmd

F32 = mybir.dt.float32
BF16 = mybir.dt.bfloat16
I32 = mybir.dt.int32
AF = mybir.ActivationFunctionType
ALU = mybir.AluOpType
AX = mybir.AxisListType

EPS = 1e-6
ROPE_THETA = 10000.0
LAM_INIT = 0.8 - 0.6 * math.exp(-0.3 * 0)
FULL_CFG = dict(D=4096, S=4096, H=8, DFF=11008, B=4)


class Sem:
    def __init__(self, nc, name):
        self.h = nc.alloc_semaphore(name)
        self.n = 0
        self.name = name


class KB:
    def __init__(self, nc):
        self.nc = nc
        self.eng = {"pe": nc.tensor, "act": nc.scalar, "dve": nc.vector, "pool": nc.gpsimd, "sp": nc.sync}
        self.prog = {e: Sem(nc, "p_" + e) for e in ("pe", "act", "dve", "pool")}
        self.waited = {e: {} for e in self.eng}
        self.dsems = []
        self.nsem = 0

    def dsem(self, name):
        s = Sem(self.nc, f"d{self.nsem}_{name}")
        self.nsem += 1
        self.dsems.append(s)
        return s

    def mark(self, e, ins):
        s = self.prog[e]
        ins.then_inc(s.h, 1)
        s.n += 1
        return (s, s.n)

    def wait(self, e, tok):
        if tok is None:
            return
        if isinstance(tok, list):
            for t in tok:
                self.wait(e, t)
            return
        s, v = tok
        w = self.waited[e]
        if w.get(s.name, 0) >= v:
            return
        w[s.name] = v
        self.eng[e].wait_ge(s.h, v)

    def dma(self, q, out, in_, sem):
        ins = self.eng[q].dma_start(out=out, in_=in_)
        ins.then_inc(sem.h, 16)
        sem.n += 16
        return (sem, sem.n)

    def barrier(self, scratch):
        nc = self.nc
        toks = []
        toks.append(self.mark("act", nc.scalar.activation(out=scratch[:, 0:1], in_=scratch[:, 4:5], func=AF.Copy)))
        toks.append(self.mark("dve", nc.vector.memset(scratch[:, 1:2], 0.0)))
        toks.append(self.mark("pool", nc.gpsimd.memset(scratch[:, 2:3], 0.0)))
        toks.append((self.prog["pe"], self.prog["pe"].n))
        for s in self.dsems:
            if s.n:
                toks.append((s, s.n))
        for e in self.eng:
            self.wait(e, toks)


_UID = [0]


def sb(st, nc, name, shape, dt):
    _UID[0] += 1
    return st.enter_context(nc.sbuf_tensor(f"sb{_UID[0]}_{name}", list(shape), dt))


def psb(st, nc, name, shape, dt=F32):
    _UID[0] += 1
    return st.enter_context(nc.psum_tensor(f"ps{_UID[0]}_{name}", list(shape), dt))


def tok_tiles(n, first=None):
    out = []
    c = 0
    if first:
        out.append((0, first))
        c = first
    while c < n:
        w = min(512, n - c)
        out.append((c, w))
        c += w
    return out


def build(cfg):
    D, S, H, DFF = cfg["D"], cfg["S"], cfg["H"], cfg["DFF"]
    KC = D // 128
    TOK = S // 2
    NB = TOK // 128
    HW = 128
    HN = 32
    TOKL = HW + TOK
    NBL = NB + 1
    AW = H * 256
    CW = D - AW
    NCC = CW // 128
    FC = DFF // 128
    NTD = D // 512
    NTV = AW // 512
    NADA = 6 * D // 512
    SCALE = 128.0 ** -0.5

    nc = bass.Bass("TRN2", target_bir_lowering=False)
    k = KB(nc)
    pe, act, dve, pool = nc.tensor, nc.scalar, nc.vector, nc.gpsimd

    def din(name, shape, dt=F32):
        return nc.dram_tensor(name, list(shape), dt, kind="ExternalInput").ap()

    DEBUG = cfg.get("debug", False)
    STOP = cfg.get("stop", 99)

    def dscr(name, shape, dt):
        return nc.dram_tensor(name, list(shape), dt, kind=("ExternalOutput" if DEBUG else "Internal")).ap()

    x_own = din("x_own", [TOKL, D])
    x_pre = din("x_pre", [TOK, D])
    pos_own = din("pos_own", [TOKL], I32)
    pos_pre = din("pos_pre", [TOK], I32)
    cT_d = din("cT", [128, KC])
    flag_d = din("flag", [128, 1])
    ident_d = din("ident", [128, 128])
    rotm_d = din("rotm", [128, 128])
    invf_d = din("invf", [128, 1])
    tri_d = din("tri", [128, 128])
    wada_d = din("wada", [NADA, 128, KC, 512])
    badaT_d = din("badaT", [128, 6 * KC])
    gvec_d = din("gvec", [128, 4 * KC])
    lam_d = din("lamv", [4, 128])
    gsub_d = din("gsub", [256])
    wq_d = din("wq", [2 * H, 128, KC, 128])
    wk_d = din("wk", [2 * H, 128, KC, 128])
    wv_d = din("wv", [NTV, 128, KC, 512])
    wcv_d = din("wcv", [3 * NCC, 128, KC, 128])
    wcm_d = din("wcm", [128, NCC * 3])
    wout_d = din("wout", [NTD, 128, KC, 512])
    wup_d = din("wup", [2 * FC, 128, KC, 128])
    wcf_d = din("wcf", [128, 2 * FC * 3])
    wdn_d = din("wdn", [NTD, 128, FC, 512])
    out_d = nc.dram_tensor("out", [TOK, D], F32, kind="ExternalOutput").ap()

    kT_s = dscr("kT_s", [2 * H, 128, 2 * TOK], BF16)
    qT_s = dscr("qT_s", [2 * H, 128, TOKL], BF16)
    v_s = dscr("v_s", [2 * TOK, AW], BF16)
    def split_groups(nblk):
        ng = (nblk + 3) // 4
        base, rem = divmod(nblk, ng)
        sizes = [base + (1 if i < rem else 0) for i in range(ng)]
        out, b0 = [], 0
        for n_ in sizes:
            out.append((b0 * 128, n_ * 128))
            b0 += n_
        return out
    OG = split_groups(NBL)
    DG = split_groups(NB)
    mixT_s = dscr("mixT_s", [len(OG), 128, KC, 512], BF16)
    xmid_s = dscr("xmid_s", [TOK, D], F32)
    h2T_s = dscr("h2T_s", [KC, 128, TOKL], BF16)
    gT_s = dscr("gT_s", [len(DG), 128, FC, 512], BF16)
    ggrow_s = dscr("ggrow_s", [2, D], F32)
    cs_own_s = dscr("cs_own_s", [2, 128, TOKL], F32)
    cs_pre_s = dscr("cs_pre_s", [2, 128, TOK], F32)
    wout_c = dscr("wout_c", [NTD, 128, KC, 512], BF16)
    wdn_c = dscr("wdn_c", [NTD, 128, FC, 512], BF16)
    cache_sem = k.dsem("wcache")
    cache_jobs = []
    for nt_ in range(NTD):
        for k0 in range(0, KC, 8):
            cache_jobs.append((wout_c[nt_][:, k0:k0 + 8, :], wout_d[nt_][:, k0:k0 + 8, :]))

    def cache_step(n=1):
        for _ in range(n):
            if cache_jobs:
                dst_, src_ = cache_jobs.pop(0)
                k.dma("pool", dst_, src_, cache_sem)

    with ExitStack() as gst:
        P = lambda name, shape, dt=F32: sb(gst, nc, name, shape, dt)
        scratch = P("scratch", [128, 8])
        ident_f = P("ident_f", [128, 128])
        ident_b = P("ident_b", [128, 128], BF16)
        rotm_f = P("rotm_f", [128, 128])
        ones_f = P("ones_f", [128, 128])
        tri_b = P("tri_b", [128, 128], BF16)
        flag = P("flag_sb", [128, 1])
        invf = P("invf_sb", [128, 1])
        modA = P("modA", [128, 2 * KC])
        modB = P("modB", [128, 2 * KC])
        nlam = P("nlam", [128, 1])
        gsub = P("gsub_sb", [128, 256])
        wcm = P("wcm_sb", [128, NCC * 3])
        wcf = P("wcf_sb", [128, 2 * FC * 3])

        cs = k.dsem("const")
        toks = []
        toks.append(k.dma("sp", ident_f[:], ident_d, cs))
        toks.append(k.dma("sp", rotm_f[:], rotm_d, cs))
        toks.append(k.dma("sp", flag[:], flag_d, cs))
        toks.append(k.dma("sp", invf[:], invf_d, cs))
        toks.append(k.dma("sp", gsub[:], gsub_d.partition_broadcast(128), cs))
        toks.append(k.dma("sp", wcm[:], wcm_d, cs))
        toks.append(k.dma("sp", wcf[:], wcf_d, cs))
        ctok = toks[-1]
        k.wait("dve", ctok)
        k.mark("dve", dve.memset(scratch[:], 0.0))
        k.mark("dve", dve.tensor_copy(ident_b[:], ident_f[:]))
        k.mark("dve", dve.memset(ones_f[:], 1.0))
        k.mark("dve", dve.tensor_scalar(gsub[:], gsub[:], 1.0 - LAM_INIT, None, ALU.mult))

        with ExitStack() as st:
            A = lambda name, shape, dt=F32: sb(st, nc, name, shape, dt)
            tri_f = A("tri_f", [128, 128])
            lamb = A("lamb", [128, 4, 128])
            lprod = A("lprod", [128, 2, 128])
            lsum = A("lsum", [128, 4])
            c_sb = A("c_sb", [128, KC])
            cond = A("cond", [128, KC], BF16)
            bada = A("bada", [128, 6 * KC])
            gvec = A("gvec", [128, 4 * KC])
            modT = A("modT", [128, 6 * KC])
            ggT = A("ggT", [128, 2 * KC])
            diag = [A(f"diag{i}", [128, 128]) for i in range(2)]
            ggrow = A("ggrow", [1, D])
            wsl = [A(f"wada{i}", [128, KC, 512], BF16) for i in range(2)]
            pm = psb(st, nc, "pm", [128, 512])
            pr = [psb(st, nc, f"pr{i}", [128, 512]) for i in range(2)]
            s0 = k.dsem("p0")
            t1 = k.dma("sp", tri_f[:], tri_d, s0)
            t2 = k.dma("sp", lamb[:], lam_d.partition_broadcast(128), s0)
            t3 = k.dma("sp", c_sb[:], cT_d, s0)
            t4 = k.dma("sp", bada[:], badaT_d, s0)
            t5 = k.dma("sp", gvec[:], gvec_d, s0)
            k.wait("dve", [t1, t2, t3, t4, t5])
            k.mark("dve", dve.tensor_copy(tri_b[:], tri_f[:]))
            k.mark("dve", dve.tensor_tensor(lprod[:, 0, :], lamb[:, 0, :], lamb[:, 1, :], ALU.mult))
            tl = k.mark("dve", dve.tensor_tensor(lprod[:, 1, :], lamb[:, 2, :], lamb[:, 3, :], ALU.mult))
            k.wait("dve", tl)
            k.mark("dve", dve.reduce_sum(lsum[:, 0:1], lprod[:, 0, :], AX.X))
            tl = k.mark("dve", dve.reduce_sum(lsum[:, 1:2], lprod[:, 1, :], AX.X))
            k.wait("act", tl)
            tl = k.mark("act", act.activation(out=lsum[:, 2:4], in_=lsum[:, 0:2], func=AF.Exp))
            k.wait("dve", tl)
            tl = k.mark("dve", dve.tensor_tensor(lsum[:, 0:1], lsum[:, 3:4], lsum[:, 2:3], ALU.subtract))
            k.wait("dve", tl)
            k.mark("dve", dve.tensor_scalar(nlam[:], lsum[:, 0:1], -LAM_INIT, None, ALU.add))
            k.wait("act", t3)
            tcond = k.mark("act", act.activation(out=cond[:], in_=c_sb[:], func=AF.Silu))
            wsem = [k.dsem(f"wada{i}") for i in range(2)]
            wfree = [None, None]
            wtok = {}

            def issue_ada(et):
                s = et % 2
                k.wait("pool", wfree[s])
                wtok[et] = k.dma("pool", wsl[s][:], wada_d[et], wsem[s])

            issue_ada(0)
            for ri, (pos_d, n, dst) in enumerate(((pos_own, TOKL, cs_own_s), (pos_pre, TOK, cs_pre_s))):
                pi_ = A(f"pi_{ri}", [128, n], I32)
                ang = A(f"ang{ri}", [128, n], F32)
                tmp = A(f"tmp{ri}", [128, n], F32)
                tabs = [A(f"tab{ri}_{i}", [128, n], F32) for i in range(2)]
                rs = k.dsem("rope")
                tpz = k.dma("sp", pi_[:], pos_d.partition_broadcast(128), rs)
                k.wait("dve", tpz)
                t = k.mark("dve", dve.tensor_copy(ang[:], pi_[:]))
                k.wait("dve", t)
                t = k.mark("dve", dve.tensor_scalar(ang[:], ang[:], invf[:, 0:1], None, ALU.mult))
                for ti, shift in enumerate((math.pi / 2, 0.0)):
                    tab = tabs[ti]
                    k.wait("dve", t)
                    t = k.mark("dve", dve.tensor_scalar(tab[:], ang[:], float(shift), None, ALU.add))
                    k.wait("dve", t)
                    t = k.mark("dve", dve.tensor_scalar(tmp[:], tab[:], float(1.0 / (2 * math.pi)), None, ALU.mult))
                    k.wait("dve", t)
                    t = k.mark("dve", dve.tensor_copy(pi_[:], tmp[:]))
                    k.wait("dve", t)
                    t = k.mark("dve", dve.tensor_copy(tmp[:], pi_[:]))
                    k.wait("dve", t)
                    t = k.mark("dve", dve.scalar_tensor_tensor(tab[:], tmp[:], -float(2 * math.pi), tab[:], ALU.mult, ALU.add))
                    k.wait("dve", t)
                    t = k.mark("dve", dve.tensor_scalar(tmp[:], tab[:], float(math.pi), float(2 * math.pi), ALU.is_gt, ALU.mult))
                    k.wait("dve", t)
                    t = k.mark("dve", dve.tensor_tensor(tab[:], tab[:], tmp[:], ALU.subtract))
                    k.wait("dve", t)
                    t = k.mark("dve", dve.tensor_scalar(tmp[:], tab[:], -float(math.pi), float(2 * math.pi), ALU.is_lt, ALU.mult))
                    k.wait("dve", t)
                    t = k.mark("dve", dve.tensor_tensor(tab[:], tab[:], tmp[:], ALU.add))
                    k.wait("act", t)
                    t2_ = k.mark("act", act.activation(out=tab[:], in_=tab[:], func=AF.Sin))
                    k.wait("sp", t2_)
                    k.dma("sp", dst[ti], tab[:], rs)
            k.wait("pe", tcond)
            lastp = None
            for et in range(NADA):
                if et + 1 < NADA:
                    issue_ada(et + 1)
                s = et % 2
                k.wait("pe", wtok[et])
                for jj in range(4):
                    j = et * 4 + jj
                    for kc in range(KC):
                        ins = pe.matmul(pm[:, j:j + 1], wsl[s][:, kc, jj * 128:(jj + 1) * 128], cond[:, kc:kc + 1],
                                        start=(kc == 0), stop=(kc == KC - 1))
                lastp = k.mark("pe", ins)
                wfree[s] = lastp
            k.wait("dve", lastp)
            tm = k.mark("dve", dve.tensor_tensor(modT[:], pm[:, 0:6 * KC], bada[:], ALU.add))
            k.wait("dve", tm)
            for i in range(2):
                o = 3 * i * KC
                k.mark("dve", dve.tensor_copy(modB[:, i * KC:(i + 1) * KC], modT[:, o:o + KC]))
                k.mark("dve", dve.scalar_tensor_tensor(modA[:, i * KC:(i + 1) * KC], modT[:, o + KC:o + 2 * KC], 1.0,
                                                       gvec[:, 2 * i * KC:(2 * i + 1) * KC], ALU.add, ALU.mult))
                tg = k.mark("dve", dve.tensor_tensor(ggT[:, i * KC:(i + 1) * KC], modT[:, o + 2 * KC:o + 3 * KC],
                                                     gvec[:, (2 * i + 1) * KC:(2 * i + 2) * KC], ALU.mult))
            k.wait("dve", tg)
            dfree = [None, None]
            pfree = [None, None]
            gs = k.dsem("ggrow")
            for i in range(2):
                evs = []
                for kc in range(KC):
                    s = kc % 2
                    k.wait("dve", dfree[s])
                    td = k.mark("dve", dve.tensor_scalar(diag[s][:], ident_f[:], ggT[:, i * KC + kc:i * KC + kc + 1], None, ALU.mult))
                    k.wait("pe", td)
                    k.wait("pe", pfree[s])
                    tp = k.mark("pe", pe.matmul(pr[s][0:1, 0:128], ones_f[:, 0:1], diag[s][:], start=True, stop=True))
                    dfree[s] = tp
                    k.wait("act", tp)
                    te = k.mark("act", act.activation(out=ggrow[0:1, kc * 128:(kc + 1) * 128], in_=pr[s][0:1, 0:128], func=AF.Copy))
                    pfree[s] = te
                    evs.append(te)
                k.wait("sp", evs)
                tdm = k.dma("sp", ggrow_s[i:i + 1, :], ggrow[0:1, :], gs)
                k.wait("act", tdm)
            k.barrier(scratch)

        def nt_alloc(st, name):
            return dict(xn=[sb(st, nc, f"{name}_xn{i}", [128, D], BF16) for i in range(2)],
                        sm=[sb(st, nc, f"{name}_sm{i}", [128, 4], F32) for i in range(2)],
                        xn_free=[None, None], pt_free=[None, None], gi=0, cnt=0)

        def nt_block(N_, b, xap, rdy, done_cb, dstT, mi, pst, store_fn=None):
            s = N_["cnt"] % 2
            N_["cnt"] += 1
            xn, sm = N_["xn"], N_["sm"]
            k.wait("dve", N_["xn_free"][s])
            tz = k.mark("dve", dve.memset(sm[s][:], 0.0))
            k.wait("act", [rdy, tz, N_["xn_free"][s]])
            t1 = k.mark("act", act.activation(out=xn[s][:], in_=xap, func=AF.Square, accum_out=sm[s][:, 0:1]))
            k.wait("act", t1)
            t2 = k.mark("act", act.activation(out=sm[s][:, 1:2], in_=sm[s][:, 0:1], func=AF.Sqrt, scale=1.0 / D, bias=EPS))
            k.wait("dve", t2)
            t3 = k.mark("dve", dve.reciprocal(sm[s][:, 2:3], sm[s][:, 1:2]))
            k.wait("act", t3)
            t4 = k.mark("act", act.activation(out=xn[s][:], in_=xap, func=AF.Identity, scale=sm[s][:, 2:3]))
            done_cb(t4)
            ev_all = []
            for g0 in range(0, KC, 4):
                gi = N_["gi"]
                pt = pst[gi % 2]
                k.wait("pe", [N_["pt_free"][gi % 2], t4])
                n4 = min(4, KC - g0)
                for j in range(n4):
                    kc = g0 + j
                    ins = pe.transpose(pt[:, j * 128:(j + 1) * 128], xn[s][:, kc * 128:(kc + 1) * 128], ident_b[:])
                tp = k.mark("pe", ins)
                evs = []
                for j in range(n4):
                    kc = g0 + j
                    a_ap = modA[:, mi * KC + kc:mi * KC + kc + 1]
                    b_ap = modB[:, mi * KC + kc:mi * KC + kc + 1]
                    o_ap = dstT(kc, b)
                    if gi % 2 == 0:
                        k.wait("dve", tp)
                        evs.append(k.mark("dve", dve.tensor_scalar(o_ap, pt[:, j * 128:(j + 1) * 128], a_ap, b_ap, ALU.mult, ALU.add)))
                    else:
                        k.wait("act", tp)
                        evs.append(k.mark("act", act.activation(out=o_ap, in_=pt[:, j * 128:(j + 1) * 128], func=AF.Identity,
                                                                scale=a_ap, bias=b_ap)))
                N_["pt_free"][gi % 2] = evs
                ev_all += evs
                N_["gi"] += 1
            N_["xn_free"][s] = tp
            if store_fn is not None:
                store_fn(b, ev_all)

        def norm_transpose(st, name, nblk, x_dram, dstT, mi):
            NG = (KC + 3) // 4
            nbk = min(8, NG)
            pst = [psb(st, nc, f"{name}_pst{i}", [128, 1024], BF16) for i in range(nbk)]
            xb = [sb(st, nc, f"{name}_xb{i}", [128, D], F32) for i in range(2)]
            xn = [sb(st, nc, f"{name}_xn{i}", [128, D], BF16) for i in range(2)]
            sm = [sb(st, nc, f"{name}_sm{i}", [128, 4], F32) for i in range(2)]
            xsem = [k.dsem(f"{name}_x{i}") for i in range(2)]
            xfree = [None, None]
            xn_free = [None, None]
            pt_free = [None] * nbk
            ltok = {}

            def issue(b):
                s_ = b % 2
                k.wait("sp", xfree[s_])
                ltok[b] = k.dma("sp", xb[s_][:], x_dram[b * 128:(b + 1) * 128, :], xsem[s_])

            def front_a(b):
                s_ = b % 2
                k.wait("dve", [xn_free[s_], ltok[b]])
                tz = k.mark("dve", dve.memset(sm[s_][:], 0.0))
                k.wait("dve", tz)
                t1 = k.mark("dve", dve.scalar_tensor_tensor(xn[s_][:], xb[s_][:], 1.0, xb[s_][:], ALU.mult, ALU.mult,
                                                            accum_out=sm[s_][:, 0:1]))
                k.wait("act", t1)
                return k.mark("act", act.activation(out=sm[s_][:, 1:2], in_=sm[s_][:, 0:1], func=AF.Sqrt, scale=1.0 / D, bias=EPS))

            def front_b(b, t2):
                s_ = b % 2
                k.wait("dve", t2)
                return k.mark("dve", dve.reciprocal(sm[s_][:, 2:3], sm[s_][:, 1:2]))

            def front(b):
                return front_b(b, front_a(b))

            issue(0)
            if nblk > 1:
                issue(1)
            t3 = front(0)
            gcount = 0
            for b in range(nblk):
                s_ = b % 2
                k.wait("act", t3)
                t4 = k.mark("act", act.activation(out=xn[s_][:], in_=xb[s_][:], func=AF.Identity, scale=sm[s_][:, 2:3]))
                xfree[s_] = t4
                if b + 2 < nblk:
                    issue(b + 2)
                groups = []
                for g0 in range(0, KC, 4):
                    bk = gcount % nbk
                    gcount += 1
                    k.wait("pe", [pt_free[bk], t4])
                    n4 = min(4, KC - g0)
                    for j in range(n4):
                        kc = g0 + j
                        ins = pe.transpose(pst[bk][:, j * 128:(j + 1) * 128], xn[s_][:, kc * 128:(kc + 1) * 128], ident_b[:])
                    tp = k.mark("pe", ins)
                    groups.append((g0, n4, bk, tp))
                xn_free[s_] = tp
                if b + 1 < nblk:
                    t3 = front(b + 1)
                for gi_, (g0, n4, bk, tp) in enumerate(groups):
                    evs = []
                    for j in range(n4):
                        kc = g0 + j
                        a_ap = modA[:, mi * KC + kc:mi * KC + kc + 1]
                        b_ap = modB[:, mi * KC + kc:mi * KC + kc + 1]
                        o_ap = dstT(kc, b)
                        if gi_ % 2 == 0:
                            k.wait("dve", tp)
                            evs.append(k.mark("dve", dve.tensor_scalar(o_ap, pst[bk][:, j * 128:(j + 1) * 128], a_ap, b_ap, ALU.mult, ALU.add)))
                        else:
                            k.wait("act", tp)
                            evs.append(k.mark("act", act.activation(out=o_ap, in_=pst[bk][:, j * 128:(j + 1) * 128], func=AF.Identity,
                                                                    scale=a_ap, bias=b_ap)))
                    pt_free[bk] = evs[-1]

        def dram_loader(st, x_dram, nblk, name):
            xb = [sb(st, nc, f"{name}_xb{i}", [128, D], F32) for i in range(2)]
            sems = [k.dsem(f"{name}{i}") for i in range(2)]
            free = [None, None]
            toks = {}

            def issue(b):
                s = b % 2
                k.wait("sp", free[s])
                toks[b] = k.dma("sp", xb[s][:], x_dram[b * 128:(b + 1) * 128, :], sems[s])

            issue(0)

            def load_fn(b):
                if b + 1 < nblk:
                    issue(b + 1)
                s = b % 2

                def done(t):
                    free[s] = t
                return xb[s][:], toks[b], done
            return load_fn

        def fm_gemm(st, name, chunks, KCx, rhs_fn, tiles, banks, callback, nslots=3, flush=None, bg=0):
            wsl = [sb(st, nc, f"{name}_w{i}", [128, KCx, 128], BF16) for i in range(nslots)]
            wsem = [k.dsem(f"{name}_w{i}") for i in range(nslots)]
            wfree = [None] * nslots
            wtok = {}
            bfree = [None] * len(banks)

            def issue(ci):
                s = ci % nslots
                k.wait("pool", wfree[s])
                wtok[ci] = k.dma("pool", wsl[s][:], chunks[ci][0], wsem[s])
                if bg:
                    cache_step(bg)

            for ci in range(min(nslots - 1, len(chunks))):
                issue(ci)
            bi = 0
            pending = None
            for ci, (_, e) in enumerate(chunks):
                if ci + nslots - 1 < len(chunks):
                    issue(ci + nslots - 1)
                s = ci % nslots
                k.wait("pe", wtok[ci])
                for ti, (c0, w) in enumerate(tiles):
                    bk = bi % len(banks)
                    bi += 1
                    k.wait("pe", bfree[bk])
                    for kc in range(KCx):
                        ins = pe.matmul(banks[bk][:, 0:w], wsl[s][:, kc, :], rhs_fn(kc, c0, w), start=(kc == 0), stop=(kc == KCx - 1))
                    tp = k.mark("pe", ins)
                    if ti == len(tiles) - 1:
                        wfree[s] = tp
                    fr, defer = callback(ci, e, ti, c0, w, banks[bk], tp)
                    bfree[bk] = fr
                    if pending is not None:
                        pending()
                    pending = defer
            if pending is not None:
                pending()
            if flush is not None:
                flush()

        def tm_gemm(st, name, w_d, KCx, groups, lhs_fn, group_load, banks, callback, group_done=None, ksub=4, nring=4, NT=None, nt_hook=None, filler=None):
            ring = [sb(st, nc, f"{name}_r{i}", [128, ksub, 512], BF16) for i in range(nring)]
            rsem = [k.dsem(f"{name}_r{i}") for i in range(nring)]
            rfree = [None] * nring
            subs = [(k0, min(ksub, KCx - k0)) for k0 in range(0, KCx, ksub)]
            seq = [(g, nt, si) for g in range(len(groups)) for nt in range(NT) for si in range(len(subs))]
            rtok = {}

            def issue(i):
                g, nt, si = seq[i]
                k0, nk = subs[si]
                s = i % nring
                k.wait("pool", rfree[s])
                rtok[i] = k.dma("pool", ring[s][:, 0:nk, :], w_d[nt][:, k0:k0 + nk, :], rsem[s])

            for i in range(min(nring - 1, len(seq))):
                issue(i)
            bfree = [None] * len(banks)
            bnext = 0
            i = 0
            gtok = group_load(0) if group_load is not None else None
            for g, blks in enumerate(groups):
                k.wait("pe", gtok)
                for nt in range(NT):
                    mybanks = []
                    for _ in blks:
                        mybanks.append(bnext % len(banks))
                        bnext += 1
                    last = {}
                    for si, (k0, nk) in enumerate(subs):
                        if i + nring - 1 < len(seq):
                            issue(i + nring - 1)
                        s = i % nring
                        k.wait("pe", rtok[i])
                        for bi, blk in enumerate(blks):
                            bk = mybanks[bi]
                            if si == 0:
                                k.wait("pe", bfree[bk])
                            for kk in range(nk):
                                kc = k0 + kk
                                ins = pe.matmul(banks[bk][:, :], lhs_fn(kc, blk, bi), ring[s][:, kk, :], start=(kc == 0), stop=(kc == KCx - 1))
                            if si == len(subs) - 1:
                                last[bi] = k.mark("pe", ins)
                                if bi == len(blks) - 1:
                                    rfree[s] = last[bi]
                            else:
                                if bi == len(blks) - 1:
                                    rfree[s] = k.mark("pe", ins)
                                if filler is not None:
                                    filler(nt)
                        i += 1
                    for bi, blk in enumerate(blks):
                        bfree[mybanks[bi]] = callback(g, blk, bi, nt, banks[mybanks[bi]], last[bi])
                    if nt_hook is not None:
                        nt_hook(g, nt)
                if group_load is not None and g + 1 < len(groups):
                    gtok = group_load(g + 1)
                if group_done is not None:
                    group_done(g, blks)

        def qk_rope_pass(st, name, hT, ntok, tiles, cs_d, jobs):
            cosT = sb(st, nc, name + "_cos", [128, ntok], F32)
            sinT = sb(st, nc, name + "_sin", [128, ntok], F32)
            t32 = [sb(st, nc, f"{name}_t32{i}", [128, 512], F32) for i in range(2)]
            tA = [sb(st, nc, f"{name}_tA{i}", [128, 512], F32) for i in range(2)]
            tB = [sb(st, nc, f"{name}_tB{i}", [128, 512], F32) for i in range(2)]
            stage = [sb(st, nc, f"{name}_st{i}", [128, ntok], BF16) for i in range(2)]
            banks = [psb(st, nc, f"{name}_b{i}", [128, 512]) for i in range(4)]
            rb = [psb(st, nc, f"{name}_rb{i}", [128, 512]) for i in range(2)]
            csem = k.dsem(name + "_cs")
            tc1 = k.dma("sp", cosT[:], cs_d[0], csem)
            tc2 = k.dma("sp", sinT[:], cs_d[1], csem)
            ssem = [k.dsem(f"{name}_st{i}") for i in range(2)]
            state = dict(cnt=0, t32_free=[None, None], rb_free=[None, None], tA_free=[None, None], tB_free=[None, None],
                         st_free=[None, None], nch=0, sums=[])
            chunks = []
            for (w_d, cl, dst_fn) in jobs:
                chunks += [(w_d[e], dst_fn(e)) for e in cl]

            def cb(ci, dst, ti, c0, w, bank, tp):
                s = state["cnt"] % 2
                state["cnt"] += 1
                k.wait("act", [tp, state["t32_free"][s]])
                ta = k.mark("act", act.activation(out=t32[s][:, 0:w], in_=bank[:, 0:w], func=AF.Copy))

                def defer():
                    ss_ = state["nch"] % 2
                    k.wait("pe", [ta, state["rb_free"][s]])
                    tr = k.mark("pe", pe.matmul(rb[s][:, 0:w], rotm_f[:], t32[s][:, 0:w], start=True, stop=True))
                    k.wait("dve", [ta, tc1, state["tA_free"][s]])
                    t_a = k.mark("dve", dve.tensor_tensor(tA[s][:, 0:w], t32[s][:, 0:w], cosT[:, c0:c0 + w], ALU.mult))
                    k.wait("dve", [tr, tc2, state["tB_free"][s]])
                    t_b = k.mark("dve", dve.tensor_tensor(tB[s][:, 0:w], rb[s][:, 0:w], sinT[:, c0:c0 + w], ALU.mult))
                    state["rb_free"][s] = t_b
                    state["t32_free"][s] = [t_a, tr]
                    k.wait("dve", [t_a, t_b])
                    if ti == 0:
                        k.wait("dve", state["st_free"][ss_])
                    t_s = k.mark("dve", dve.tensor_tensor(stage[ss_][:, c0:c0 + w], tA[s][:, 0:w], tB[s][:, 0:w], ALU.add))
                    state["tA_free"][s] = t_s
                    state["tB_free"][s] = t_s
                    state["sums"].append(t_s)
                    if ti == len(tiles) - 1:
                        k.wait("sp", state["sums"])
                        state["sums"] = []
                        state["st_free"][ss_] = k.dma("sp", dst, stage[ss_][:], ssem[ss_])
                        state["nch"] += 1
                return ta, defer
            fm_gemm(st, name, chunks, KC, lambda kc, c0, w: hT[:, kc, c0:c0 + w], tiles, banks, cb, nslots=2)

        def v_pass(st, name, hT, blks_cols, dst_row0, use_flag):
            banks = [psb(st, nc, f"{name}_b{i}", [128, 512]) for i in range(8)]
            vst = [sb(st, nc, f"{name}_vs{i}", [128, 512], BF16) for i in range(4)]
            vsem = [k.dsem(f"{name}_vs{i}") for i in range(4)]
            vfree = [None] * 4
            cnt = [0]
            nb_ = len(blks_cols)
            groups = [list(range(g0, min(g0 + 6, nb_))) for g0 in range(0, nb_, 6)]

            def cb(g, blk, bi, nt, bank, tp):
                s = cnt[0] % 4
                cnt[0] += 1
                k.wait("act", [tp, vfree[s]])
                if use_flag:
                    te = k.mark("act", act.activation(out=vst[s][:], in_=bank[:, :], func=AF.Identity, scale=flag[:, 0:1]))
                else:
                    te = k.mark("act", act.activation(out=vst[s][:], in_=bank[:, :], func=AF.Copy))
                k.wait("sp", te)
                r0 = dst_row0 + blk * 128
                vfree[s] = k.dma("sp", v_s[r0:r0 + 128, nt * 512:(nt + 1) * 512], vst[s][:], vsem[s])
                return te
            tm_gemm(st, name, wv_d, KC, groups, lambda kc, blk, bi: hT[:, kc, blks_cols[blk]:blks_cols[blk] + 128], None, banks, cb,
                    ksub=8 if KC >= 8 else KC, nring=4, NT=NTV)

        if STOP < 1:
            return nc
        with ExitStack() as st1:
            hTp = sb(st1, nc, "hTp", [128, KC, TOK], BF16)
            with ExitStack() as st:
                norm_transpose(st, "p1n", NB, x_pre, lambda kc, b: hTp[:, kc, b * 128:(b + 1) * 128], 0)
                k.barrier(scratch)
            if STOP < 1.3:
                return nc
            with ExitStack() as st:
                qk_rope_pass(st, "p1k", hTp, TOK, tok_tiles(TOK), cs_pre_s,
                             [(wk_d, list(range(2 * H)), lambda e: kT_s[e][:, 0:TOK])])
                k.barrier(scratch)
            if STOP < 1.6:
                return nc
            with ExitStack() as st:
                v_pass(st, "p1v", hTp, [b * 128 for b in range(NB)], 0, True)
                k.barrier(scratch)

        if STOP < 2:
            return nc
        with ExitStack() as st2:
            hT = sb(st2, nc, "hT", [128, KC, TOKL], BF16)
            with ExitStack() as st:
                norm_transpose(st, "p2n", NBL, x_own, lambda kc, b: hT[:, kc, b * 128:(b + 1) * 128], 0)
                k.barrier(scratch)
            tiles_l = tok_tiles(TOKL, first=HW)
            with ExitStack() as st:
                qk_rope_pass(st, "p2qk", hT, TOKL, tiles_l, cs_own_s,
                             [(wq_d, list(range(2 * H)), lambda e: qT_s[e]),
                              (wk_d, list(range(2 * H)), lambda e: kT_s[e][:, TOK - HW:2 * TOK])])
                k.barrier(scratch)
            with ExitStack() as st:
                Cb = sb(st, nc, "cv_C", [128, TOKL], F32)
                mb = sb(st, nc, "cv_m", [128, TOKL + 2], F32)
                stg = [sb(st, nc, f"cv_st{i}", [128, TOKL], BF16) for i in range(2)]
                banks = [psb(st, nc, f"cv_b{i}", [128, 512]) for i in range(6)]
                ssem = [k.dsem(f"cv_st{i}") for i in range(2)]
                S_ = dict(C_free=None, m_free=None, st_free=[None, None], Ctoks=[], mtoks=[], Btoks=[], convtok=None)
                tiles_c = [(HW - HN, HN)] + tiles_l[1:]
                k.mark("dve", dve.memset(mb[:], 0.0))
                k.mark("dve", dve.memset(Cb[:], 0.0))
                k.mark("dve", dve.memset(stg[0][:], 0.0))
                tz = k.mark("dve", dve.memset(stg[1][:], 0.0))
                k.wait("act", tz)

                def cb(ci, e, ti, c0, w, bank, tp):
                    j, kind = divmod(e, 3)
                    if kind == 0:
                        k.wait("act", [tp, S_["C_free"]] if ti == 0 else tp)
                        t = k.mark("act", act.activation(out=Cb[:, c0:c0 + w], in_=bank[:, 0:w], func=AF.Copy))
                        S_["Ctoks"].append(t)
                        return t, None
                    if kind == 1:
                        k.wait("dve", [tp, tz] + S_["Ctoks"] + ([S_["m_free"]] if ti == 0 else []))
                        t = k.mark("dve", dve.tensor_tensor(mb[:, 2 + c0:2 + c0 + w], bank[:, 0:w], Cb[:, c0:c0 + w], ALU.mult))
                        S_["mtoks"].append(t)
                        if ti == len(tiles_l) - 1:
                            k.wait("dve", S_["mtoks"])
                            t0 = k.mark("dve", dve.tensor_scalar(mb[:, 2:2 + HW], mb[:, 2:2 + HW], flag[:, 0:1], None, ALU.mult))
                            k.wait("dve", t0)
                            wj = lambda tap: wcm[:, j * 3 + tap:j * 3 + tap + 1]
                            t1 = k.mark("dve", dve.tensor_scalar(Cb[:, :], mb[:, 2:2 + TOKL], wj(2), None, ALU.mult))
                            k.wait("dve", t1)
                            t2 = k.mark("dve", dve.scalar_tensor_tensor(Cb[:, :], mb[:, 1:1 + TOKL], wj(1), Cb[:, :], ALU.mult, ALU.add))
                            k.wait("dve", t2)
                            t3 = k.mark("dve", dve.scalar_tensor_tensor(Cb[:, :], mb[:, 0:TOKL], wj(0), Cb[:, :], ALU.mult, ALU.add))
                            S_["convtok"] = t3
                            S_["m_free"] = t3
                            S_["Ctoks"] = []
                            S_["mtoks"] = []
                        return t, None
                    s = j % 2
                    k.wait("dve", [tp, S_["convtok"]] + ([S_["st_free"][s]] if ti == 0 else []))
                    t = k.mark("dve", dve.tensor_tensor(stg[s][:, c0:c0 + w], bank[:, 0:w], Cb[:, c0:c0 + w], ALU.mult))
                    S_["Btoks"].append(t)
                    if ti == len(tiles_l) - 1:
                        k.wait("sp", S_["Btoks"])
                        for g_, (gs_, gn_) in enumerate(OG):
                            S_["st_free"][s] = k.dma("sp", mixT_s[g_][:, AW // 128 + j, 0:gn_], stg[s][:, gs_:gs_ + gn_], ssem[s])
                        S_["C_free"] = S_["Btoks"][-1]
                        S_["C_free"] = list(S_["Btoks"])
                        S_["Btoks"] = []
                    return t, None
                fm_gemm(st, "cv", [(wcv_d[e], e) for e in range(3 * NCC)], KC, lambda kc, c0, w: hT[:, kc, c0:c0 + w], tiles_c, banks, cb, bg=2)
                while cache_jobs:
                    cache_step()
                k.barrier(scratch)
            with ExitStack() as st:
                v_pass(st, "p2v", hT, [HW + b * 128 for b in range(NB)], TOK, False)
                k.barrier(scratch)

        if STOP < 3:
            return nc
        with ExitStack() as st:
            A = lambda name, shape, dt=F32: sb(st, nc, name, shape, dt)
            NKB = 2 * NB
            kTb = [A(f"at_k{i}", [128, 2, 2 * TOK], BF16) for i in range(2)]
            qTb = [A(f"at_q{i}", [128, 2, TOKL], BF16) for i in range(2)]
            va = [A(f"at_v{i}", [128, NKB, 257], BF16) for i in range(2)]
            mixst = [A(f"at_m{i}", [128, 2, TOKL], BF16) for i in range(2)]
            pT = [A(f"at_p{i}", [128, 512], BF16) for i in range(3)]
            Osb = [[A(f"at_o{c}{q}", [128, 257]) for q in range(4)] for c in range(2)]
            sm = [A(f"at_sm{q}", [128, 8]) for q in range(4)]
            ta_ = [A(f"at_ta{q}", [128, 256]) for q in range(4)]
            tb_ = [A(f"at_tb{q}", [128, 256]) for q in range(4)]
            ssall = A("at_ssall", [128, 12])
            ssall_free = [None]
            atb = [A(f"at_ab{q}", [128, 256], BF16) for q in range(4)]
            sbank = [psb(st, nc, f"at_sb{i}", [128, 512]) for i in range(3)]
            obank = [psb(st, nc, f"at_ob{i}", [128, 512]) for i in range(4)]
            tbank = [psb(st, nc, f"at_tb{i}", [128, 1024], BF16) for i in range(1)]
            lsem = [k.dsem(f"at_ld{i}") for i in range(2)]
            msem = [k.dsem(f"at_ms{i}") for i in range(2)]
            for i in range(2):
                k.mark("pool", pool.memset(va[i][:, :, 256:257], 1.0))
                tv = k.mark("pool", pool.tensor_scalar(va[i][:, 0:NB, 256:257], va[i][:, 0:NB, 256:257], flag[:, 0:1], None, ALU.mult))
            head_free = [None, None]
            mix_free = [None, None]
            ltok = {}

            def issue_head(h):
                s = h % 2
                k.wait("sp", head_free[s])
                k.dma("sp", kTb[s][:], kT_s[2 * h:2 * h + 2].rearrange("c p t -> p c t"), lsem[s])
                k.dma("sp", qTb[s][:], qT_s[2 * h:2 * h + 2].rearrange("c p t -> p c t"), lsem[s])
                ltok[h] = k.dma("sp", va[s][:, :, 0:256], v_s[:, h * 256:(h + 1) * 256].rearrange("(kb p) e -> p kb e", p=128), lsem[s])

            issue_head(0)
            qtiles = [(0, HW, [NB - 1])]
            for (c0, w) in tok_tiles(TOK):
                qtiles.append((HW + c0, w, [NB + (c0 + i * 128) // 128 for i in range(w // 128)]))
            sfree = [None] * 3
            pfree = [None] * 3
            ofree = [None] * 4
            osb_free = [[None] * 4 for _ in range(2)]
            tfree = [None] * 1
            atb_free = [None] * 4
            comb_free = [None] * 4
            sidx = [0]
            for h in range(H):
                s = h % 2
                if h + 1 < H:
                    issue_head(h + 1)
                k.wait("pe", [ltok[h], tv])
                lastpe = None
                mixtoks = []
                epi_q = []
                st2_q = []
                for (c0, w, diags) in qtiles:
                    nqb = len(diags)
                    kmax = diags[-1]
                    for c in range(2):
                        pend = []
                        for kb in range(kmax + 1):
                            qb0 = 0
                            while diags[qb0] < kb:
                                qb0 += 1
                            qoff = qb0 * 128
                            r = sidx[0] % 3
                            sidx[0] += 1
                            k.wait("pe", [sfree[r]])
                            tS = k.mark("pe", pe.matmul(sbank[r][:, qoff:w], kTb[s][:, c, kb * 128:(kb + 1) * 128],
                                                        qTb[s][:, c, c0 + qoff:c0 + w], start=True, stop=True))
                            k.wait("act", [tS, pfree[r]])
                            tE = k.mark("act", act.activation(out=pT[r][:, qoff:w], in_=sbank[r][:, qoff:w], func=AF.Exp, scale=SCALE))
                            sfree[r] = tE
                            tP = tE
                            if kb in diags:
                                qd = diags.index(kb)
                                k.wait("dve", tE)
                                tP = k.mark("dve", dve.tensor_tensor(pT[r][:, qd * 128:(qd + 1) * 128], pT[r][:, qd * 128:(qd + 1) * 128],
                                                                     tri_b[:], ALU.mult))

                            def do_pv(kb=kb, qb0=qb0, r=r, tP=tP, tE=tE):
                                k.wait("pe", [tP, tE])
                                for qb in range(qb0, nqb):
                                    if kb == 0:
                                        k.wait("pe", ofree[qb])
                                    ins = pe.matmul(obank[qb][:, 0:257], pT[r][:, qb * 128:(qb + 1) * 128], va[s][:, kb, :],
                                                    start=(kb == 0), stop=(kb == diags[qb]))
                                    if kb == diags[qb]:
                                        to = k.mark("pe", ins)
                                        eng = "act" if qb % 2 == 0 else "dve"
                                        k.wait(eng, [to, osb_free[c][qb]])
                                        if eng == "act":
                                            te = k.mark("act", act.activation(out=Osb[c][qb][:], in_=obank[qb][:, 0:257], func=AF.Copy))
                                        else:
                                            te = k.mark("dve", dve.tensor_copy(Osb[c][qb][:], obank[qb][:, 0:257]))
                                        ofree[qb] = te
                                        osb_free[c][qb] = te
                                pfree[r] = k.mark("pe", ins) if kb != diags[nqb - 1] else (k.prog["pe"], k.prog["pe"].n)
                            pend.append(do_pv)
                            if len(pend) > 2:
                                pend.pop(0)()
                            if kb == kmax // 2 and c == 0:
                                while st2_q:
                                    st2_q.pop(0)()
                            if kb == kmax and c == 0:
                                while st2_q:
                                    st2_q.pop(0)()
                                while epi_q:
                                    epi_q.pop(0)()
                        while pend:
                            pend.pop(0)()
                    k.wait("dve", ssall_free[0])
                    for qb in range(nqb):
                        O1, O2, m_ = Osb[0][qb], Osb[1][qb], sm[qb]
                        k.wait("dve", [osb_free[0][qb], osb_free[1][qb], comb_free[qb]])
                        t = k.mark("dve", dve.tensor_scalar(m_[:, 0:1], O1[:, 256:257], 1e-30, None, ALU.add))
                        t = k.mark("dve", dve.tensor_scalar(m_[:, 1:2], O2[:, 256:257], 1e-30, None, ALU.add))
                        k.wait("dve", t)
                        t = k.mark("dve", dve.reciprocal(m_[:, 2:4], m_[:, 0:2]))
                        k.wait("dve", t)
                        t = k.mark("dve", dve.tensor_tensor(m_[:, 3:4], m_[:, 3:4], nlam[:, 0:1], ALU.mult))
                        t1 = k.mark("dve", dve.tensor_scalar(ta_[qb][:], O1[:, 0:256], m_[:, 2:3], None, ALU.mult))
                        k.wait("dve", [t, t1])
                        t = k.mark("dve", dve.scalar_tensor_tensor(tb_[qb][:], O2[:, 0:256], m_[:, 3:4], ta_[qb][:], ALU.mult, ALU.add))
                        osb_free[0][qb] = t
                        osb_free[1][qb] = t
                        k.wait("dve", t)
                        t = k.mark("dve", dve.tensor_tensor(ta_[qb][:], tb_[qb][:], tb_[qb][:], ALU.mult))
                        k.wait("dve", t)
                        tss = k.mark("dve", dve.reduce_sum(ssall[:, qb:qb + 1], ta_[qb][:], AX.X))
                    fin = {}

                    def stage2(nqb=nqb, tss=tss, fin=fin):
                        k.wait("act", tss)
                        tl = k.mark("act", act.activation(out=ssall[:, 4:4 + nqb], in_=ssall[:, 0:nqb], func=AF.Ln, scale=1.0 / 256, bias=EPS))
                        k.wait("act", tl)
                        te_ = k.mark("act", act.activation(out=ssall[:, 8:8 + nqb], in_=ssall[:, 4:4 + nqb], func=AF.Exp, scale=-0.5))
                        k.wait("dve", te_)
                        for qb in range(nqb):
                            k.wait("dve", atb_free[qb])
                            t = k.mark("dve", dve.scalar_tensor_tensor(atb[qb][:], tb_[qb][:], ssall[:, 8 + qb:9 + qb], gsub[:], ALU.mult, ALU.mult))
                            comb_free[qb] = t
                            ssall_free[0] = t
                            fin[qb] = t
                    st2_q.append(stage2)
                    for qb in range(nqb):
                        def epi(qb=qb, fin=fin, c0=c0, s=s):
                            tb2 = 0
                            k.wait("pe", [fin[qb], tfree[tb2]])
                            for j in range(2):
                                ins = pe.transpose(tbank[tb2][:, j * 128:(j + 1) * 128], atb[qb][:, j * 128:(j + 1) * 128], ident_b[:])
                            tt = k.mark("pe", ins)
                            atb_free[qb] = tt
                            k.wait("act", [tt, mix_free[s]])
                            col = c0 + qb * 128
                            te = k.mark("act", act.activation(out=mixst[s][:, :, col:col + 128],
                                                              in_=tbank[tb2][:, 0:256].rearrange("p (j c) -> p j c", j=2), func=AF.Copy))
                            tfree[tb2] = te
                            mixtoks.append(te)
                        epi_q.append(epi)
                while st2_q:
                    st2_q.pop(0)()
                while epi_q:
                    epi_q.pop(0)()
                head_free[s] = (k.prog["pe"], k.prog["pe"].n)
                k.wait("sp", mixtoks)
                for g_, (gs_, gn_) in enumerate(OG):
                    mix_free[s] = k.dma("sp", mixT_s[g_][:, 2 * h:2 * h + 2, 0:gn_], mixst[s][:, :, gs_:gs_ + gn_], msem[s])
            k.barrier(scratch)

        def tm_phase(name, srcT_s, KCx, w_d, gsplit, ggi, resid_fn, final):
            GB = 4
            groups = [list(range(gs_ // 128, (gs_ + gn_) // 128)) for (gs_, gn_) in gsplit]
            nx = 1
            with ExitStack() as st:
                A = lambda nm, shape, dt=F32: sb(st, nc, f"{name}_{nm}", shape, dt)
                src = A("src", [128, KCx, GB * 128], BF16)
                yb = [A(f"yb{i}", [128, D]) for i in range(GB)]
                xb = [A(f"xb{i}", [128, D]) for i in range(nx)]
                ggrow = A("gg", [128, D])
                ssq = [A(f"ssq{i}", [128, NTD + 8]) for i in range(GB)]
                jk = A("jk", [128, 512], BF16)
                nbanks = 8 if final else 6
                banks = [psb(st, nc, f"{name}_b{i}", [128, 512]) for i in range(nbanks)]
                gsem = k.dsem(name + "_gg")
                tgg = k.dma("sp", ggrow[:], ggrow_s[ggi].partition_broadcast(128), gsem)
                ssem = k.dsem(name + "_src")
                xsem = [k.dsem(f"{name}_x{i}") for i in range(nx)]
                osem = [k.dsem(f"{name}_o{i}") for i in range(nx)]
                S_ = dict(yb_free=[None] * GB, xb_free=[None] * nx, xcnt=0, sq=[[] for _ in range(GB)])
                if not final:
                    pst = [psb(st, nc, f"{name}_pst{i}", [128, 1024], BF16) for i in range(2)]
                    h2st = [A(f"h2st{i}", [128, KC, 128], BF16) for i in range(2)]
                    hsem = [k.dsem(f"{name}_h{i}") for i in range(2)]
                    h2free = [None, None]
                    xnb = [A(f"xn{i}", [128, D], BF16) for i in range(GB)]
                    smb = [A(f"sm{i}", [128, 4]) for i in range(GB)]
                    xn_free = [None] * GB
                    pt_free = [None, None]
                    backs = []
                    gcnt = [0]

                def group_load(g):
                    n = len(groups[g]) * 128
                    k.wait("sp", (k.prog["pe"], k.prog["pe"].n))
                    return k.dma("sp", src[:, :, 0:n], srcT_s[g][:, :, 0:n], ssem)

                def cb(g, blk, bi, nt, bank, tp):
                    if nt == 0:
                        k.wait("dve", S_["yb_free"][bi])
                        tz = k.mark("dve", dve.memset(ssq[bi][:], 0.0))
                        k.wait("act", [tz, S_["yb_free"][bi]])
                    k.wait("dve", tp)
                    t1 = k.mark("dve", dve.tensor_copy(yb[bi][:, nt * 512:(nt + 1) * 512], bank[:, :]))
                    k.wait("act", t1)
                    t2 = k.mark("act", act.activation(out=jk[:], in_=yb[bi][:, nt * 512:(nt + 1) * 512], func=AF.Square,
                                                      accum_out=ssq[bi][:, nt:nt + 1]))
                    S_["sq"][bi] += [t1, t2]
                    return t1

                def group_done(g, blks):
                    if not final:
                        while backs:
                            for _ in backs.pop(0):
                                pass
                    for bi, blk in enumerate(blks):
                        m_ = ssq[bi]
                        xs = S_["xcnt"] % nx
                        S_["xcnt"] += 1
                        k.wait("sp", S_["xb_free"][xs])
                        tx = k.dma("sp", xb[xs][:], resid_fn(blk), xsem[xs])
                        k.wait("dve", S_["sq"][bi])
                        S_["sq"][bi] = []
                        t = k.mark("dve", dve.reduce_sum(m_[:, NTD:NTD + 1], m_[:, 0:NTD], AX.X))
                        k.wait("act", t)
                        t = k.mark("act", act.activation(out=m_[:, NTD + 1:NTD + 2], in_=m_[:, NTD:NTD + 1], func=AF.Sqrt, scale=1.0 / D, bias=EPS))
                        k.wait("dve", t)
                        t = k.mark("dve", dve.reciprocal(m_[:, NTD + 2:NTD + 3], m_[:, NTD + 1:NTD + 2]))
                        k.wait("dve", [t, tgg])
                        t = k.mark("dve", dve.scalar_tensor_tensor(yb[bi][:], yb[bi][:], m_[:, NTD + 2:NTD + 3], ggrow[:], ALU.mult, ALU.mult))
                        k.wait("dve", [t, tx])
                        t = k.mark("dve", dve.tensor_tensor(xb[xs][:], yb[bi][:], xb[xs][:], ALU.add))
                        S_["yb_free"][bi] = t
                        if final:
                            k.wait("sp", t)
                            S_["xb_free"][xs] = k.dma("sp", out_d[blk * 128:(blk + 1) * 128, :], xb[xs][:], osem[xs])
                        else:
                            stoks = []
                            if blk >= 1:
                                k.wait("sp", t)
                                stoks.append(k.dma("sp", xmid_s[(blk - 1) * 128:blk * 128, :], xb[xs][:], osem[xs]))

                            sm_ = smb[bi]
                            k.wait("dve", xn_free[bi])
                            tz = k.mark("dve", dve.memset(sm_[:], 0.0))
                            k.wait("act", [t, tz, xn_free[bi]])
                            t1 = k.mark("act", act.activation(out=xnb[bi][:], in_=xb[xs][:], func=AF.Square, accum_out=sm_[:, 0:1]))
                            k.wait("act", t1)
                            t2 = k.mark("act", act.activation(out=sm_[:, 1:2], in_=sm_[:, 0:1], func=AF.Sqrt, scale=1.0 / D, bias=EPS))
                            k.wait("dve", t2)
                            t3 = k.mark("dve", dve.reciprocal(sm_[:, 2:3], sm_[:, 1:2]))
                            k.wait("act", t3)
                            t4 = k.mark("act", act.activation(out=xnb[bi][:], in_=xb[xs][:], func=AF.Identity, scale=sm_[:, 2:3]))
                            S_["xb_free"][xs] = [t4] + stoks

                            def back(blk=blk, bi=bi, t4=t4):
                                hs = blk % 2
                                ev_all = []
                                for g0 in range(0, KC, 4):
                                    gi = gcnt[0]
                                    gcnt[0] += 1
                                    pt = pst[gi % 2]
                                    k.wait("pe", [pt_free[gi % 2], t4])
                                    n4 = min(4, KC - g0)
                                    for j in range(n4):
                                        kc = g0 + j
                                        ins = pe.transpose(pt[:, j * 128:(j + 1) * 128], xnb[bi][:, kc * 128:(kc + 1) * 128], ident_b[:])
                                    tp = k.mark("pe", ins)
                                    evs = []
                                    for j in range(n4):
                                        kc = g0 + j
                                        a_ap = modA[:, KC + kc:KC + kc + 1]
                                        b_ap = modB[:, KC + kc:KC + kc + 1]
                                        if g0 == 0 and j == 0:
                                            k.wait("dve", h2free[hs])
                                            k.wait("act", h2free[hs])
                                        o_ap = h2st[hs][:, kc, :]
                                        if gi % 2 == 0:
                                            k.wait("dve", tp)
                                            evs.append(k.mark("dve", dve.tensor_scalar(o_ap, pt[:, j * 128:(j + 1) * 128], a_ap, b_ap, ALU.mult, ALU.add)))
                                        else:
                                            k.wait("act", tp)
                                            evs.append(k.mark("act", act.activation(out=o_ap, in_=pt[:, j * 128:(j + 1) * 128], func=AF.Identity,
                                                                                    scale=a_ap, bias=b_ap)))
                                    pt_free[gi % 2] = evs[-1]
                                    ev_all += evs
                                    if g0 + 4 < KC:
                                        yield
                                xn_free[bi] = tp
                                k.wait("sp", ev_all)
                                h2free[hs] = k.dma("sp", h2T_s[:, :, blk * 128:(blk + 1) * 128].rearrange("kc p t -> p kc t"),
                                                   h2st[hs][:], hsem[hs])
                            backs.append(back())

                def filler(nt):
                    if final or nt < min(NTD // 2, NTD - 1):
                        return
                    while backs:
                        try:
                            next(backs[0])
                            return
                        except StopIteration:
                            backs.pop(0)

                tm_gemm(st, name, w_d, KCx, groups, lambda kc, blk, bi: src[:, kc, bi * 128:(bi + 1) * 128], group_load, banks, cb,
                        group_done=group_done, ksub=(2 if final else 4), nring=4, NT=NTD, filler=filler)
                if not final:
                    while backs:
                        for _ in backs.pop(0):
                            pass
                k.barrier(scratch)

        if STOP < 4:
            return nc
        tm_phase("op", mixT_s, KC, wout_c, OG, 0,
                 lambda blk: x_own[blk * 128:(blk + 1) * 128, :], False)

        if STOP < 5:
            return nc
        with ExitStack() as st:
            A = lambda name, shape, dt=F32: sb(st, nc, name, shape, dt)
            h2T = A("h2T", [128, KC, TOKL], BF16)
            U = A("up_U", [128, TOKL + 2])
            CG = A("up_CG", [128, TOKL])
            CV = A("up_CV", [128, TOKL])
            gst_ = [A(f"up_g{i}", [128, TOK], BF16) for i in range(2)]
            banks = [psb(st, nc, f"up_b{i}", [128, 512]) for i in range(8)]
            hs_ = k.dsem("up_h2")
            th = k.dma("sp", h2T[:], h2T_s.rearrange("kc p t -> p kc t"), hs_)
            k.wait("pe", th)
            gsem = [k.dsem(f"up_g{i}") for i in range(2)]
            tiles_l = [(HW - HN, HN)] + tok_tiles(TOKL, first=HW)[1:]
            for nt_ in range(NTD):
                for k0 in range(0, FC, 22):
                    k1 = min(FC, k0 + 22)
                    cache_jobs.append((wdn_c[nt_][:, k0:k1, :], wdn_d[nt_][:, k0:k1, :]))
            tz = k.mark("dve", dve.memset(U[:], 0.0))
            S_ = dict(U_free=None, CG_free=None, CV_free=None, g_free=[None, None], ev=[])

            def cb(ci, e, ti, c0, w, bank, tp):
                j, kind = divmod(e, 2)
                k.wait("act", [tp, tz] + ([S_["U_free"]] if ti == 0 else []))
                if ti == 0:
                    t = k.mark("act", act.activation(out=U[:, 2 + c0:2 + c0 + w], in_=bank[:, 0:w], func=AF.Identity, scale=flag[:, 0:1]))
                else:
                    t = k.mark("act", act.activation(out=U[:, 2 + c0:2 + c0 + w], in_=bank[:, 0:w], func=AF.Copy))
                S_["ev"].append(t)
                if ti == len(tiles_l) - 1:
                    dst = CG if kind == 0 else CV
                    wj = lambda tap: wcf[:, e * 3 + tap:e * 3 + tap + 1]
                    k.wait("dve", S_["ev"] + [S_["CG_free"] if kind == 0 else S_["CV_free"]])
                    S_["ev"] = []
                    t1 = k.mark("dve", dve.tensor_scalar(dst[:, :], U[:, 2:2 + TOKL], wj(2), None, ALU.mult))
                    k.wait("dve", t1)
                    t2 = k.mark("dve", dve.scalar_tensor_tensor(dst[:, :], U[:, 1:1 + TOKL], wj(1), dst[:, :], ALU.mult, ALU.add))
                    k.wait("dve", t2)
                    t3 = k.mark("dve", dve.scalar_tensor_tensor(dst[:, :], U[:, 0:TOKL], wj(0), dst[:, :], ALU.mult, ALU.add))
                    S_["U_free"] = t3
                    if kind == 0:
                        k.wait("act", t3)
                        S_["sil"] = k.mark("act", act.activation(out=CG[:, HW:], in_=CG[:, HW:], func=AF.Silu))
                    else:
                        s = j % 2
                        k.wait("dve", [t3, S_["sil"], S_["g_free"][s]])
                        tg = k.mark("dve", dve.tensor_tensor(gst_[s][:], CG[:, HW:], CV[:, HW:], ALU.mult))
                        S_["CG_free"] = tg
                        S_["CV_free"] = tg
                        k.wait("sp", tg)
                        for g_, (gs_, gn_) in enumerate(DG):
                            S_["g_free"][s] = k.dma("sp", gT_s[g_][:, j, 0:gn_], gst_[s][:, gs_:gs_ + gn_], gsem[s])
                return t, None
            fm_gemm(st, "up", [(wup_d[e], e) for e in range(2 * FC)], KC, lambda kc, c0, w: h2T[:, kc, c0:c0 + w], tiles_l, banks, cb, nslots=2, bg=1)
            while cache_jobs:
                cache_step()
            k.barrier(scratch)

        if STOP < 6:
            return nc
        tm_phase("dn", gT_s, FC, wdn_c, DG, 1,
                 lambda blk: xmid_s[blk * 128:(blk + 1) * 128, :], True)
        k.barrier(scratch)
    return nc


def _fm(W, cols):
    K = W.shape[0]
    sub = W[:, cols]
    ne = sub.shape[1] // 128
    return np.ascontiguousarray(sub.reshape(K // 128, 128, ne, 128).transpose(2, 1, 0, 3))


def _tm(W):
    K, N = W.shape
    return np.ascontiguousarray(W.reshape(K // 128, 128, N // 512, 512).transpose(2, 1, 0, 3))


def _featT(v):
    return np.ascontiguousarray(v.reshape(-1, 128).T)


def prepare(cfg, inp):
    D, S, H, DFF, B = cfg["D"], cfg["S"], cfg["H"], cfg["DFF"], cfg["B"]
    KC = D // 128
    TOK = S // 2
    HW = 128
    AW = H * 256
    CW = D - AW
    NCC = CW // 128
    FC = DFF // 128
    f32 = np.float32
    x = np.asarray(inp["x"], f32)
    c = np.asarray(inp["c"], f32)
    pos = np.asarray(inp["positions"], np.int32)
    w_in = np.asarray(inp["w_in"][0], f32)
    ar = np.arange
    shared = {}
    shared["ident"] = np.eye(128, dtype=f32)
    rot = np.zeros((128, 128), f32)
    for do in range(64):
        rot[do + 64, do] = -1.0
    for do in range(64, 128):
        rot[do - 64, do] = 1.0
    shared["rotm"] = rot
    invf = (ROPE_THETA ** (-(np.arange(0, 128, 2, dtype=np.float32)) / np.float32(128))).astype(f32)
    shared["invf"] = np.concatenate([invf, invf]).reshape(128, 1).astype(f32)
    shared["tri"] = (ar(128)[:, None] <= ar(128)[None, :]).astype(f32)
    shared["wada"] = _tm(np.asarray(inp["w_ada"][0], f32))
    shared["badaT"] = _featT(np.asarray(inp["b_ada"][0], f32))
    shared["gvec"] = np.concatenate([_featT(np.asarray(inp[n][0], f32)) for n in ("g_pre_mix", "g_post_mix", "g_pre_ffn", "g_post_ffn")], axis=1)
    shared["lamv"] = np.stack([np.asarray(inp[n][0], f32) for n in ("lambda_q1", "lambda_k1", "lambda_q2", "lambda_k2")])
    shared["gsub"] = np.asarray(inp["g_subln"][0], f32)
    shared["wq"] = _fm(w_in, ar(0, AW))
    shared["wk"] = _fm(w_in, ar(AW, 2 * AW))
    shared["wv"] = _tm(w_in[:, 2 * AW:3 * AW])
    oB, oC, oH = 3 * AW, 3 * AW + CW, 3 * AW + 2 * CW
    cols = np.concatenate([np.concatenate([ar(oC + j * 128, oC + (j + 1) * 128), ar(oH + j * 128, oH + (j + 1) * 128),
                                           ar(oB + j * 128, oB + (j + 1) * 128)]) for j in range(NCC)])
    shared["wcv"] = _fm(w_in, cols)
    wcm = np.asarray(inp["w_conv_mix"][0], f32)
    shared["wcm"] = np.ascontiguousarray(wcm.reshape(3, NCC, 128).transpose(2, 1, 0).reshape(128, NCC * 3))
    shared["wout"] = _tm(np.asarray(inp["w_out"][0], f32))
    w_up = np.asarray(inp["w_up"][0], f32)
    cols = np.concatenate([np.concatenate([ar(j * 128, (j + 1) * 128), ar(DFF + j * 128, DFF + (j + 1) * 128)]) for j in range(FC)])
    shared["wup"] = _fm(w_up, cols)
    wcf = np.asarray(inp["w_conv_ffn"][0], f32)[:, cols]
    shared["wcf"] = np.ascontiguousarray(wcf.reshape(3, 2 * FC, 128).transpose(2, 1, 0).reshape(128, 2 * FC * 3))
    shared["wdn"] = _tm(np.asarray(inp["w_down"][0], f32))
    in_maps = []
    for b in range(B):
        for h in range(2):
            m = dict(shared)
            t0 = h * TOK
            if h == 0:
                xo = np.concatenate([x[b, 0:HW], x[b, 0:TOK]], axis=0)
                po = np.concatenate([pos[b, 0:HW], pos[b, 0:TOK]])
            else:
                xo = x[b, t0 - HW:t0 + TOK]
                po = pos[b, t0 - HW:t0 + TOK]
            m["x_own"] = np.ascontiguousarray(xo)
            m["x_pre"] = np.ascontiguousarray(x[b, 0:TOK])
            m["pos_own"] = np.ascontiguousarray(po)
            m["pos_pre"] = np.ascontiguousarray(pos[b, 0:TOK])
            m["cT"] = _featT(c[b])
            m["flag"] = np.full((128, 1), float(h), f32)
            in_maps.append(m)
    return in_maps


def run(cfg, inputs, trace=False):
    _UID[0] = 0
    nc = build(cfg)
    in_maps = prepare(cfg, inputs)
    n = len(in_maps)
    res = run_bass_kernel_spmd(nc, in_maps, core_ids=list(range(n)), **({"trace": True} if trace else {}))
    B, S, D = cfg["B"], cfg["S"], cfg["D"]
    TOK = S // 2
    out = np.empty((B, S, D), np.float32)
    for b in range(B):
        for h in range(2):
            out[b, h * TOK:(h + 1) * TOK] = res.results[b * 2 + h]["out"]
    return out, res


def kernel(**inputs):
    out, _ = run(FULL_CFG, inputs)
    return out
```

```python
import math
from contextlib import ExitStack
import numpy as np
import concourse.bass as bass
import concourse.mybir as mybir
from concourse.bass_utils import run_bass_kernel_spmd

F32 = mybir.dt.float32
BF16 = mybir.dt.bfloat16
I32 = mybir.dt.int32
AF = mybir.ActivationFunctionType
ALU = mybir.AluOpType
AX = mybir.AxisListType

EPS = 1e-6
ROPE_THETA = 10000.0
LAM_INIT = 0.8 - 0.6 * math.exp(-0.3 * 0)
FULL_CFG = dict(D=4096, S=4096, H=8, DFF=11008, B=4)


class Sem:
    def __init__(self, nc, name):
        self.h = nc.alloc_semaphore(name)
        self.n = 0
        self.name = name


class KB:
    def __init__(self, nc):
        self.nc = nc
        self.eng = {"pe": nc.tensor, "act": nc.scalar, "dve": nc.vector, "pool": nc.gpsimd, "sp": nc.sync}
        self.prog = {e: Sem(nc, "p_" + e) for e in ("pe", "act", "dve", "pool")}
        self.waited = {e: {} for e in self.eng}
        self.dsems = []
        self.nsem = 0

    def dsem(self, name):
        s = Sem(self.nc, f"d{self.nsem}_{name}")
        self.nsem += 1
        self.dsems.append(s)
        return s

    def mark(self, e, ins):
        s = self.prog[e]
        ins.then_inc(s.h, 1)
        s.n += 1
        return (s, s.n)

    def wait(self, e, tok):
        if tok is None:
            return
        if isinstance(tok, list):
            for t in tok:
                self.wait(e, t)
            return
        s, v = tok
        w = self.waited[e]
        if w.get(s.name, 0) >= v:
            return
        w[s.name] = v
        self.eng[e].wait_ge(s.h, v)

    def dma(self, q, out, in_, sem):
        ins = self.eng[q].dma_start(out=out, in_=in_)
        ins.then_inc(sem.h, 16)
        sem.n += 16
        return (sem, sem.n)

    def barrier(self, scratch):
        nc = self.nc
        toks = []
        toks.append(self.mark("act", nc.scalar.activation(out=scratch[:, 0:1], in_=scratch[:, 4:5], func=AF.Copy)))
        toks.append(self.mark("dve", nc.vector.memset(scratch[:, 1:2], 0.0)))
        toks.append(self.mark("pool", nc.gpsimd.memset(scratch[:, 2:3], 0.0)))
        toks.append((self.prog["pe"], self.prog["pe"].n))
        for s in self.dsems:
            if s.n:
                toks.append((s, s.n))
        for e in self.eng:
            self.wait(e, toks)


_UID = [0]


def sb(st, nc, name, shape, dt):
    _UID[0] += 1
    return st.enter_context(nc.sbuf_tensor(f"sb{_UID[0]}_{name}", list(shape), dt))


def psb(st, nc, name, shape, dt=F32):
    _UID[0] += 1
    return st.enter_context(nc.psum_tensor(f"ps{_UID[0]}_{name}", list(shape), dt))


def tok_tiles(n, first=None):
    out = []
    c = 0
    if first:
        out.append((0, first))
        c = first
    while c < n:
        w = min(512, n - c)
        out.append((c, w))
        c += w
    return out


def build(cfg):
    D, S, H, DFF = cfg["D"], cfg["S"], cfg["H"], cfg["DFF"]
    KC = D // 128
    TOK = S // 2
    NB = TOK // 128
    HW = 128
    HN = 32
    TOKL = HW + TOK
    NBL = NB + 1
    AW = H * 256
    CW = D - AW
    NCC = CW // 128
    FC = DFF // 128
    NTD = D // 512
    NTV = AW // 512
    NADA = 6 * D // 512
    SCALE = 128.0 ** -0.5

    nc = bass.Bass("TRN2", target_bir_lowering=False)
    k = KB(nc)
    pe, act, dve, pool = nc.tensor, nc.scalar, nc.vector, nc.gpsimd

    def din(name, shape, dt=F32):
        return nc.dram_tensor(name, list(shape), dt, kind="ExternalInput").ap()

    DEBUG = cfg.get("debug", False)
    STOP = cfg.get("stop", 99)

    def dscr(name, shape, dt):
        return nc.dram_tensor(name, list(shape), dt, kind=("ExternalOutput" if DEBUG else "Internal")).ap()

    x_own = din("x_own", [TOKL, D])
    x_pre = din("x_pre", [TOK, D])
    pos_own = din("pos_own", [TOKL], I32)
    pos_pre = din("pos_pre", [TOK], I32)
    cT_d = din("cT", [128, KC])
    flag_d = din("flag", [128, 1])
    ident_d = din("ident", [128, 128])
    rotm_d = din("rotm", [128, 128])
    invf_d = din("invf", [128, 1])
    tri_d = din("tri", [128, 128])
    wada_d = din("wada", [NADA, 128, KC, 512])
    badaT_d = din("badaT", [128, 6 * KC])
    gvec_d = din("gvec", [128, 4 * KC])
    lam_d = din("lamv", [4, 128])
    gsub_d = din("gsub", [256])
    wq_d = din("wq", [2 * H, 128, KC, 128])
    wk_d = din("wk", [2 * H, 128, KC, 128])
    wv_d = din("wv", [NTV, 128, KC, 512])
    wcv_d = din("wcv", [3 * NCC, 128, KC, 128])
    wcm_d = din("wcm", [128, NCC * 3])
    wout_d = din("wout", [NTD, 128, KC, 512])
    wup_d = din("wup", [2 * FC, 128, KC, 128])
    wcf_d = din("wcf", [128, 2 * FC * 3])
    wdn_d = din("wdn", [NTD, 128, FC, 512])
    out_d = nc.dram_tensor("out", [TOK, D], F32, kind="ExternalOutput").ap()

    kT_s = dscr("kT_s", [2 * H, 128, 2 * TOK], BF16)
    qT_s = dscr("qT_s", [2 * H, 128, TOKL], BF16)
    v_s = dscr("v_s", [2 * TOK, AW], BF16)
    def split_groups(nblk):
        ng = (nblk + 3) // 4
        base, rem = divmod(nblk, ng)
        sizes = [base + (1 if i < rem else 0) for i in range(ng)]
        out, b0 = [], 0
        for n_ in sizes:
            out.append((b0 * 128, n_ * 128))
            b0 += n_
        return out
    OG = split_groups(NBL)
    DG = split_groups(NB)
    mixT_s = dscr("mixT_s", [len(OG), 128, KC, 512], BF16)
    xmid_s = dscr("xmid_s", [TOK, D], F32)
    h2T_s = dscr("h2T_s", [KC, 128, TOKL], BF16)
    gT_s = dscr("gT_s", [len(DG), 128, FC, 512], BF16)
    ggrow_s = dscr("ggrow_s", [2, D], F32)
    cs_own_s = dscr("cs_own_s", [2, 128, TOKL], F32)
    cs_pre_s = dscr("cs_pre_s", [2, 128, TOK], F32)
    wout_c = dscr("wout_c", [NTD, 128, KC, 512], BF16)
    wdn_c = dscr("wdn_c", [NTD, 128, FC, 512], BF16)
    cache_sem = k.dsem("wcache")
    cache_jobs = []
    for nt_ in range(NTD):
        for k0 in range(0, KC, 8):
            cache_jobs.append((wout_c[nt_][:, k0:k0 + 8, :], wout_d[nt_][:, k0:k0 + 8, :]))

    def cache_step(n=1):
        for _ in range(n):
            if cache_jobs:
                dst_, src_ = cache_jobs.pop(0)
                k.dma("pool", dst_, src_, cache_sem)

    with ExitStack() as gst:
        P = lambda name, shape, dt=F32: sb(gst, nc, name, shape, dt)
        scratch = P("scratch", [128, 8])
        ident_f = P("ident_f", [128, 128])
        ident_b = P("ident_b", [128, 128], BF16)
        rotm_f = P("rotm_f", [128, 128])
        ones_f = P("ones_f", [128, 128])
        tri_b = P("tri_b", [128, 128], BF16)
        flag = P("flag_sb", [128, 1])
        invf = P("invf_sb", [128, 1])
        modA = P("modA", [128, 2 * KC])
        modB = P("modB", [128, 2 * KC])
        nlam = P("nlam", [128, 1])
        gsub = P("gsub_sb", [128, 256])
        wcm = P("wcm_sb", [128, NCC * 3])
        wcf = P("wcf_sb", [128, 2 * FC * 3])

        cs = k.dsem("const")
        toks = []
        toks.append(k.dma("sp", ident_f[:], ident_d, cs))
        toks.append(k.dma("sp", rotm_f[:], rotm_d, cs))
        toks.append(k.dma("sp", flag[:], flag_d, cs))
        toks.append(k.dma("sp", invf[:], invf_d, cs))
        toks.append(k.dma("sp", gsub[:], gsub_d.partition_broadcast(128), cs))
        toks.append(k.dma("sp", wcm[:], wcm_d, cs))
        toks.append(k.dma("sp", wcf[:], wcf_d, cs))
        ctok = toks[-1]
        k.wait("dve", ctok)
        k.mark("dve", dve.memset(scratch[:], 0.0))
        k.mark("dve", dve.tensor_copy(ident_b[:], ident_f[:]))
        k.mark("dve", dve.memset(ones_f[:], 1.0))
        k.mark("dve", dve.tensor_scalar(gsub[:], gsub[:], 1.0 - LAM_INIT, None, ALU.mult))

        with ExitStack() as st:
            A = lambda name, shape, dt=F32: sb(st, nc, name, shape, dt)
            tri_f = A("tri_f", [128, 128])
            lamb = A("lamb", [128, 4, 128])
            lprod = A("lprod", [128, 2, 128])
            lsum = A("lsum", [128, 4])
            c_sb = A("c_sb", [128, KC])
            cond = A("cond", [128, KC], BF16)
            bada = A("bada", [128, 6 * KC])
            gvec = A("gvec", [128, 4 * KC])
            modT = A("modT", [128, 6 * KC])
            ggT = A("ggT", [128, 2 * KC])
            diag = [A(f"diag{i}", [128, 128]) for i in range(2)]
            ggrow = A("ggrow", [1, D])
            wsl = [A(f"wada{i}", [128, KC, 512], BF16) for i in range(2)]
            pm = psb(st, nc, "pm", [128, 512])
            pr = [psb(st, nc, f"pr{i}", [128, 512]) for i in range(2)]
            s0 = k.dsem("p0")
            t1 = k.dma("sp", tri_f[:], tri_d, s0)
            t2 = k.dma("sp", lamb[:], lam_d.partition_broadcast(128), s0)
            t3 = k.dma("sp", c_sb[:], cT_d, s0)
            t4 = k.dma("sp", bada[:], badaT_d, s0)
            t5 = k.dma("sp", gvec[:], gvec_d, s0)
            k.wait("dve", [t1, t2, t3, t4, t5])
            k.mark("dve", dve.tensor_copy(tri_b[:], tri_f[:]))
            k.mark("dve", dve.tensor_tensor(lprod[:, 0, :], lamb[:, 0, :], lamb[:, 1, :], ALU.mult))
            tl = k.mark("dve", dve.tensor_tensor(lprod[:, 1, :], lamb[:, 2, :], lamb[:, 3, :], ALU.mult))
            k.wait("dve", tl)
            k.mark("dve", dve.reduce_sum(lsum[:, 0:1], lprod[:, 0, :], AX.X))
            tl = k.mark("dve", dve.reduce_sum(lsum[:, 1:2], lprod[:, 1, :], AX.X))
            k.wait("act", tl)
            tl = k.mark("act", act.activation(out=lsum[:, 2:4], in_=lsum[:, 0:2], func=AF.Exp))
            k.wait("dve", tl)
            tl = k.mark("dve", dve.tensor_tensor(lsum[:, 0:1], lsum[:, 3:4], lsum[:, 2:3], ALU.subtract))
            k.wait("dve", tl)
            k.mark("dve", dve.tensor_scalar(nlam[:], lsum[:, 0:1], -LAM_INIT, None, ALU.add))
            k.wait("act", t3)
            tcond = k.mark("act", act.activation(out=cond[:], in_=c_sb[:], func=AF.Silu))
            wsem = [k.dsem(f"wada{i}") for i in range(2)]
            wfree = [None, None]
            wtok = {}

            def issue_ada(et):
                s = et % 2
                k.wait("pool", wfree[s])
                wtok[et] = k.dma("pool", wsl[s][:], wada_d[et], wsem[s])

            issue_ada(0)
            for ri, (pos_d, n, dst) in enumerate(((pos_own, TOKL, cs_own_s), (pos_pre, TOK, cs_pre_s))):
                pi_ = A(f"pi_{ri}", [128, n], I32)
                ang = A(f"ang{ri}", [128, n], F32)
                tmp = A(f"tmp{ri}", [128, n], F32)
                tabs = [A(f"tab{ri}_{i}", [128, n], F32) for i in range(2)]
                rs = k.dsem("rope")
                tpz = k.dma("sp", pi_[:], pos_d.partition_broadcast(128), rs)
                k.wait("dve", tpz)
                t = k.mark("dve", dve.tensor_copy(ang[:], pi_[:]))
                k.wait("dve", t)
                t = k.mark("dve", dve.tensor_scalar(ang[:], ang[:], invf[:, 0:1], None, ALU.mult))
                for ti, shift in enumerate((math.pi / 2, 0.0)):
                    tab = tabs[ti]
                    k.wait("dve", t)
                    t = k.mark("dve", dve.tensor_scalar(tab[:], ang[:], float(shift), None, ALU.add))
                    k.wait("dve", t)
                    t = k.mark("dve", dve.tensor_scalar(tmp[:], tab[:], float(1.0 / (2 * math.pi)), None, ALU.mult))
                    k.wait("dve", t)
                    t = k.mark("dve", dve.tensor_copy(pi_[:], tmp[:]))
                    k.wait("dve", t)
                    t = k.mark("dve", dve.tensor_copy(tmp[:], pi_[:]))
                    k.wait("dve", t)
                    t = k.mark("dve", dve.scalar_tensor_tensor(tab[:], tmp[:], -float(2 * math.pi), tab[:], ALU.mult, ALU.add))
                    k.wait("dve", t)
                    t = k.mark("dve", dve.tensor_scalar(tmp[:], tab[:], float(math.pi), float(2 * math.pi), ALU.is_gt, ALU.mult))
                    k.wait("dve", t)
                    t = k.mark("dve", dve.tensor_tensor(tab[:], tab[:], tmp[:], ALU.subtract))
                    k.wait("dve", t)
                    t = k.mark("dve", dve.tensor_scalar(tmp[:], tab[:], -float(math.pi), float(2 * math.pi), ALU.is_lt, ALU.mult))
                    k.wait("dve", t)
                    t = k.mark("dve", dve.tensor_tensor(tab[:], tab[:], tmp[:], ALU.add))
                    k.wait("act", t)
                    t2_ = k.mark("act", act.activation(out=tab[:], in_=tab[:], func=AF.Sin))
                    k.wait("sp", t2_)
                    k.dma("sp", dst[ti], tab[:], rs)
            k.wait("pe", tcond)
            lastp = None
            for et in range(NADA):
                if et + 1 < NADA:
                    issue_ada(et + 1)
                s = et % 2
                k.wait("pe", wtok[et])
                for jj in range(4):
                    j = et * 4 + jj
                    for kc in range(KC):
                        ins = pe.matmul(pm[:, j:j + 1], wsl[s][:, kc, jj * 128:(jj + 1) * 128], cond[:, kc:kc + 1],
                                        start=(kc == 0), stop=(kc == KC - 1))
                lastp = k.mark("pe", ins)
                wfree[s] = lastp
            k.wait("dve", lastp)
            tm = k.mark("dve", dve.tensor_tensor(modT[:], pm[:, 0:6 * KC], bada[:], ALU.add))
            k.wait("dve", tm)
            for i in range(2):
                o = 3 * i * KC
                k.mark("dve", dve.tensor_copy(modB[:, i * KC:(i + 1) * KC], modT[:, o:o + KC]))
                k.mark("dve", dve.scalar_tensor_tensor(modA[:, i * KC:(i + 1) * KC], modT[:, o + KC:o + 2 * KC], 1.0,
                                                       gvec[:, 2 * i * KC:(2 * i + 1) * KC], ALU.add, ALU.mult))
                tg = k.mark("dve", dve.tensor_tensor(ggT[:, i * KC:(i + 1) * KC], modT[:, o + 2 * KC:o + 3 * KC],
                                                     gvec[:, (2 * i + 1) * KC:(2 * i + 2) * KC], ALU.mult))
            k.wait("dve", tg)
            dfree = [None, None]
            pfree = [None, None]
            gs = k.dsem("ggrow")
            for i in range(2):
                evs = []
                for kc in range(KC):
                    s = kc % 2
                    k.wait("dve", dfree[s])
                    td = k.mark("dve", dve.tensor_scalar(diag[s][:], ident_f[:], ggT[:, i * KC + kc:i * KC + kc + 1], None, ALU.mult))
                    k.wait("pe", td)
                    k.wait("pe", pfree[s])
                    tp = k.mark("pe", pe.matmul(pr[s][0:1, 0:128], ones_f[:, 0:1], diag[s][:], start=True, stop=True))
                    dfree[s] = tp
                    k.wait("act", tp)
                    te = k.mark("act", act.activation(out=ggrow[0:1, kc * 128:(kc + 1) * 128], in_=pr[s][0:1, 0:128], func=AF.Copy))
                    pfree[s] = te
                    evs.append(te)
                k.wait("sp", evs)
                tdm = k.dma("sp", ggrow_s[i:i + 1, :], ggrow[0:1, :], gs)
                k.wait("act", tdm)
            k.barrier(scratch)

        def nt_alloc(st, name):
            return dict(xn=[sb(st, nc, f"{name}_xn{i}", [128, D], BF16) for i in range(2)],
                        sm=[sb(st, nc, f"{name}_sm{i}", [128, 4], F32) for i in range(2)],
                        xn_free=[None, None], pt_free=[None, None], gi=0, cnt=0)

        def nt_block(N_, b, xap, rdy, done_cb, dstT, mi, pst, store_fn=None):
            s = N_["cnt"] % 2
            N_["cnt"] += 1
            xn, sm = N_["xn"], N_["sm"]
            k.wait("dve", N_["xn_free"][s])
            tz = k.mark("dve", dve.memset(sm[s][:], 0.0))
            k.wait("act", [rdy, tz, N_["xn_free"][s]])
            t1 = k.mark("act", act.activation(out=xn[s][:], in_=xap, func=AF.Square, accum_out=sm[s][:, 0:1]))
            k.wait("act", t1)
            t2 = k.mark("act", act.activation(out=sm[s][:, 1:2], in_=sm[s][:, 0:1], func=AF.Sqrt, scale=1.0 / D, bias=EPS))
            k.wait("dve", t2)
            t3 = k.mark("dve", dve.reciprocal(sm[s][:, 2:3], sm[s][:, 1:2]))
            k.wait("act", t3)
            t4 = k.mark("act", act.activation(out=xn[s][:], in_=xap, func=AF.Identity, scale=sm[s][:, 2:3]))
            done_cb(t4)
            ev_all = []
            for g0 in range(0, KC, 4):
                gi = N_["gi"]
                pt = pst[gi % 2]
                k.wait("pe", [N_["pt_free"][gi % 2], t4])
                n4 = min(4, KC - g0)
                for j in range(n4):
                    kc = g0 + j
                    ins = pe.transpose(pt[:, j * 128:(j + 1) * 128], xn[s][:, kc * 128:(kc + 1) * 128], ident_b[:])
                tp = k.mark("pe", ins)
                evs = []
                for j in range(n4):
                    kc = g0 + j
                    a_ap = modA[:, mi * KC + kc:mi * KC + kc + 1]
                    b_ap = modB[:, mi * KC + kc:mi * KC + kc + 1]
                    o_ap = dstT(kc, b)
                    if gi % 2 == 0:
                        k.wait("dve", tp)
                        evs.append(k.mark("dve", dve.tensor_scalar(o_ap, pt[:, j * 128:(j + 1) * 128], a_ap, b_ap, ALU.mult, ALU.add)))
                    else:
                        k.wait("act", tp)
                        evs.append(k.mark("act", act.activation(out=o_ap, in_=pt[:, j * 128:(j + 1) * 128], func=AF.Identity,
                                                                scale=a_ap, bias=b_ap)))
                N_["pt_free"][gi % 2] = evs
                ev_all += evs
                N_["gi"] += 1
            N_["xn_free"][s] = tp
            if store_fn is not None:
                store_fn(b, ev_all)

        def norm_transpose(st, name, nblk, x_dram, dstT, mi):
            NG = (KC + 3) // 4
            nbk = min(8, NG)
            pst = [psb(st, nc, f"{name}_pst{i}", [128, 1024], BF16) for i in range(nbk)]
            xb = [sb(st, nc, f"{name}_xb{i}", [128, D], F32) for i in range(2)]
            xn = [sb(st, nc, f"{name}_xn{i}", [128, D], BF16) for i in range(2)]
            sm = [sb(st, nc, f"{name}_sm{i}", [128, 4], F32) for i in range(2)]
            xsem = [k.dsem(f"{name}_x{i}") for i in range(2)]
            xfree = [None, None]
            xn_free = [None, None]
            pt_free = [None] * nbk
            ltok = {}

            def issue(b):
                s_ = b % 2
                k.wait("sp", xfree[s_])
                ltok[b] = k.dma("sp", xb[s_][:], x_dram[b * 128:(b + 1) * 128, :], xsem[s_])

            def front_a(b):
                s_ = b % 2
                k.wait("dve", [xn_free[s_], ltok[b]])
                tz = k.mark("dve", dve.memset(sm[s_][:], 0.0))
                k.wait("dve", tz)
                t1 = k.mark("dve", dve.scalar_tensor_tensor(xn[s_][:], xb[s_][:], 1.0, xb[s_][:], ALU.mult, ALU.mult,
                                                            accum_out=sm[s_][:, 0:1]))
                k.wait("act", t1)
                return k.mark("act", act.activation(out=sm[s_][:, 1:2], in_=sm[s_][:, 0:1], func=AF.Sqrt, scale=1.0 / D, bias=EPS))

            def front_b(b, t2):
                s_ = b % 2
                k.wait("dve", t2)
                return k.mark("dve", dve.reciprocal(sm[s_][:, 2:3], sm[s_][:, 1:2]))

            def front(b):
                return front_b(b, front_a(b))

            issue(0)
            if nblk > 1:
                issue(1)
            t3 = front(0)
            gcount = 0
            for b in range(nblk):
                s_ = b % 2
                k.wait("act", t3)
                t4 = k.mark("act", act.activation(out=xn[s_][:], in_=xb[s_][:], func=AF.Identity, scale=sm[s_][:, 2:3]))
                xfree[s_] = t4
                if b + 2 < nblk:
                    issue(b + 2)
                groups = []
                for g0 in range(0, KC, 4):
                    bk = gcount % nbk
                    gcount += 1
                    k.wait("pe", [pt_free[bk], t4])
                    n4 = min(4, KC - g0)
                    for j in range(n4):
                        kc = g0 + j
                        ins = pe.transpose(pst[bk][:, j * 128:(j + 1) * 128], xn[s_][:, kc * 128:(kc + 1) * 128], ident_b[:])
                    tp = k.mark("pe", ins)
                    groups.append((g0, n4, bk, tp))
                xn_free[s_] = tp
                if b + 1 < nblk:
                    t3 = front(b + 1)
                for gi_, (g0, n4, bk, tp) in enumerate(groups):
                    evs = []
                    for j in range(n4):
                        kc = g0 + j
                        a_ap = modA[:, mi * KC + kc:mi * KC + kc + 1]
                        b_ap = modB[:, mi * KC + kc:mi * KC + kc + 1]
                        o_ap = dstT(kc, b)
                        if gi_ % 2 == 0:
                            k.wait("dve", tp)
                            evs.append(k.mark("dve", dve.tensor_scalar(o_ap, pst[bk][:, j * 128:(j + 1) * 128], a_ap, b_ap, ALU.mult, ALU.add)))
                        else:
                            k.wait("act", tp)
                            evs.append(k.mark("act", act.activation(out=o_ap, in_=pst[bk][:, j * 128:(j + 1) * 128], func=AF.Identity,
                                                                    scale=a_ap, bias=b_ap)))
                    pt_free[bk] = evs[-1]

        def dram_loader(st, x_dram, nblk, name):
            xb = [sb(st, nc, f"{name}_xb{i}", [128, D], F32) for i in range(2)]
            sems = [k.dsem(f"{name}{i}") for i in range(2)]
            free = [None, None]
            toks = {}

            def issue(b):
                s = b % 2
                k.wait("sp", free[s])
                toks[b] = k.dma("sp", xb[s][:], x_dram[b * 128:(b + 1) * 128, :], sems[s])

            issue(0)

            def load_fn(b):
                if b + 1 < nblk:
                    issue(b + 1)
                s = b % 2

                def done(t):
                    free[s] = t
                return xb[s][:], toks[b], done
            return load_fn

        def fm_gemm(st, name, chunks, KCx, rhs_fn, tiles, banks, callback, nslots=3, flush=None, bg=0):
            wsl = [sb(st, nc, f"{name}_w{i}", [128, KCx, 128], BF16) for i in range(nslots)]
            wsem = [k.dsem(f"{name}_w{i}") for i in range(nslots)]
            wfree = [None] * nslots
            wtok = {}
            bfree = [None] * len(banks)

            def issue(ci):
                s = ci % nslots
                k.wait("pool", wfree[s])
                wtok[ci] = k.dma("pool", wsl[s][:], chunks[ci][0], wsem[s])
                if bg:
                    cache_step(bg)

            for ci in range(min(nslots - 1, len(chunks))):
                issue(ci)
            bi = 0
            pending = None
            for ci, (_, e) in enumerate(chunks):
                if ci + nslots - 1 < len(chunks):
                    issue(ci + nslots - 1)
                s = ci % nslots
                k.wait("pe", wtok[ci])
                for ti, (c0, w) in enumerate(tiles):
                    bk = bi % len(banks)
                    bi += 1
                    k.wait("pe", bfree[bk])
                    for kc in range(KCx):
                        ins = pe.matmul(banks[bk][:, 0:w], wsl[s][:, kc, :], rhs_fn(kc, c0, w), start=(kc == 0), stop=(kc == KCx - 1))
                    tp = k.mark("pe", ins)
                    if ti == len(tiles) - 1:
                        wfree[s] = tp
                    fr, defer = callback(ci, e, ti, c0, w, banks[bk], tp)
                    bfree[bk] = fr
                    if pending is not None:
                        pending()
                    pending = defer
            if pending is not None:
                pending()
            if flush is not None:
                flush()

        def tm_gemm(st, name, w_d, KCx, groups, lhs_fn, group_load, banks, callback, group_done=None, ksub=4, nring=4, NT=None, nt_hook=None, filler=None):
            ring = [sb(st, nc, f"{name}_r{i}", [128, ksub, 512], BF16) for i in range(nring)]
            rsem = [k.dsem(f"{name}_r{i}") for i in range(nring)]
            rfree = [None] * nring
            subs = [(k0, min(ksub, KCx - k0)) for k0 in range(0, KCx, ksub)]
            seq = [(g, nt, si) for g in range(len(groups)) for nt in range(NT) for si in range(len(subs))]
            rtok = {}

            def issue(i):
                g, nt, si = seq[i]
                k0, nk = subs[si]
                s = i % nring
                k.wait("pool", rfree[s])
                rtok[i] = k.dma("pool", ring[s][:, 0:nk, :], w_d[nt][:, k0:k0 + nk, :], rsem[s])

            for i in range(min(nring - 1, len(seq))):
                issue(i)
            bfree = [None] * len(banks)
            bnext = 0
            i = 0
            gtok = group_load(0) if group_load is not None else None
            for g, blks in enumerate(groups):
                k.wait("pe", gtok)
                for nt in range(NT):
                    mybanks = []
                    for _ in blks:
                        mybanks.append(bnext % len(banks))
                        bnext += 1
                    last = {}
                    for si, (k0, nk) in enumerate(subs):
                        if i + nring - 1 < len(seq):
                            issue(i + nring - 1)
                        s = i % nring
                        k.wait("pe", rtok[i])
                        for bi, blk in enumerate(blks):
                            bk = mybanks[bi]
                            if si == 0:
                                k.wait("pe", bfree[bk])
                            for kk in range(nk):
                                kc = k0 + kk
                                ins = pe.matmul(banks[bk][:, :], lhs_fn(kc, blk, bi), ring[s][:, kk, :], start=(kc == 0), stop=(kc == KCx - 1))
                            if si == len(subs) - 1:
                                last[bi] = k.mark("pe", ins)
                                if bi == len(blks) - 1:
                                    rfree[s] = last[bi]
                            else:
                                if bi == len(blks) - 1:
                                    rfree[s] = k.mark("pe", ins)
                                if filler is not None:
                                    filler(nt)
                        i += 1
                    for bi, blk in enumerate(blks):
                        bfree[mybanks[bi]] = callback(g, blk, bi, nt, banks[mybanks[bi]], last[bi])
                    if nt_hook is not None:
                        nt_hook(g, nt)
                if group_load is not None and g + 1 < len(groups):
                    gtok = group_load(g + 1)
                if group_done is not None:
                    group_done(g, blks)

        def qk_rope_pass(st, name, hT, ntok, tiles, cs_d, jobs):
            cosT = sb(st, nc, name + "_cos", [128, ntok], F32)
            sinT = sb(st, nc, name + "_sin", [128, ntok], F32)
            t32 = [sb(st, nc, f"{name}_t32{i}", [128, 512], F32) for i in range(2)]
            tA = [sb(st, nc, f"{name}_tA{i}", [128, 512], F32) for i in range(2)]
            tB = [sb(st, nc, f"{name}_tB{i}", [128, 512], F32) for i in range(2)]
            stage = [sb(st, nc, f"{name}_st{i}", [128, ntok], BF16) for i in range(2)]
            banks = [psb(st, nc, f"{name}_b{i}", [128, 512]) for i in range(4)]
            rb = [psb(st, nc, f"{name}_rb{i}", [128, 512]) for i in range(2)]
            csem = k.dsem(name + "_cs")
            tc1 = k.dma("sp", cosT[:], cs_d[0], csem)
            tc2 = k.dma("sp", sinT[:], cs_d[1], csem)
            ssem = [k.dsem(f"{name}_st{i}") for i in range(2)]
            state = dict(cnt=0, t32_free=[None, None], rb_free=[None, None], tA_free=[None, None], tB_free=[None, None],
                         st_free=[None, None], nch=0, sums=[])
            chunks = []
            for (w_d, cl, dst_fn) in jobs:
                chunks += [(w_d[e], dst_fn(e)) for e in cl]

            def cb(ci, dst, ti, c0, w, bank, tp):
                s = state["cnt"] % 2
                state["cnt"] += 1
                k.wait("act", [tp, state["t32_free"][s]])
                ta = k.mark("act", act.activation(out=t32[s][:, 0:w], in_=bank[:, 0:w], func=AF.Copy))

                def defer():
                    ss_ = state["nch"] % 2
                    k.wait("pe", [ta, state["rb_free"][s]])
                    tr = k.mark("pe", pe.matmul(rb[s][:, 0:w], rotm_f[:], t32[s][:, 0:w], start=True, stop=True))
                    k.wait("dve", [ta, tc1, state["tA_free"][s]])
                    t_a = k.mark("dve", dve.tensor_tensor(tA[s][:, 0:w], t32[s][:, 0:w], cosT[:, c0:c0 + w], ALU.mult))
                    k.wait("dve", [tr, tc2, state["tB_free"][s]])
                    t_b = k.mark("dve", dve.tensor_tensor(tB[s][:, 0:w], rb[s][:, 0:w], sinT[:, c0:c0 + w], ALU.mult))
                    state["rb_free"][s] = t_b
                    state["t32_free"][s] = [t_a, tr]
                    k.wait("dve", [t_a, t_b])
                    if ti == 0:
                        k.wait("dve", state["st_free"][ss_])
                    t_s = k.mark("dve", dve.tensor_tensor(stage[ss_][:, c0:c0 + w], tA[s][:, 0:w], tB[s][:, 0:w], ALU.add))
                    state["tA_free"][s] = t_s
                    state["tB_free"][s] = t_s
                    state["sums"].append(t_s)
                    if ti == len(tiles) - 1:
                        k.wait("sp", state["sums"])
                        state["sums"] = []
                        state["st_free"][ss_] = k.dma("sp", dst, stage[ss_][:], ssem[ss_])
                        state["nch"] += 1
                return ta, defer
            fm_gemm(st, name, chunks, KC, lambda kc, c0, w: hT[:, kc, c0:c0 + w], tiles, banks, cb, nslots=2)

        def v_pass(st, name, hT, blks_cols, dst_row0, use_flag):
            banks = [psb(st, nc, f"{name}_b{i}", [128, 512]) for i in range(8)]
            vst = [sb(st, nc, f"{name}_vs{i}", [128, 512], BF16) for i in range(4)]
            vsem = [k.dsem(f"{name}_vs{i}") for i in range(4)]
            vfree = [None] * 4
            cnt = [0]
            nb_ = len(blks_cols)
            groups = [list(range(g0, min(g0 + 6, nb_))) for g0 in range(0, nb_, 6)]

            def cb(g, blk, bi, nt, bank, tp):
                s = cnt[0] % 4
                cnt[0] += 1
                k.wait("act", [tp, vfree[s]])
                if use_flag:
                    te = k.mark("act", act.activation(out=vst[s][:], in_=bank[:, :], func=AF.Identity, scale=flag[:, 0:1]))
                else:
                    te = k.mark("act", act.activation(out=vst[s][:], in_=bank[:, :], func=AF.Copy))
                k.wait("sp", te)
                r0 = dst_row0 + blk * 128
                vfree[s] = k.dma("sp", v_s[r0:r0 + 128, nt * 512:(nt + 1) * 512], vst[s][:], vsem[s])
                return te
            tm_gemm(st, name, wv_d, KC, groups, lambda kc, blk, bi: hT[:, kc, blks_cols[blk]:blks_cols[blk] + 128], None, banks, cb,
                    ksub=8 if KC >= 8 else KC, nring=4, NT=NTV)

        if STOP < 1:
            return nc
        with ExitStack() as st1:
            hTp = sb(st1, nc, "hTp", [128, KC, TOK], BF16)
            with ExitStack() as st:
                norm_transpose(st, "p1n", NB, x_pre, lambda kc, b: hTp[:, kc, b * 128:(b + 1) * 128], 0)
                k.barrier(scratch)
            if STOP < 1.3:
                return nc
            with ExitStack() as st:
                qk_rope_pass(st, "p1k", hTp, TOK, tok_tiles(TOK), cs_pre_s,
                             [(wk_d, list(range(2 * H)), lambda e: kT_s[e][:, 0:TOK])])
                k.barrier(scratch)
            if STOP < 1.6:
                return nc
            with ExitStack() as st:
                v_pass(st, "p1v", hTp, [b * 128 for b in range(NB)], 0, True)
                k.barrier(scratch)

        if STOP < 2:
            return nc
        with ExitStack() as st2:
            hT = sb(st2, nc, "hT", [128, KC, TOKL], BF16)
            with ExitStack() as st:
                norm_transpose(st, "p2n", NBL, x_own, lambda kc, b: hT[:, kc, b * 128:(b + 1) * 128], 0)
                k.barrier(scratch)
            tiles_l = tok_tiles(TOKL, first=HW)
            with ExitStack() as st:
                qk_rope_pass(st, "p2qk", hT, TOKL, tiles_l, cs_own_s,
                             [(wq_d, list(range(2 * H)), lambda e: qT_s[e]),
                              (wk_d, list(range(2 * H)), lambda e: kT_s[e][:, TOK - HW:2 * TOK])])
                k.barrier(scratch)
            with ExitStack() as st:
                Cb = sb(st, nc, "cv_C", [128, TOKL], F32)
                mb = sb(st, nc, "cv_m", [128, TOKL + 2], F32)
                stg = [sb(st, nc, f"cv_st{i}", [128, TOKL], BF16) for i in range(2)]
                banks = [psb(st, nc, f"cv_b{i}", [128, 512]) for i in range(6)]
                ssem = [k.dsem(f"cv_st{i}") for i in range(2)]
                S_ = dict(C_free=None, m_free=None, st_free=[None, None], Ctoks=[], mtoks=[], Btoks=[], convtok=None)
                tiles_c = [(HW - HN, HN)] + tiles_l[1:]
                k.mark("dve", dve.memset(mb[:], 0.0))
                k.mark("dve", dve.memset(Cb[:], 0.0))
                k.mark("dve", dve.memset(stg[0][:], 0.0))
                tz = k.mark("dve", dve.memset(stg[1][:], 0.0))
                k.wait("act", tz)

                def cb(ci, e, ti, c0, w, bank, tp):
                    j, kind = divmod(e, 3)
                    if kind == 0:
                        k.wait("act", [tp, S_["C_free"]] if ti == 0 else tp)
                        t = k.mark("act", act.activation(out=Cb[:, c0:c0 + w], in_=bank[:, 0:w], func=AF.Copy))
                        S_["Ctoks"].append(t)
                        return t, None
                    if kind == 1:
                        k.wait("dve", [tp, tz] + S_["Ctoks"] + ([S_["m_free"]] if ti == 0 else []))
                        t = k.mark("dve", dve.tensor_tensor(mb[:, 2 + c0:2 + c0 + w], bank[:, 0:w], Cb[:, c0:c0 + w], ALU.mult))
                        S_["mtoks"].append(t)
                        if ti == len(tiles_l) - 1:
                            k.wait("dve", S_["mtoks"])
                            t0 = k.mark("dve", dve.tensor_scalar(mb[:, 2:2 + HW], mb[:, 2:2 + HW], flag[:, 0:1], None, ALU.mult))
                            k.wait("dve", t0)
                            wj = lambda tap: wcm[:, j * 3 + tap:j * 3 + tap + 1]
                            t1 = k.mark("dve", dve.tensor_scalar(Cb[:, :], mb[:, 2:2 + TOKL], wj(2), None, ALU.mult))
                            k.wait("dve", t1)
                            t2 = k.mark("dve", dve.scalar_tensor_tensor(Cb[:, :], mb[:, 1:1 + TOKL], wj(1), Cb[:, :], ALU.mult, ALU.add))
                            k.wait("dve", t2)
                            t3 = k.mark("dve", dve.scalar_tensor_tensor(Cb[:, :], mb[:, 0:TOKL], wj(0), Cb[:, :], ALU.mult, ALU.add))
                            S_["convtok"] = t3
                            S_["m_free"] = t3
                            S_["Ctoks"] = []
                            S_["mtoks"] = []
                        return t, None
                    s = j % 2
                    k.wait("dve", [tp, S_["convtok"]] + ([S_["st_free"][s]] if ti == 0 else []))
                    t = k.mark("dve", dve.tensor_tensor(stg[s][:, c0:c0 + w], bank[:, 0:w], Cb[:, c0:c0 + w], ALU.mult))
                    S_["Btoks"].append(t)
                    if ti == len(tiles_l) - 1:
                        k.wait("sp", S_["Btoks"])
                        for g_, (gs_, gn_) in enumerate(OG):
                            S_["st_free"][s] = k.dma("sp", mixT_s[g_][:, AW // 128 + j, 0:gn_], stg[s][:, gs_:gs_ + gn_], ssem[s])
                        S_["C_free"] = S_["Btoks"][-1]
                        S_["C_free"] = list(S_["Btoks"])
                        S_["Btoks"] = []
                    return t, None
                fm_gemm(st, "cv", [(wcv_d[e], e) for e in range(3 * NCC)], KC, lambda kc, c0, w: hT[:, kc, c0:c0 + w], tiles_c, banks, cb, bg=2)
                while cache_jobs:
                    cache_step()
                k.barrier(scratch)
            with ExitStack() as st:
                v_pass(st, "p2v", hT, [HW + b * 128 for b in range(NB)], TOK, False)
                k.barrier(scratch)

        if STOP < 3:
            return nc
        with ExitStack() as st:
            A = lambda name, shape, dt=F32: sb(st, nc, name, shape, dt)
            NKB = 2 * NB
            kTb = [A(f"at_k{i}", [128, 2, 2 * TOK], BF16) for i in range(2)]
            qTb = [A(f"at_q{i}", [128, 2, TOKL], BF16) for i in range(2)]
            va = [A(f"at_v{i}", [128, NKB, 257], BF16) for i in range(2)]
            mixst = [A(f"at_m{i}", [128, 2, TOKL], BF16) for i in range(2)]
            pT = [A(f"at_p{i}", [128, 512], BF16) for i in range(3)]
            Osb = [[A(f"at_o{c}{q}", [128, 257]) for q in range(4)] for c in range(2)]
            sm = [A(f"at_sm{q}", [128, 8]) for q in range(4)]
            ta_ = [A(f"at_ta{q}", [128, 256]) for q in range(4)]
            tb_ = [A(f"at_tb{q}", [128, 256]) for q in range(4)]
            ssall = A("at_ssall", [128, 12])
            ssall_free = [None]
            atb = [A(f"at_ab{q}", [128, 256], BF16) for q in range(4)]
            sbank = [psb(st, nc, f"at_sb{i}", [128, 512]) for i in range(3)]
            obank = [psb(st, nc, f"at_ob{i}", [128, 512]) for i in range(4)]
            tbank = [psb(st, nc, f"at_tb{i}", [128, 1024], BF16) for i in range(1)]
            lsem = [k.dsem(f"at_ld{i}") for i in range(2)]
            msem = [k.dsem(f"at_ms{i}") for i in range(2)]
            for i in range(2):
                k.mark("pool", pool.memset(va[i][:, :, 256:257], 1.0))
                tv = k.mark("pool", pool.tensor_scalar(va[i][:, 0:NB, 256:257], va[i][:, 0:NB, 256:257], flag[:, 0:1], None, ALU.mult))
            head_free = [None, None]
            mix_free = [None, None]
            ltok = {}

            def issue_head(h):
                s = h % 2
                k.wait("sp", head_free[s])
                k.dma("sp", kTb[s][:], kT_s[2 * h:2 * h + 2].rearrange("c p t -> p c t"), lsem[s])
                k.dma("sp", qTb[s][:], qT_s[2 * h:2 * h + 2].rearrange("c p t -> p c t"), lsem[s])
                ltok[h] = k.dma("sp", va[s][:, :, 0:256], v_s[:, h * 256:(h + 1) * 256].rearrange("(kb p) e -> p kb e", p=128), lsem[s])

            issue_head(0)
            qtiles = [(0, HW, [NB - 1])]
            for (c0, w) in tok_tiles(TOK):
                qtiles.append((HW + c0, w, [NB + (c0 + i * 128) // 128 for i in range(w // 128)]))
            sfree = [None] * 3
            pfree = [None] * 3
            ofree = [None] * 4
            osb_free = [[None] * 4 for _ in range(2)]
            tfree = [None] * 1
            atb_free = [None] * 4
            comb_free = [None] * 4
            sidx = [0]
            for h in range(H):
                s = h % 2
                if h + 1 < H:
                    issue_head(h + 1)
                k.wait("pe", [ltok[h], tv])
                lastpe = None
                mixtoks = []
                epi_q = []
                st2_q = []
                for (c0, w, diags) in qtiles:
                    nqb = len(diags)
                    kmax = diags[-1]
                    for c in range(2):
                        pend = []
                        for kb in range(kmax + 1):
                            qb0 = 0
                            while diags[qb0] < kb:
                                qb0 += 1
                            qoff = qb0 * 128
                            r = sidx[0] % 3
                            sidx[0] += 1
                            k.wait("pe", [sfree[r]])
                            tS = k.mark("pe", pe.matmul(sbank[r][:, qoff:w], kTb[s][:, c, kb * 128:(kb + 1) * 128],
                                                        qTb[s][:, c, c0 + qoff:c0 + w], start=True, stop=True))
                            k.wait("act", [tS, pfree[r]])
                            tE = k.mark("act", act.activation(out=pT[r][:, qoff:w], in_=sbank[r][:, qoff:w], func=AF.Exp, scale=SCALE))
                            sfree[r] = tE
                            tP = tE
                            if kb in diags:
                                qd = diags.index(kb)
                                k.wait("dve", tE)
                                tP = k.mark("dve", dve.tensor_tensor(pT[r][:, qd * 128:(qd + 1) * 128], pT[r][:, qd * 128:(qd + 1) * 128],
                                                                     tri_b[:], ALU.mult))

                            def do_pv(kb=kb, qb0=qb0, r=r, tP=tP, tE=tE):
                                k.wait("pe", [tP, tE])
                                for qb in range(qb0, nqb):
                                    if kb == 0:
                                        k.wait("pe", ofree[qb])
                                    ins = pe.matmul(obank[qb][:, 0:257], pT[r][:, qb * 128:(qb + 1) * 128], va[s][:, kb, :],
                                                    start=(kb == 0), stop=(kb == diags[qb]))
                                    if kb == diags[qb]:
                                        to = k.mark("pe", ins)
                                        eng = "act" if qb % 2 == 0 else "dve"
                                        k.wait(eng, [to, osb_free[c][qb]])
                                        if eng == "act":
                                            te = k.mark("act", act.activation(out=Osb[c][qb][:], in_=obank[qb][:, 0:257], func=AF.Copy))
                                        else:
                                            te = k.mark("dve", dve.tensor_copy(Osb[c][qb][:], obank[qb][:, 0:257]))
                                        ofree[qb] = te
                                        osb_free[c][qb] = te
                                pfree[r] = k.mark("pe", ins) if kb != diags[nqb - 1] else (k.prog["pe"], k.prog["pe"].n)
                            pend.append(do_pv)
                            if len(pend) > 2:
                                pend.pop(0)()
                            if kb == kmax // 2 and c == 0:
                                while st2_q:
                                    st2_q.pop(0)()
                            if kb == kmax and c == 0:
                                while st2_q:
                                    st2_q.pop(0)()
                                while epi_q:
                                    epi_q.pop(0)()
                        while pend:
                            pend.pop(0)()
                    k.wait("dve", ssall_free[0])
                    for qb in range(nqb):
                        O1, O2, m_ = Osb[0][qb], Osb[1][qb], sm[qb]
                        k.wait("dve", [osb_free[0][qb], osb_free[1][qb], comb_free[qb]])
                        t = k.mark("dve", dve.tensor_scalar(m_[:, 0:1], O1[:, 256:257], 1e-30, None, ALU.add))
                        t = k.mark("dve", dve.tensor_scalar(m_[:, 1:2], O2[:, 256:257], 1e-30, None, ALU.add))
                        k.wait("dve", t)
                        t = k.mark("dve", dve.reciprocal(m_[:, 2:4], m_[:, 0:2]))
                        k.wait("dve", t)
                        t = k.mark("dve", dve.tensor_tensor(m_[:, 3:4], m_[:, 3:4], nlam[:, 0:1], ALU.mult))
                        t1 = k.mark("dve", dve.tensor_scalar(ta_[qb][:], O1[:, 0:256], m_[:, 2:3], None, ALU.mult))
                        k.wait("dve", [t, t1])
                        t = k.mark("dve", dve.scalar_tensor_tensor(tb_[qb][:], O2[:, 0:256], m_[:, 3:4], ta_[qb][:], ALU.mult, ALU.add))
                        osb_free[0][qb] = t
                        osb_free[1][qb] = t
                        k.wait("dve", t)
                        t = k.mark("dve", dve.tensor_tensor(ta_[qb][:], tb_[qb][:], tb_[qb][:], ALU.mult))
                        k.wait("dve", t)
                        tss = k.mark("dve", dve.reduce_sum(ssall[:, qb:qb + 1], ta_[qb][:], AX.X))
                    fin = {}

                    def stage2(nqb=nqb, tss=tss, fin=fin):
                        k.wait("act", tss)
                        tl = k.mark("act", act.activation(out=ssall[:, 4:4 + nqb], in_=ssall[:, 0:nqb], func=AF.Ln, scale=1.0 / 256, bias=EPS))
                        k.wait("act", tl)
                        te_ = k.mark("act", act.activation(out=ssall[:, 8:8 + nqb], in_=ssall[:, 4:4 + nqb], func=AF.Exp, scale=-0.5))
                        k.wait("dve", te_)
                        for qb in range(nqb):
                            k.wait("dve", atb_free[qb])
                            t = k.mark("dve", dve.scalar_tensor_tensor(atb[qb][:], tb_[qb][:], ssall[:, 8 + qb:9 + qb], gsub[:], ALU.mult, ALU.mult))
                            comb_free[qb] = t
                            ssall_free[0] = t
                            fin[qb] = t
                    st2_q.append(stage2)
                    for qb in range(nqb):
                        def epi(qb=qb, fin=fin, c0=c0, s=s):
                            tb2 = 0
                            k.wait("pe", [fin[qb], tfree[tb2]])
                            for j in range(2):
                                ins = pe.transpose(tbank[tb2][:, j * 128:(j + 1) * 128], atb[qb][:, j * 128:(j + 1) * 128], ident_b[:])
                            tt = k.mark("pe", ins)
                            atb_free[qb] = tt
                            k.wait("act", [tt, mix_free[s]])
                            col = c0 + qb * 128
                            te = k.mark("act", act.activation(out=mixst[s][:, :, col:col + 128],
                                                              in_=tbank[tb2][:, 0:256].rearrange("p (j c) -> p j c", j=2), func=AF.Copy))
                            tfree[tb2] = te
                            mixtoks.append(te)
                        epi_q.append(epi)
                while st2_q:
                    st2_q.pop(0)()
                while epi_q:
                    epi_q.pop(0)()
                head_free[s] = (k.prog["pe"], k.prog["pe"].n)
                k.wait("sp", mixtoks)
                for g_, (gs_, gn_) in enumerate(OG):
                    mix_free[s] = k.dma("sp", mixT_s[g_][:, 2 * h:2 * h + 2, 0:gn_], mixst[s][:, :, gs_:gs_ + gn_], msem[s])
            k.barrier(scratch)

        def tm_phase(name, srcT_s, KCx, w_d, gsplit, ggi, resid_fn, final):
            GB = 4
            groups = [list(range(gs_ // 128, (gs_ + gn_) // 128)) for (gs_, gn_) in gsplit]
            nx = 1
            with ExitStack() as st:
                A = lambda nm, shape, dt=F32: sb(st, nc, f"{name}_{nm}", shape, dt)
                src = A("src", [128, KCx, GB * 128], BF16)
                yb = [A(f"yb{i}", [128, D]) for i in range(GB)]
                xb = [A(f"xb{i}", [128, D]) for i in range(nx)]
                ggrow = A("gg", [128, D])
                ssq = [A(f"ssq{i}", [128, NTD + 8]) for i in range(GB)]
                jk = A("jk", [128, 512], BF16)
                nbanks = 8 if final else 6
                banks = [psb(st, nc, f"{name}_b{i}", [128, 512]) for i in range(nbanks)]
                gsem = k.dsem(name + "_gg")
                tgg = k.dma("sp", ggrow[:], ggrow_s[ggi].partition_broadcast(128), gsem)
                ssem = k.dsem(name + "_src")
                xsem = [k.dsem(f"{name}_x{i}") for i in range(nx)]
                osem = [k.dsem(f"{name}_o{i}") for i in range(nx)]
                S_ = dict(yb_free=[None] * GB, xb_free=[None] * nx, xcnt=0, sq=[[] for _ in range(GB)])
                if not final:
                    pst = [psb(st, nc, f"{name}_pst{i}", [128, 1024], BF16) for i in range(2)]
                    h2st = [A(f"h2st{i}", [128, KC, 128], BF16) for i in range(2)]
                    hsem = [k.dsem(f"{name}_h{i}") for i in range(2)]
                    h2free = [None, None]
                    xnb = [A(f"xn{i}", [128, D], BF16) for i in range(GB)]
                    smb = [A(f"sm{i}", [128, 4]) for i in range(GB)]
                    xn_free = [None] * GB
                    pt_free = [None, None]
                    backs = []
                    gcnt = [0]

                def group_load(g):
                    n = len(groups[g]) * 128
                    k.wait("sp", (k.prog["pe"], k.prog["pe"].n))
                    return k.dma("sp", src[:, :, 0:n], srcT_s[g][:, :, 0:n], ssem)

                def cb(g, blk, bi, nt, bank, tp):
                    if nt == 0:
                        k.wait("dve", S_["yb_free"][bi])
                        tz = k.mark("dve", dve.memset(ssq[bi][:], 0.0))
                        k.wait("act", [tz, S_["yb_free"][bi]])
                    k.wait("dve", tp)
                    t1 = k.mark("dve", dve.tensor_copy(yb[bi][:, nt * 512:(nt + 1) * 512], bank[:, :]))
                    k.wait("act", t1)
                    t2 = k.mark("act", act.activation(out=jk[:], in_=yb[bi][:, nt * 512:(nt + 1) * 512], func=AF.Square,
                                                      accum_out=ssq[bi][:, nt:nt + 1]))
                    S_["sq"][bi] += [t1, t2]
                    return t1

                def group_done(g, blks):
                    if not final:
                        while backs:
                            for _ in backs.pop(0):
                                pass
                    for bi, blk in enumerate(blks):
                        m_ = ssq[bi]
                        xs = S_["xcnt"] % nx
                        S_["xcnt"] += 1
                        k.wait("sp", S_["xb_free"][xs])
                        tx = k.dma("sp", xb[xs][:], resid_fn(blk), xsem[xs])
                        k.wait("dve", S_["sq"][bi])
                        S_["sq"][bi] = []
                        t = k.mark("dve", dve.reduce_sum(m_[:, NTD:NTD + 1], m_[:, 0:NTD], AX.X))
                        k.wait("act", t)
                        t = k.mark("act", act.activation(out=m_[:, NTD + 1:NTD + 2], in_=m_[:, NTD:NTD + 1], func=AF.Sqrt, scale=1.0 / D, bias=EPS))
                        k.wait("dve", t)
                        t = k.mark("dve", dve.reciprocal(m_[:, NTD + 2:NTD + 3], m_[:, NTD + 1:NTD + 2]))
                        k.wait("dve", [t, tgg])
                        t = k.mark("dve", dve.scalar_tensor_tensor(yb[bi][:], yb[bi][:], m_[:, NTD + 2:NTD + 3], ggrow[:], ALU.mult, ALU.mult))
                        k.wait("dve", [t, tx])
                        t = k.mark("dve", dve.tensor_tensor(yb[bi][:], yb[bi][:], xb[xs][:], ALU.add))
                        S_["xb_free"][xs] = t
                        if final:
                            k.wait("sp", t)
                            S_["yb_free"][bi] = k.dma("sp", out_d[blk * 128:(blk + 1) * 128, :], yb[bi][:], osem[xs])
                        else:
                            stoks = []
                            if blk >= 1:
                                k.wait("sp", t)
                                stoks.append(k.dma("sp", xmid_s[(blk - 1) * 128:blk * 128, :], yb[bi][:], osem[xs]))

                            sm_ = smb[bi]
                            k.wait("dve", xn_free[bi])
                            tz = k.mark("dve", dve.memset(sm_[:], 0.0))
                            k.wait("act", [t, tz, xn_free[bi]])
                            t1 = k.mark("act", act.activation(out=xnb[bi][:], in_=yb[bi][:], func=AF.Square, accum_out=sm_[:, 0:1]))
                            k.wait("act", t1)
                            t2 = k.mark("act", act.activation(out=sm_[:, 1:2], in_=sm_[:, 0:1], func=AF.Sqrt, scale=1.0 / D, bias=EPS))
                            k.wait("dve", t2)
                            t3 = k.mark("dve", dve.reciprocal(sm_[:, 2:3], sm_[:, 1:2]))
                            k.wait("act", t3)
                            t4 = k.mark("act", act.activation(out=xnb[bi][:], in_=yb[bi][:], func=AF.Identity, scale=sm_[:, 2:3]))
                            S_["yb_free"][bi] = [t4] + stoks

                            def back(blk=blk, bi=bi, t4=t4):
                                hs = blk % 2
                                ev_all = []
                                for g0 in range(0, KC, 4):
                                    gi = gcnt[0]
                                    gcnt[0] += 1
                                    pt = pst[gi % 2]
                                    k.wait("pe", [pt_free[gi % 2], t4])
                                    n4 = min(4, KC - g0)
                                    for j in range(n4):
                                        kc = g0 + j
                                        ins = pe.transpose(pt[:, j * 128:(j + 1) * 128], xnb[bi][:, kc * 128:(kc + 1) * 128], ident_b[:])
                                    tp = k.mark("pe", ins)
                                    evs = []
                                    for j in range(n4):
                                        kc = g0 + j
                                        a_ap = modA[:, KC + kc:KC + kc + 1]
                                        b_ap = modB[:, KC + kc:KC + kc + 1]
                                        if g0 == 0 and j == 0:
                                            k.wait("dve", h2free[hs])
                                            k.wait("act", h2free[hs])
                                        o_ap = h2st[hs][:, kc, :]
                                        if gi % 2 == 0:
                                            k.wait("dve", tp)
                                            evs.append(k.mark("dve", dve.tensor_scalar(o_ap, pt[:, j * 128:(j + 1) * 128], a_ap, b_ap, ALU.mult, ALU.add)))
                                        else:
                                            k.wait("act", tp)
                                            evs.append(k.mark("act", act.activation(out=o_ap, in_=pt[:, j * 128:(j + 1) * 128], func=AF.Identity,
                                                                                    scale=a_ap, bias=b_ap)))
                                    pt_free[gi % 2] = evs[-1]
                                    ev_all += evs
                                    if g0 + 4 < KC:
                                        yield
                                xn_free[bi] = tp
                                k.wait("sp", ev_all)
                                h2free[hs] = k.dma("sp", h2T_s[:, :, blk * 128:(blk + 1) * 128].rearrange("kc p t -> p kc t"),
                                                   h2st[hs][:], hsem[hs])
                            backs.append(back())

                def filler(nt):
                    if final or nt < min(NTD // 2, NTD - 1):
                        return
                    while backs:
                        try:
                            next(backs[0])
                            return
                        except StopIteration:
                            backs.pop(0)

                tm_gemm(st, name, w_d, KCx, groups, lambda kc, blk, bi: src[:, kc, bi * 128:(bi + 1) * 128], group_load, banks, cb,
                        group_done=group_done, ksub=(2 if final else 4), nring=4, NT=NTD, filler=filler)
                if not final:
                    while backs:
                        for _ in backs.pop(0):
                            pass
                k.barrier(scratch)

        if STOP < 4:
            return nc
        tm_phase("op", mixT_s, KC, wout_c, OG, 0,
                 lambda blk: x_own[blk * 128:(blk + 1) * 128, :], False)

        if STOP < 5:
            return nc
        with ExitStack() as st:
            A = lambda name, shape, dt=F32: sb(st, nc, name, shape, dt)
            h2T = A("h2T", [128, KC, TOKL], BF16)
            U = A("up_U", [128, TOKL + 2])
            CG = A("up_CG", [128, TOKL])
            CV = A("up_CV", [128, TOKL])
            gst_ = [A(f"up_g{i}", [128, TOK], BF16) for i in range(2)]
            banks = [psb(st, nc, f"up_b{i}", [128, 512]) for i in range(8)]
            hs_ = k.dsem("up_h2")
            th = k.dma("sp", h2T[:], h2T_s.rearrange("kc p t -> p kc t"), hs_)
            k.wait("pe", th)
            gsem = [k.dsem(f"up_g{i}") for i in range(2)]
            tiles_l = [(HW - HN, HN)] + tok_tiles(TOKL, first=HW)[1:]
            for nt_ in range(NTD):
                for k0 in range(0, FC, 22):
                    k1 = min(FC, k0 + 22)
                    cache_jobs.append((wdn_c[nt_][:, k0:k1, :], wdn_d[nt_][:, k0:k1, :]))
            tz = k.mark("dve", dve.memset(U[:], 0.0))
            S_ = dict(U_free=None, CG_free=None, CV_free=None, g_free=[None, None], ev=[])

            def cb(ci, e, ti, c0, w, bank, tp):
                j, kind = divmod(e, 2)
                k.wait("act", [tp, tz] + ([S_["U_free"]] if ti == 0 else []))
                if ti == 0:
                    t = k.mark("act", act.activation(out=U[:, 2 + c0:2 + c0 + w], in_=bank[:, 0:w], func=AF.Identity, scale=flag[:, 0:1]))
                else:
                    t = k.mark("act", act.activation(out=U[:, 2 + c0:2 + c0 + w], in_=bank[:, 0:w], func=AF.Copy))
                S_["ev"].append(t)
                if ti == len(tiles_l) - 1:
                    dst = CG if kind == 0 else CV
                    wj = lambda tap: wcf[:, e * 3 + tap:e * 3 + tap + 1]
                    k.wait("dve", S_["ev"] + [S_["CG_free"] if kind == 0 else S_["CV_free"]])
                    S_["ev"] = []
                    t1 = k.mark("dve", dve.tensor_scalar(dst[:, :], U[:, 2:2 + TOKL], wj(2), None, ALU.mult))
                    k.wait("dve", t1)
                    t2 = k.mark("dve", dve.scalar_tensor_tensor(dst[:, :], U[:, 1:1 + TOKL], wj(1), dst[:, :], ALU.mult, ALU.add))
                    k.wait("dve", t2)
                    t3 = k.mark("dve", dve.scalar_tensor_tensor(dst[:, :], U[:, 0:TOKL], wj(0), dst[:, :], ALU.mult, ALU.add))
                    S_["U_free"] = t3
                    if kind == 0:
                        k.wait("act", t3)
                        S_["sil"] = k.mark("act", act.activation(out=CG[:, HW:], in_=CG[:, HW:], func=AF.Silu))
                    else:
                        s = j % 2
                        k.wait("dve", [t3, S_["sil"], S_["g_free"][s]])
                        tg = k.mark("dve", dve.tensor_tensor(gst_[s][:], CG[:, HW:], CV[:, HW:], ALU.mult))
                        S_["CG_free"] = tg
                        S_["CV_free"] = tg
                        k.wait("sp", tg)
                        for g_, (gs_, gn_) in enumerate(DG):
                            S_["g_free"][s] = k.dma("sp", gT_s[g_][:, j, 0:gn_], gst_[s][:, gs_:gs_ + gn_], gsem[s])
                return t, None
            fm_gemm(st, "up", [(wup_d[e], e) for e in range(2 * FC)], KC, lambda kc, c0, w: h2T[:, kc, c0:c0 + w], tiles_l, banks, cb, nslots=2, bg=1)
            while cache_jobs:
                cache_step()
            k.barrier(scratch)

        if STOP < 6:
            return nc
        tm_phase("dn", gT_s, FC, wdn_c, DG, 1,
                 lambda blk: xmid_s[blk * 128:(blk + 1) * 128, :], True)
        k.barrier(scratch)
    return nc


def _fm(W, cols):
    K = W.shape[0]
    sub = W[:, cols]
    ne = sub.shape[1] // 128
    return np.ascontiguousarray(sub.reshape(K // 128, 128, ne, 128).transpose(2, 1, 0, 3))


def _tm(W):
    K, N = W.shape
    return np.ascontiguousarray(W.reshape(K // 128, 128, N // 512, 512).transpose(2, 1, 0, 3))


def _featT(v):
    return np.ascontiguousarray(v.reshape(-1, 128).T)


def prepare(cfg, inp):
    D, S, H, DFF, B = cfg["D"], cfg["S"], cfg["H"], cfg["DFF"], cfg["B"]
    KC = D // 128
    TOK = S // 2
    HW = 128
    AW = H * 256
    CW = D - AW
    NCC = CW // 128
    FC = DFF // 128
    f32 = np.float32
    x = np.asarray(inp["x"], f32)
    c = np.asarray(inp["c"], f32)
    pos = np.asarray(inp["positions"], np.int32)
    w_in = np.asarray(inp["w_in"][0], f32)
    ar = np.arange
    shared = {}
    shared["ident"] = np.eye(128, dtype=f32)
    rot = np.zeros((128, 128), f32)
    for do in range(64):
        rot[do + 64, do] = -1.0
    for do in range(64, 128):
        rot[do - 64, do] = 1.0
    shared["rotm"] = rot
    invf = (ROPE_THETA ** (-(np.arange(0, 128, 2, dtype=np.float32)) / np.float32(128))).astype(f32)
    shared["invf"] = np.concatenate([invf, invf]).reshape(128, 1).astype(f32)
    shared["tri"] = (ar(128)[:, None] <= ar(128)[None, :]).astype(f32)
    shared["wada"] = _tm(np.asarray(inp["w_ada"][0], f32))
    shared["badaT"] = _featT(np.asarray(inp["b_ada"][0], f32))
    shared["gvec"] = np.concatenate([_featT(np.asarray(inp[n][0], f32)) for n in ("g_pre_mix", "g_post_mix", "g_pre_ffn", "g_post_ffn")], axis=1)
    shared["lamv"] = np.stack([np.asarray(inp[n][0], f32) for n in ("lambda_q1", "lambda_k1", "lambda_q2", "lambda_k2")])
    shared["gsub"] = np.asarray(inp["g_subln"][0], f32)
    shared["wq"] = _fm(w_in, ar(0, AW))
    shared["wk"] = _fm(w_in, ar(AW, 2 * AW))
    shared["wv"] = _tm(w_in[:, 2 * AW:3 * AW])
    oB, oC, oH = 3 * AW, 3 * AW + CW, 3 * AW + 2 * CW
    cols = np.concatenate([np.concatenate([ar(oC + j * 128, oC + (j + 1) * 128), ar(oH + j * 128, oH + (j + 1) * 128),
                                           ar(oB + j * 128, oB + (j + 1) * 128)]) for j in range(NCC)])
    shared["wcv"] = _fm(w_in, cols)
    wcm = np.asarray(inp["w_conv_mix"][0], f32)
    shared["wcm"] = np.ascontiguousarray(wcm.reshape(3, NCC, 128).transpose(2, 1, 0).reshape(128, NCC * 3))
    shared["wout"] = _tm(np.asarray(inp["w_out"][0], f32))
    w_up = np.asarray(inp["w_up"][0], f32)
    cols = np.concatenate([np.concatenate([ar(j * 128, (j + 1) * 128), ar(DFF + j * 128, DFF + (j + 1) * 128)]) for j in range(FC)])
    shared["wup"] = _fm(w_up, cols)
    wcf = np.asarray(inp["w_conv_ffn"][0], f32)[:, cols]
    shared["wcf"] = np.ascontiguousarray(wcf.reshape(3, 2 * FC, 128).transpose(2, 1, 0).reshape(128, 2 * FC * 3))
    shared["wdn"] = _tm(np.asarray(inp["w_down"][0], f32))
    in_maps = []
    for b in range(B):
        for h in range(2):
            m = dict(shared)
            t0 = h * TOK
            if h == 0:
                xo = np.concatenate([x[b, 0:HW], x[b, 0:TOK]], axis=0)
                po = np.concatenate([pos[b, 0:HW], pos[b, 0:TOK]])
            else:
                xo = x[b, t0 - HW:t0 + TOK]
                po = pos[b, t0 - HW:t0 + TOK]
            m["x_own"] = np.ascontiguousarray(xo)
            m["x_pre"] = np.ascontiguousarray(x[b, 0:TOK])
            m["pos_own"] = np.ascontiguousarray(po)
            m["pos_pre"] = np.ascontiguousarray(pos[b, 0:TOK])
            m["cT"] = _featT(c[b])
            m["flag"] = np.full((128, 1), float(h), f32)
            in_maps.append(m)
    return in_maps


def run(cfg, inputs, trace=False):
    _UID[0] = 0
    nc = build(cfg)
    in_maps = prepare(cfg, inputs)
    n = len(in_maps)
    res = run_bass_kernel_spmd(nc, in_maps, core_ids=list(range(n)), **({"trace": True} if trace else {}))
    B, S, D = cfg["B"], cfg["S"], cfg["D"]
    TOK = S // 2
    out = np.empty((B, S, D), np.float32)
    for b in range(B):
        for h in range(2):
            out[b, h * TOK:(h + 1) * TOK] = res.results[b * 2 + h]["out"]
    return out, res


def kernel(**inputs):
    out, _ = run(FULL_CFG, inputs)
    return out
```

```python
import math
from contextlib import ExitStack
import numpy as np
import concourse.bass as bass
import concourse.mybir as mybir
from concourse.bass_utils import run_bass_kernel_spmd

F32 = mybir.dt.float32
BF16 = mybir.dt.bfloat16
I32 = mybir.dt.int32
AF = mybir.ActivationFunctionType
ALU = mybir.AluOpType
AX = mybir.AxisListType

EPS = 1e-6
ROPE_THETA = 10000.0
LAM_INIT = 0.8 - 0.6 * math.exp(-0.3 * 0)
FULL_CFG = dict(D=4096, S=4096, H=8, DFF=11008, B=4)


class Sem:
    def __init__(self, nc, name):
        self.h = nc.alloc_semaphore(name)
        self.n = 0
        self.name = name


class KB:
    def __init__(self, nc):
        self.nc = nc
        self.eng = {"pe": nc.tensor, "act": nc.scalar, "dve": nc.vector, "pool": nc.gpsimd, "sp": nc.sync}
        self.prog = {e: Sem(nc, "p_" + e) for e in ("pe", "act", "dve", "pool")}
        self.waited = {e: {} for e in self.eng}
        self.dsems = []
        self.nsem = 0

    def dsem(self, name):
        s = Sem(self.nc, f"d{self.nsem}_{name}")
        self.nsem += 1
        self.dsems.append(s)
        return s

    def mark(self, e, ins):
        s = self.prog[e]
        ins.then_inc(s.h, 1)
        s.n += 1
        return (s, s.n)

    def wait(self, e, tok):
        if tok is None:
            return
        if isinstance(tok, list):
            for t in tok:
                self.wait(e, t)
            return
        s, v = tok
        w = self.waited[e]
        if w.get(s.name, 0) >= v:
            return
        w[s.name] = v
        self.eng[e].wait_ge(s.h, v)

    def dma(self, q, out, in_, sem):
        ins = self.eng[q].dma_start(out=out, in_=in_)
        ins.then_inc(sem.h, 16)
        sem.n += 16
        return (sem, sem.n)

    def barrier(self, scratch):
        nc = self.nc
        toks = []
        toks.append(self.mark("act", nc.scalar.activation(out=scratch[:, 0:1], in_=scratch[:, 4:5], func=AF.Copy)))
        toks.append(self.mark("dve", nc.vector.memset(scratch[:, 1:2], 0.0)))
        toks.append(self.mark("pool", nc.gpsimd.memset(scratch[:, 2:3], 0.0)))
        toks.append((self.prog["pe"], self.prog["pe"].n))
        for s in self.dsems:
            if s.n:
                toks.append((s, s.n))
        for e in self.eng:
            self.wait(e, toks)


_UID = [0]


def sb(st, nc, name, shape, dt):
    _UID[0] += 1
    return st.enter_context(nc.sbuf_tensor(f"sb{_UID[0]}_{name}", list(shape), dt))


def psb(st, nc, name, shape, dt=F32):
    _UID[0] += 1
    return st.enter_context(nc.psum_tensor(f"ps{_UID[0]}_{name}", list(shape), dt))


def tok_tiles(n, first=None):
    out = []
    c = 0
    if first:
        out.append((0, first))
        c = first
    while c < n:
        w = min(512, n - c)
        out.append((c, w))
        c += w
    return out


def build(cfg):
    D, S, H, DFF = cfg["D"], cfg["S"], cfg["H"], cfg["DFF"]
    KC = D // 128
    TOK = S // 2
    NB = TOK // 128
    HW = 128
    HN = 32
    TOKL = HW + TOK
    NBL = NB + 1
    AW = H * 256
    CW = D - AW
    NCC = CW // 128
    FC = DFF // 128
    NTD = D // 512
    NTV = AW // 512
    NADA = 2 * D // 512
    NB_T = 4 * D // 128
    SCALE = 128.0 ** -0.5

    nc = bass.Bass("TRN2", target_bir_lowering=False)
    k = KB(nc)
    pe, act, dve, pool = nc.tensor, nc.scalar, nc.vector, nc.gpsimd

    def din(name, shape, dt=F32):
        return nc.dram_tensor(name, list(shape), dt, kind="ExternalInput").ap()

    DEBUG = cfg.get("debug", False)
    STOP = cfg.get("stop", 99)

    def dscr(name, shape, dt):
        return nc.dram_tensor(name, list(shape), dt, kind=("ExternalOutput" if DEBUG else "Internal")).ap()

    x_own = din("x_own", [TOKL, D])
    x_pre = din("x_pre", [TOK, D])
    pos_own = din("pos_own", [TOKL], I32)
    pos_pre = din("pos_pre", [TOK], I32)
    cT_d = din("cT", [128, KC])
    flag_d = din("flag", [128, 1])
    ident_d = din("ident", [128, 128])
    rotm_d = din("rotm", [128, 128])
    invf_d = din("invf", [128, 1])
    tri_d = din("tri", [128, 128])
    wada_d = din("wada", [NADA, 128, KC, 512])
    wadab_d = din("wadab", [NB_T, 128, KC, 128])
    badaT_d = din("badaT", [128, 6 * KC])
    gvec_d = din("gvec", [128, 4 * KC])
    lam_d = din("lamv", [4, 128])
    gsub_d = din("gsub", [256])
    wq_d = din("wq", [2 * H, 128, KC, 128])
    wk_d = din("wk", [2 * H, 128, KC, 128])
    wv_d = din("wv", [NTV, 128, KC, 512])
    wcv_d = din("wcv", [3 * NCC, 128, KC, 128])
    wcm_d = din("wcm", [128, NCC * 3])
    wout_d = din("wout", [NTD, 128, KC, 512])
    wup_d = din("wup", [2 * FC, 128, KC, 128])
    wcf_d = din("wcf", [128, 2 * FC * 3])
    wdn_d = din("wdn", [NTD, 128, FC, 512])
    out_d = nc.dram_tensor("out", [TOK, D], F32, kind="ExternalOutput").ap()

    kT_s = dscr("kT_s", [2 * H, 128, 2 * TOK], BF16)
    qT_s = dscr("qT_s", [2 * H, 128, TOKL], BF16)
    v_s = dscr("v_s", [2 * TOK, AW], BF16)
    def split_groups(nblk):
        ng = (nblk + 3) // 4
        base, rem = divmod(nblk, ng)
        sizes = [base + (1 if i < rem else 0) for i in range(ng)]
        out, b0 = [], 0
        for n_ in sizes:
            out.append((b0 * 128, n_ * 128))
            b0 += n_
        return out
    OG = split_groups(NBL)
    DG = split_groups(NB)
    mixT_s = dscr("mixT_s", [len(OG), 128, KC, 512], BF16)
    xmid_s = dscr("xmid_s", [TOK, D], F32)
    h2T_s = dscr("h2T_s", [KC, 128, TOKL], BF16)
    gT_s = dscr("gT_s", [len(DG), 128, FC, 512], BF16)
    ggrow_s = dscr("ggrow_s", [2, D], F32)
    cs_own_s = dscr("cs_own_s", [2, 128, TOKL], F32)
    cs_pre_s = dscr("cs_pre_s", [2, 128, TOK], F32)
    wout_c = dscr("wout_c", [NTD, 128, KC, 512], BF16)
    wdn_c = dscr("wdn_c", [NTD, 128, FC, 512], BF16)
    cache_sem = k.dsem("wcache")
    cache_jobs = []
    for nt_ in range(NTD):
        for k0 in range(0, KC, 8):
            cache_jobs.append((wout_c[nt_][:, k0:k0 + 8, :], wout_d[nt_][:, k0:k0 + 8, :]))

    def cache_step(n=1):
        for _ in range(n):
            if cache_jobs:
                dst_, src_ = cache_jobs.pop(0)
                k.dma("pool", dst_, src_, cache_sem)

    with ExitStack() as gst:
        P = lambda name, shape, dt=F32: sb(gst, nc, name, shape, dt)
        scratch = P("scratch", [128, 8])
        ident_f = P("ident_f", [128, 128])
        ident_b = P("ident_b", [128, 128], BF16)
        rotm_f = P("rotm_f", [128, 128])
        ones_f = P("ones_f", [128, 128])
        tri_b = P("tri_b", [128, 128], BF16)
        flag = P("flag_sb", [128, 1])
        invf = P("invf_sb", [128, 1])
        modA = P("modA", [128, 2 * KC])
        modB = P("modB", [128, 2 * KC])
        nlam = P("nlam", [128, 1])
        gsub = P("gsub_sb", [128, 256])
        wcm = P("wcm_sb", [128, NCC * 3])
        wcf = P("wcf_sb", [128, 2 * FC * 3])
        cond = P("cond", [128, KC], BF16)
        bada = P("bada", [128, 6 * KC])
        gvec = P("gvec", [128, 4 * KC])
        modT = P("modT", [128, 6 * KC])

        cs = k.dsem("const")
        toks = []
        toks.append(k.dma("sp", ident_f[:], ident_d, cs))
        toks.append(k.dma("sp", rotm_f[:], rotm_d, cs))
        toks.append(k.dma("sp", flag[:], flag_d, cs))
        toks.append(k.dma("sp", invf[:], invf_d, cs))
        toks.append(k.dma("sp", gsub[:], gsub_d.partition_broadcast(128), cs))
        toks.append(k.dma("sp", wcm[:], wcm_d, cs))
        toks.append(k.dma("sp", wcf[:], wcf_d, cs))
        ctok = toks[-1]
        k.wait("dve", ctok)
        k.mark("dve", dve.memset(scratch[:], 0.0))
        k.mark("dve", dve.tensor_copy(ident_b[:], ident_f[:]))
        k.mark("dve", dve.memset(ones_f[:], 1.0))
        k.mark("dve", dve.tensor_scalar(gsub[:], gsub[:], 1.0 - LAM_INIT, None, ALU.mult))

        with ExitStack() as st:
            A = lambda name, shape, dt=F32: sb(st, nc, name, shape, dt)
            tri_f = A("tri_f", [128, 128])
            lamb = A("lamb", [128, 4, 128])
            lprod = A("lprod", [128, 2, 128])
            lsum = A("lsum", [128, 4])
            c_sb = A("c_sb", [128, KC])
            wsl = [A(f"wada{i}", [128, KC, 512], BF16) for i in range(2)]
            pm = psb(st, nc, "pm", [128, 512])
            s0 = k.dsem("p0")
            t1 = k.dma("sp", tri_f[:], tri_d, s0)
            t2 = k.dma("sp", lamb[:], lam_d.partition_broadcast(128), s0)
            t3 = k.dma("sp", c_sb[:], cT_d, s0)
            t4 = k.dma("sp", bada[:], badaT_d, s0)
            t5 = k.dma("sp", gvec[:], gvec_d, s0)
            k.wait("dve", [t1, t2, t3, t4, t5])
            k.mark("dve", dve.tensor_copy(tri_b[:], tri_f[:]))
            k.mark("dve", dve.tensor_tensor(lprod[:, 0, :], lamb[:, 0, :], lamb[:, 1, :], ALU.mult))
            tl = k.mark("dve", dve.tensor_tensor(lprod[:, 1, :], lamb[:, 2, :], lamb[:, 3, :], ALU.mult))
            k.wait("dve", tl)
            k.mark("dve", dve.reduce_sum(lsum[:, 0:1], lprod[:, 0, :], AX.X))
            tl = k.mark("dve", dve.reduce_sum(lsum[:, 1:2], lprod[:, 1, :], AX.X))
            k.wait("act", tl)
            tl = k.mark("act", act.activation(out=lsum[:, 2:4], in_=lsum[:, 0:2], func=AF.Exp))
            k.wait("dve", tl)
            tl = k.mark("dve", dve.tensor_tensor(lsum[:, 0:1], lsum[:, 3:4], lsum[:, 2:3], ALU.subtract))
            k.wait("dve", tl)
            k.mark("dve", dve.tensor_scalar(nlam[:], lsum[:, 0:1], -LAM_INIT, None, ALU.add))
            k.wait("act", t3)
            tcond = k.mark("act", act.activation(out=cond[:], in_=c_sb[:], func=AF.Silu))
            wsem = [k.dsem(f"wada{i}") for i in range(2)]
            wfree = [None, None]
            wtok = {}

            def issue_ada(et):
                s = et % 2
                k.wait("pool", wfree[s])
                wtok[et] = k.dma("pool", wsl[s][:], wada_d[et], wsem[s])

            issue_ada(0)
            for ri, (pos_d, n, dst) in enumerate(((pos_own, TOKL, cs_own_s), (pos_pre, TOK, cs_pre_s))):
                pi_ = A(f"pi_{ri}", [128, n], I32)
                ang = A(f"ang{ri}", [128, n], F32)
                tmp = A(f"tmp{ri}", [128, n], F32)
                tabs = [A(f"tab{ri}_{i}", [128, n], F32) for i in range(2)]
                rs = k.dsem("rope")
                tpz = k.dma("sp", pi_[:], pos_d.partition_broadcast(128), rs)
                k.wait("dve", tpz)
                t = k.mark("dve", dve.tensor_copy(ang[:], pi_[:]))
                k.wait("dve", t)
                t = k.mark("dve", dve.tensor_scalar(ang[:], ang[:], invf[:, 0:1], None, ALU.mult))
                for ti, shift in enumerate((math.pi / 2, 0.0)):
                    tab = tabs[ti]
                    k.wait("dve", t)
                    t = k.mark("dve", dve.tensor_scalar(tab[:], ang[:], float(shift), None, ALU.add))
                    k.wait("dve", t)
                    t = k.mark("dve", dve.tensor_scalar(tmp[:], tab[:], float(1.0 / (2 * math.pi)), None, ALU.mult))
                    k.wait("dve", t)
                    t = k.mark("dve", dve.tensor_copy(pi_[:], tmp[:]))
                    k.wait("dve", t)
                    t = k.mark("dve", dve.tensor_copy(tmp[:], pi_[:]))
                    k.wait("dve", t)
                    t = k.mark("dve", dve.scalar_tensor_tensor(tab[:], tmp[:], -float(2 * math.pi), tab[:], ALU.mult, ALU.add))
                    k.wait("dve", t)
                    t = k.mark("dve", dve.tensor_scalar(tmp[:], tab[:], float(math.pi), float(2 * math.pi), ALU.is_gt, ALU.mult))
                    k.wait("dve", t)
                    t = k.mark("dve", dve.tensor_tensor(tab[:], tab[:], tmp[:], ALU.subtract))
                    k.wait("dve", t)
                    t = k.mark("dve", dve.tensor_scalar(tmp[:], tab[:], -float(math.pi), float(2 * math.pi), ALU.is_lt, ALU.mult))
                    k.wait("dve", t)
                    t = k.mark("dve", dve.tensor_tensor(tab[:], tab[:], tmp[:], ALU.add))
                    k.wait("act", t)
                    t2_ = k.mark("act", act.activation(out=tab[:], in_=tab[:], func=AF.Sin))
                    k.wait("sp", t2_)
                    k.dma("sp", dst[ti], tab[:], rs)
            k.wait("pe", tcond)
            lastp = None
            for et in range(NADA):
                if et + 1 < NADA:
                    issue_ada(et + 1)
                s = et % 2
                k.wait("pe", wtok[et])
                for jj in range(4):
                    j = et * 4 + jj
                    for kc in range(KC):
                        ins = pe.matmul(pm[:, j:j + 1], wsl[s][:, kc, jj * 128:(jj + 1) * 128], cond[:, kc:kc + 1],
                                        start=(kc == 0), stop=(kc == KC - 1))
                lastp = k.mark("pe", ins)
                wfree[s] = lastp
            k.wait("dve", lastp)
            tm = k.mark("dve", dve.tensor_tensor(modT[:, 0:2 * KC], pm[:, 0:2 * KC], bada[:, 0:2 * KC], ALU.add))
            k.wait("dve", tm)
            k.mark("dve", dve.tensor_copy(modB[:, 0:KC], modT[:, 0:KC]))
            k.mark("dve", dve.scalar_tensor_tensor(modA[:, 0:KC], modT[:, KC:2 * KC], 1.0, gvec[:, 0:KC], ALU.add, ALU.mult))
            k.barrier(scratch)

        BGS = dict(issued=0, done=0, free=[None, None], tok={}, last=None, slots=None, sems=None, pm2=None, guard=None)

        def bg_compute():
            j = BGS["done"]
            s_ = j % 2
            k.wait("pe", [BGS["tok"][j], BGS["guard"]()])
            for kc in range(KC):
                ins = pe.matmul(BGS["pm2"][:, j:j + 1], BGS["slots"][s_][:, kc, :], cond[:, kc:kc + 1], start=(kc == 0), stop=(kc == KC - 1))
            t = k.mark("pe", ins)
            BGS["free"][s_] = t
            BGS["last"] = t
            BGS["done"] += 1

        def ada_bg(n=1):
            for _ in range(n):
                i = BGS["issued"]
                if i < NB_T:
                    s_ = i % 2
                    k.wait("pool", BGS["free"][s_])
                    BGS["tok"][i] = k.dma("pool", BGS["slots"][s_][:], wadab_d[i], BGS["sems"][s_])
                    BGS["issued"] += 1
                if BGS["done"] < BGS["issued"] - 1:
                    bg_compute()

        def ada_finalize():
            with ExitStack() as st:
                A = lambda name, shape, dt=F32: sb(st, nc, name, shape, dt)
                ggT = A("ggT", [128, 2 * KC])
                diag = [A(f"diag{i}", [128, 128]) for i in range(2)]
                ggrow = A("ggrow", [1, D])
                pr = [psb(st, nc, f"pr{i}", [128, 512]) for i in range(2)]
                k.mark("dve", dve.tensor_copy(modB[:, KC:2 * KC], modT[:, 3 * KC:4 * KC]))
                k.mark("dve", dve.scalar_tensor_tensor(modA[:, KC:2 * KC], modT[:, 4 * KC:5 * KC], 1.0, gvec[:, 2 * KC:3 * KC], ALU.add, ALU.mult))
                k.mark("dve", dve.tensor_tensor(ggT[:, 0:KC], modT[:, 2 * KC:3 * KC], gvec[:, KC:2 * KC], ALU.mult))
                tg = k.mark("dve", dve.tensor_tensor(ggT[:, KC:2 * KC], modT[:, 5 * KC:6 * KC], gvec[:, 3 * KC:4 * KC], ALU.mult))
                k.wait("dve", tg)
                dfree = [None, None]
                pfree = [None, None]
                gs = k.dsem("ggrow")
                for i in range(2):
                    evs = []
                    for kc in range(KC):
                        s = kc % 2
                        k.wait("dve", dfree[s])
                        td = k.mark("dve", dve.tensor_scalar(diag[s][:], ident_f[:], ggT[:, i * KC + kc:i * KC + kc + 1], None, ALU.mult))
                        k.wait("pe", td)
                        k.wait("pe", pfree[s])
                        tp = k.mark("pe", pe.matmul(pr[s][0:1, 0:128], ones_f[:, 0:1], diag[s][:], start=True, stop=True))
                        dfree[s] = tp
                        k.wait("act", tp)
                        te = k.mark("act", act.activation(out=ggrow[0:1, kc * 128:(kc + 1) * 128], in_=pr[s][0:1, 0:128], func=AF.Copy))
                        pfree[s] = te
                        evs.append(te)
                    k.wait("sp", evs)
                    tdm = k.dma("sp", ggrow_s[i:i + 1, :], ggrow[0:1, :], gs)
                    k.wait("act", tdm)
                k.barrier(scratch)

        def nt_alloc(st, name):
            return dict(xn=[sb(st, nc, f"{name}_xn{i}", [128, D], BF16) for i in range(2)],
                        sm=[sb(st, nc, f"{name}_sm{i}", [128, 4], F32) for i in range(2)],
                        xn_free=[None, None], pt_free=[None, None], gi=0, cnt=0)

        def nt_block(N_, b, xap, rdy, done_cb, dstT, mi, pst, store_fn=None):
            s = N_["cnt"] % 2
            N_["cnt"] += 1
            xn, sm = N_["xn"], N_["sm"]
            k.wait("dve", N_["xn_free"][s])
            tz = k.mark("dve", dve.memset(sm[s][:], 0.0))
            k.wait("act", [rdy, tz, N_["xn_free"][s]])
            t1 = k.mark("act", act.activation(out=xn[s][:], in_=xap, func=AF.Square, accum_out=sm[s][:, 0:1]))
            k.wait("act", t1)
            t2 = k.mark("act", act.activation(out=sm[s][:, 1:2], in_=sm[s][:, 0:1], func=AF.Sqrt, scale=1.0 / D, bias=EPS))
            k.wait("dve", t2)
            t3 = k.mark("dve", dve.reciprocal(sm[s][:, 2:3], sm[s][:, 1:2]))
            k.wait("act", t3)
            t4 = k.mark("act", act.activation(out=xn[s][:], in_=xap, func=AF.Identity, scale=sm[s][:, 2:3]))
            done_cb(t4)
            ev_all = []
            for g0 in range(0, KC, 4):
                gi = N_["gi"]
                pt = pst[gi % 2]
                k.wait("pe", [N_["pt_free"][gi % 2], t4])
                n4 = min(4, KC - g0)
                for j in range(n4):
                    kc = g0 + j
                    ins = pe.transpose(pt[:, j * 128:(j + 1) * 128], xn[s][:, kc * 128:(kc + 1) * 128], ident_b[:])
                tp = k.mark("pe", ins)
                evs = []
                for j in range(n4):
                    kc = g0 + j
                    a_ap = modA[:, mi * KC + kc:mi * KC + kc + 1]
                    b_ap = modB[:, mi * KC + kc:mi * KC + kc + 1]
                    o_ap = dstT(kc, b)
                    if gi % 2 == 0:
                        k.wait("dve", tp)
                        evs.append(k.mark("dve", dve.tensor_scalar(o_ap, pt[:, j * 128:(j + 1) * 128], a_ap, b_ap, ALU.mult, ALU.add)))
                    else:
                        k.wait("act", tp)
                        evs.append(k.mark("act", act.activation(out=o_ap, in_=pt[:, j * 128:(j + 1) * 128], func=AF.Identity,
                                                                scale=a_ap, bias=b_ap)))
                N_["pt_free"][gi % 2] = evs
                ev_all += evs
                N_["gi"] += 1
            N_["xn_free"][s] = tp
            if store_fn is not None:
                store_fn(b, ev_all)

        def norm_transpose(st, name, nblk, x_dram, dstT, mi):
            NG = (KC + 3) // 4
            nbk = min(8, NG)
            pst = [psb(st, nc, f"{name}_pst{i}", [128, 1024], BF16) for i in range(nbk)]
            xb = [sb(st, nc, f"{name}_xb{i}", [128, D], F32) for i in range(2)]
            xn = [sb(st, nc, f"{name}_xn{i}", [128, D], BF16) for i in range(2)]
            sm = [sb(st, nc, f"{name}_sm{i}", [128, 4], F32) for i in range(2)]
            xsem = [k.dsem(f"{name}_x{i}") for i in range(2)]
            xfree = [None, None]
            xn_free = [None, None]
            pt_free = [None] * nbk
            ltok = {}

            def issue(b):
                s_ = b % 2
                k.wait("sp", xfree[s_])
                ltok[b] = k.dma("sp", xb[s_][:], x_dram[b * 128:(b + 1) * 128, :], xsem[s_])

            def front_a(b):
                s_ = b % 2
                k.wait("dve", [xn_free[s_], ltok[b]])
                tz = k.mark("dve", dve.memset(sm[s_][:], 0.0))
                k.wait("dve", tz)
                t1 = k.mark("dve", dve.scalar_tensor_tensor(xn[s_][:], xb[s_][:], 1.0, xb[s_][:], ALU.mult, ALU.mult,
                                                            accum_out=sm[s_][:, 0:1]))
                k.wait("act", t1)
                return k.mark("act", act.activation(out=sm[s_][:, 1:2], in_=sm[s_][:, 0:1], func=AF.Sqrt, scale=1.0 / D, bias=EPS))

            def front_b(b, t2):
                s_ = b % 2
                k.wait("dve", t2)
                return k.mark("dve", dve.reciprocal(sm[s_][:, 2:3], sm[s_][:, 1:2]))

            def front(b):
                return front_b(b, front_a(b))

            issue(0)
            if nblk > 1:
                issue(1)
            t3 = front(0)
            gcount = 0
            for b in range(nblk):
                s_ = b % 2
                k.wait("act", t3)
                t4 = k.mark("act", act.activation(out=xn[s_][:], in_=xb[s_][:], func=AF.Identity, scale=sm[s_][:, 2:3]))
                xfree[s_] = t4
                if b + 2 < nblk:
                    issue(b + 2)
                groups = []
                for g0 in range(0, KC, 4):
                    bk = gcount % nbk
                    gcount += 1
                    k.wait("pe", [pt_free[bk], t4])
                    n4 = min(4, KC - g0)
                    for j in range(n4):
                        kc = g0 + j
                        ins = pe.transpose(pst[bk][:, j * 128:(j + 1) * 128], xn[s_][:, kc * 128:(kc + 1) * 128], ident_b[:])
                    tp = k.mark("pe", ins)
                    groups.append((g0, n4, bk, tp))
                xn_free[s_] = tp
                if b + 1 < nblk:
                    t3 = front(b + 1)
                for gi_, (g0, n4, bk, tp) in enumerate(groups):
                    evs = []
                    for j in range(n4):
                        kc = g0 + j
                        a_ap = modA[:, mi * KC + kc:mi * KC + kc + 1]
                        b_ap = modB[:, mi * KC + kc:mi * KC + kc + 1]
                        o_ap = dstT(kc, b)
                        if gi_ % 2 == 0:
                            k.wait("dve", tp)
                            evs.append(k.mark("dve", dve.tensor_scalar(o_ap, pst[bk][:, j * 128:(j + 1) * 128], a_ap, b_ap, ALU.mult, ALU.add)))
                        else:
                            k.wait("act", tp)
                            evs.append(k.mark("act", act.activation(out=o_ap, in_=pst[bk][:, j * 128:(j + 1) * 128], func=AF.Identity,
                                                                    scale=a_ap, bias=b_ap)))
                    pt_free[bk] = evs[-1]

        def dram_loader(st, x_dram, nblk, name):
            xb = [sb(st, nc, f"{name}_xb{i}", [128, D], F32) for i in range(2)]
            sems = [k.dsem(f"{name}{i}") for i in range(2)]
            free = [None, None]
            toks = {}

            def issue(b):
                s = b % 2
                k.wait("sp", free[s])
                toks[b] = k.dma("sp", xb[s][:], x_dram[b * 128:(b + 1) * 128, :], sems[s])

            issue(0)

            def load_fn(b):
                if b + 1 < nblk:
                    issue(b + 1)
                s = b % 2

                def done(t):
                    free[s] = t
                return xb[s][:], toks[b], done
            return load_fn

        def fm_gemm(st, name, chunks, KCx, rhs_fn, tiles, banks, callback, nslots=3, flush=None, bg=0):
            wsl = [sb(st, nc, f"{name}_w{i}", [128, KCx, 128], BF16) for i in range(nslots)]
            wsem = [k.dsem(f"{name}_w{i}") for i in range(nslots)]
            wfree = [None] * nslots
            wtok = {}
            bfree = [None] * len(banks)

            def issue(ci):
                s = ci % nslots
                k.wait("pool", wfree[s])
                wtok[ci] = k.dma("pool", wsl[s][:], chunks[ci][0], wsem[s])
                if bg:
                    cache_step(bg)

            for ci in range(min(nslots - 1, len(chunks))):
                issue(ci)
            bi = 0
            pending = None
            for ci, (_, e) in enumerate(chunks):
                if ci + nslots - 1 < len(chunks):
                    issue(ci + nslots - 1)
                s = ci % nslots
                k.wait("pe", wtok[ci])
                for ti, (c0, w) in enumerate(tiles):
                    bk = bi % len(banks)
                    bi += 1
                    k.wait("pe", bfree[bk])
                    for kc in range(KCx):
                        ins = pe.matmul(banks[bk][:, 0:w], wsl[s][:, kc, :], rhs_fn(kc, c0, w), start=(kc == 0), stop=(kc == KCx - 1))
                    tp = k.mark("pe", ins)
                    if ti == len(tiles) - 1:
                        wfree[s] = tp
                    fr, defer = callback(ci, e, ti, c0, w, banks[bk], tp)
                    bfree[bk] = fr
                    if pending is not None:
                        pending()
                    pending = defer
            if pending is not None:
                pending()
            if flush is not None:
                flush()

        def tm_gemm(st, name, w_d, KCx, groups, lhs_fn, group_load, banks, callback, group_done=None, ksub=4, nring=4, NT=None, nt_hook=None, filler=None):
            ring = [sb(st, nc, f"{name}_r{i}", [128, ksub, 512], BF16) for i in range(nring)]
            rsem = [k.dsem(f"{name}_r{i}") for i in range(nring)]
            rfree = [None] * nring
            subs = [(k0, min(ksub, KCx - k0)) for k0 in range(0, KCx, ksub)]
            seq = [(g, nt, si) for g in range(len(groups)) for nt in range(NT) for si in range(len(subs))]
            rtok = {}

            def issue(i):
                g, nt, si = seq[i]
                k0, nk = subs[si]
                s = i % nring
                k.wait("pool", rfree[s])
                rtok[i] = k.dma("pool", ring[s][:, 0:nk, :], w_d[nt][:, k0:k0 + nk, :], rsem[s])

            for i in range(min(nring - 1, len(seq))):
                issue(i)
            bfree = [None] * len(banks)
            bnext = 0
            i = 0
            gtok = group_load(0) if group_load is not None else None
            for g, blks in enumerate(groups):
                k.wait("pe", gtok)
                for nt in range(NT):
                    mybanks = []
                    for _ in blks:
                        mybanks.append(bnext % len(banks))
                        bnext += 1
                    last = {}
                    for si, (k0, nk) in enumerate(subs):
                        if i + nring - 1 < len(seq):
                            issue(i + nring - 1)
                        s = i % nring
                        k.wait("pe", rtok[i])
                        for bi, blk in enumerate(blks):
                            bk = mybanks[bi]
                            if si == 0:
                                k.wait("pe", bfree[bk])
                            for kk in range(nk):
                                kc = k0 + kk
                                ins = pe.matmul(banks[bk][:, :], lhs_fn(kc, blk, bi), ring[s][:, kk, :], start=(kc == 0), stop=(kc == KCx - 1))
                            if si == len(subs) - 1:
                                last[bi] = k.mark("pe", ins)
                                if bi == len(blks) - 1:
                                    rfree[s] = last[bi]
                            else:
                                if bi == len(blks) - 1:
                                    rfree[s] = k.mark("pe", ins)
                                if filler is not None:
                                    filler(nt)
                        i += 1
                    for bi, blk in enumerate(blks):
                        bfree[mybanks[bi]] = callback(g, blk, bi, nt, banks[mybanks[bi]], last[bi])
                    if nt_hook is not None:
                        nt_hook(g, nt)
                if group_load is not None and g + 1 < len(groups):
                    gtok = group_load(g + 1)
                if group_done is not None:
                    group_done(g, blks)

        def qk_rope_pass(st, name, hT, ntok, tiles, cs_d, jobs):
            cosT = sb(st, nc, name + "_cos", [128, ntok], F32)
            sinT = sb(st, nc, name + "_sin", [128, ntok], F32)
            t32 = [sb(st, nc, f"{name}_t32{i}", [128, 512], F32) for i in range(2)]
            tA = [sb(st, nc, f"{name}_tA{i}", [128, 512], F32) for i in range(2)]
            tB = [sb(st, nc, f"{name}_tB{i}", [128, 512], F32) for i in range(2)]
            stage = [sb(st, nc, f"{name}_st{i}", [128, ntok], BF16) for i in range(2)]
            banks = [psb(st, nc, f"{name}_b{i}", [128, 512]) for i in range(4)]
            rb = [psb(st, nc, f"{name}_rb{i}", [128, 512]) for i in range(2)]
            csem = k.dsem(name + "_cs")
            tc1 = k.dma("sp", cosT[:], cs_d[0], csem)
            tc2 = k.dma("sp", sinT[:], cs_d[1], csem)
            ssem = [k.dsem(f"{name}_st{i}") for i in range(2)]
            state = dict(cnt=0, t32_free=[None, None], rb_free=[None, None], tA_free=[None, None], tB_free=[None, None],
                         st_free=[None, None], nch=0, sums=[])
            chunks = []
            for (w_d, cl, dst_fn) in jobs:
                chunks += [(w_d[e], dst_fn(e)) for e in cl]

            def cb(ci, dst, ti, c0, w, bank, tp):
                s = state["cnt"] % 2
                state["cnt"] += 1
                k.wait("act", [tp, state["t32_free"][s]])
                ta = k.mark("act", act.activation(out=t32[s][:, 0:w], in_=bank[:, 0:w], func=AF.Copy))

                def defer():
                    ss_ = state["nch"] % 2
                    k.wait("pe", [ta, state["rb_free"][s]])
                    tr = k.mark("pe", pe.matmul(rb[s][:, 0:w], rotm_f[:], t32[s][:, 0:w], start=True, stop=True))
                    k.wait("dve", [ta, tc1, state["tA_free"][s]])
                    t_a = k.mark("dve", dve.tensor_tensor(tA[s][:, 0:w], t32[s][:, 0:w], cosT[:, c0:c0 + w], ALU.mult))
                    k.wait("dve", [tr, tc2, state["tB_free"][s]])
                    t_b = k.mark("dve", dve.tensor_tensor(tB[s][:, 0:w], rb[s][:, 0:w], sinT[:, c0:c0 + w], ALU.mult))
                    state["rb_free"][s] = t_b
                    state["t32_free"][s] = [t_a, tr]
                    k.wait("dve", [t_a, t_b])
                    if ti == 0:
                        k.wait("dve", state["st_free"][ss_])
                    t_s = k.mark("dve", dve.tensor_tensor(stage[ss_][:, c0:c0 + w], tA[s][:, 0:w], tB[s][:, 0:w], ALU.add))
                    state["tA_free"][s] = t_s
                    state["tB_free"][s] = t_s
                    state["sums"].append(t_s)
                    if ti == len(tiles) - 1:
                        k.wait("sp", state["sums"])
                        state["sums"] = []
                        state["st_free"][ss_] = k.dma("sp", dst, stage[ss_][:], ssem[ss_])
                        state["nch"] += 1
                return ta, defer
            fm_gemm(st, name, chunks, KC, lambda kc, c0, w: hT[:, kc, c0:c0 + w], tiles, banks, cb, nslots=2)

        def v_pass(st, name, hT, blks_cols, dst_row0, use_flag):
            banks = [psb(st, nc, f"{name}_b{i}", [128, 512]) for i in range(8)]
            vst = [sb(st, nc, f"{name}_vs{i}", [128, 512], BF16) for i in range(4)]
            vsem = [k.dsem(f"{name}_vs{i}") for i in range(4)]
            vfree = [None] * 4
            cnt = [0]
            nb_ = len(blks_cols)
            groups = [list(range(g0, min(g0 + 6, nb_))) for g0 in range(0, nb_, 6)]

            def cb(g, blk, bi, nt, bank, tp):
                s = cnt[0] % 4
                cnt[0] += 1
                k.wait("act", [tp, vfree[s]])
                if use_flag:
                    te = k.mark("act", act.activation(out=vst[s][:], in_=bank[:, :], func=AF.Identity, scale=flag[:, 0:1]))
                else:
                    te = k.mark("act", act.activation(out=vst[s][:], in_=bank[:, :], func=AF.Copy))
                k.wait("sp", te)
                r0 = dst_row0 + blk * 128
                vfree[s] = k.dma("sp", v_s[r0:r0 + 128, nt * 512:(nt + 1) * 512], vst[s][:], vsem[s])
                return te
            tm_gemm(st, name, wv_d, KC, groups, lambda kc, blk, bi: hT[:, kc, blks_cols[blk]:blks_cols[blk] + 128], None, banks, cb,
                    ksub=8 if KC >= 8 else KC, nring=4, NT=NTV)

        if STOP < 1:
            return nc
        with ExitStack() as st1:
            hTp = sb(st1, nc, "hTp", [128, KC, TOK], BF16)
            with ExitStack() as st:
                norm_transpose(st, "p1n", NB, x_pre, lambda kc, b: hTp[:, kc, b * 128:(b + 1) * 128], 0)
                k.barrier(scratch)
            if STOP < 1.3:
                return nc
            with ExitStack() as st:
                qk_rope_pass(st, "p1k", hTp, TOK, tok_tiles(TOK), cs_pre_s,
                             [(wk_d, list(range(2 * H)), lambda e: kT_s[e][:, 0:TOK])])
                k.barrier(scratch)
            if STOP < 1.6:
                return nc
            with ExitStack() as st:
                v_pass(st, "p1v", hTp, [b * 128 for b in range(NB)], 0, True)
                k.barrier(scratch)

        if STOP < 2:
            return nc
        with ExitStack() as st2:
            hT = sb(st2, nc, "hT", [128, KC, TOKL], BF16)
            with ExitStack() as st:
                norm_transpose(st, "p2n", NBL, x_own, lambda kc, b: hT[:, kc, b * 128:(b + 1) * 128], 0)
                k.barrier(scratch)
            tiles_l = tok_tiles(TOKL, first=HW)
            with ExitStack() as st:
                qk_rope_pass(st, "p2qk", hT, TOKL, tiles_l, cs_own_s,
                             [(wq_d, list(range(2 * H)), lambda e: qT_s[e]),
                              (wk_d, list(range(2 * H)), lambda e: kT_s[e][:, TOK - HW:2 * TOK])])
                k.barrier(scratch)
            with ExitStack() as st:
                Cb = sb(st, nc, "cv_C", [128, TOKL], F32)
                mb = sb(st, nc, "cv_m", [128, TOKL + 2], F32)
                stg = [sb(st, nc, f"cv_st{i}", [128, TOKL], BF16) for i in range(2)]
                banks = [psb(st, nc, f"cv_b{i}", [128, 512]) for i in range(6)]
                ssem = [k.dsem(f"cv_st{i}") for i in range(2)]
                S_ = dict(C_free=None, m_free=None, st_free=[None, None], Ctoks=[], mtoks=[], Btoks=[], convtok=None)
                tiles_c = [(HW - HN, HN)] + tiles_l[1:]
                k.mark("dve", dve.memset(mb[:], 0.0))
                k.mark("dve", dve.memset(Cb[:], 0.0))
                k.mark("dve", dve.memset(stg[0][:], 0.0))
                tz = k.mark("dve", dve.memset(stg[1][:], 0.0))
                k.wait("act", tz)

                def cb(ci, e, ti, c0, w, bank, tp):
                    j, kind = divmod(e, 3)
                    if kind == 0:
                        k.wait("act", [tp, S_["C_free"]] if ti == 0 else tp)
                        t = k.mark("act", act.activation(out=Cb[:, c0:c0 + w], in_=bank[:, 0:w], func=AF.Copy))
                        S_["Ctoks"].append(t)
                        return t, None
                    if kind == 1:
                        k.wait("dve", [tp, tz] + S_["Ctoks"] + ([S_["m_free"]] if ti == 0 else []))
                        t = k.mark("dve", dve.tensor_tensor(mb[:, 2 + c0:2 + c0 + w], bank[:, 0:w], Cb[:, c0:c0 + w], ALU.mult))
                        S_["mtoks"].append(t)
                        if ti == len(tiles_l) - 1:
                            k.wait("dve", S_["mtoks"])
                            t0 = k.mark("dve", dve.tensor_scalar(mb[:, 2:2 + HW], mb[:, 2:2 + HW], flag[:, 0:1], None, ALU.mult))
                            k.wait("dve", t0)
                            wj = lambda tap: wcm[:, j * 3 + tap:j * 3 + tap + 1]
                            t1 = k.mark("dve", dve.tensor_scalar(Cb[:, :], mb[:, 2:2 + TOKL], wj(2), None, ALU.mult))
                            k.wait("dve", t1)
                            t2 = k.mark("dve", dve.scalar_tensor_tensor(Cb[:, :], mb[:, 1:1 + TOKL], wj(1), Cb[:, :], ALU.mult, ALU.add))
                            k.wait("dve", t2)
                            t3 = k.mark("dve", dve.scalar_tensor_tensor(Cb[:, :], mb[:, 0:TOKL], wj(0), Cb[:, :], ALU.mult, ALU.add))
                            S_["convtok"] = t3
                            S_["m_free"] = t3
                            S_["Ctoks"] = []
                            S_["mtoks"] = []
                        return t, None
                    s = j % 2
                    k.wait("dve", [tp, S_["convtok"]] + ([S_["st_free"][s]] if ti == 0 else []))
                    t = k.mark("dve", dve.tensor_tensor(stg[s][:, c0:c0 + w], bank[:, 0:w], Cb[:, c0:c0 + w], ALU.mult))
                    S_["Btoks"].append(t)
                    if ti == len(tiles_l) - 1:
                        k.wait("sp", S_["Btoks"])
                        for g_, (gs_, gn_) in enumerate(OG):
                            S_["st_free"][s] = k.dma("sp", mixT_s[g_][:, AW // 128 + j, 0:gn_], stg[s][:, gs_:gs_ + gn_], ssem[s])
                        S_["C_free"] = S_["Btoks"][-1]
                        S_["C_free"] = list(S_["Btoks"])
                        S_["Btoks"] = []
                    return t, None
                fm_gemm(st, "cv", [(wcv_d[e], e) for e in range(3 * NCC)], KC, lambda kc, c0, w: hT[:, kc, c0:c0 + w], tiles_c, banks, cb, bg=2)
                while cache_jobs:
                    cache_step()
                k.barrier(scratch)
            with ExitStack() as st:
                v_pass(st, "p2v", hT, [HW + b * 128 for b in range(NB)], TOK, False)
                k.barrier(scratch)

        if STOP < 3:
            return nc
        with ExitStack() as st:
            A = lambda name, shape, dt=F32: sb(st, nc, name, shape, dt)
            NKB = 2 * NB
            kTb = [A(f"at_k{i}", [128, 2, 2 * TOK], BF16) for i in range(2)]
            qTb = [A(f"at_q{i}", [128, 2, TOKL], BF16) for i in range(2)]
            va = [A(f"at_v{i}", [128, NKB, 257], BF16) for i in range(2)]
            mixst = [A(f"at_m{i}", [128, 2, TOKL], BF16) for i in range(2)]
            pT = [A(f"at_p{i}", [128, 512], BF16) for i in range(3)]
            Osb = [[A(f"at_o{c}{q}", [128, 257]) for q in range(4)] for c in range(2)]
            sm = [A(f"at_sm{q}", [128, 8]) for q in range(4)]
            ta_ = [A(f"at_ta{q}", [128, 256]) for q in range(4)]
            tb_ = [A(f"at_tb{q}", [128, 256]) for q in range(4)]
            ssall = A("at_ssall", [128, 12])
            ssall_free = [None]
            atb = [A(f"at_ab{q}", [128, 256], BF16) for q in range(4)]
            sbank = [psb(st, nc, f"at_sb{i}", [128, 512]) for i in range(3)]
            obank = [psb(st, nc, f"at_ob{i}", [128, 512]) for i in range(4)]
            tb_all = psb(st, nc, "at_tball", [128, 512], F32)
            tbank = [tb_all[:, 0:128].bitcast(BF16)]
            BGS["pm2"] = tb_all[:, 256:512]
            BGS["slots"] = [A(f"adab{i}", [128, KC, 128], BF16) for i in range(2)]
            BGS["sems"] = [k.dsem("adab0"), k.dsem("adab1")]
            BGS["guard"] = lambda: tfree[0]
            bgstep = [0]
            lsem = [k.dsem(f"at_ld{i}") for i in range(2)]
            msem = [k.dsem(f"at_ms{i}") for i in range(2)]
            for i in range(2):
                k.mark("pool", pool.memset(va[i][:, :, 256:257], 1.0))
                tv = k.mark("pool", pool.tensor_scalar(va[i][:, 0:NB, 256:257], va[i][:, 0:NB, 256:257], flag[:, 0:1], None, ALU.mult))
            head_free = [None, None]
            mix_free = [None, None]
            ltok = {}

            def issue_head(h):
                s = h % 2
                k.wait("sp", head_free[s])
                k.dma("sp", kTb[s][:], kT_s[2 * h:2 * h + 2].rearrange("c p t -> p c t"), lsem[s])
                k.dma("sp", qTb[s][:], qT_s[2 * h:2 * h + 2].rearrange("c p t -> p c t"), lsem[s])
                ltok[h] = k.dma("sp", va[s][:, :, 0:256], v_s[:, h * 256:(h + 1) * 256].rearrange("(kb p) e -> p kb e", p=128), lsem[s])

            issue_head(0)
            qtiles = [(0, HW, [NB - 1])]
            for (c0, w) in tok_tiles(TOK):
                qtiles.append((HW + c0, w, [NB + (c0 + i * 128) // 128 for i in range(w // 128)]))
            sfree = [None] * 3
            pfree = [None] * 3
            ofree = [None] * 4
            osb_free = [[None] * 4 for _ in range(2)]
            tfree = [None] * 1
            atb_free = [None] * 4
            comb_free = [None] * 4
            sidx = [0]
            for h in range(H):
                s = h % 2
                if h + 1 < H:
                    issue_head(h + 1)
                k.wait("pe", [ltok[h], tv])
                lastpe = None
                mixtoks = []
                epi_q = []
                st2_q = []
                for (c0, w, diags) in qtiles:
                    nqb = len(diags)
                    kmax = diags[-1]
                    for c in range(2):
                        pend = []
                        for kb in range(kmax + 1):
                            qb0 = 0
                            while diags[qb0] < kb:
                                qb0 += 1
                            qoff = qb0 * 128
                            r = sidx[0] % 3
                            sidx[0] += 1
                            k.wait("pe", [sfree[r]])
                            tS = k.mark("pe", pe.matmul(sbank[r][:, qoff:w], kTb[s][:, c, kb * 128:(kb + 1) * 128],
                                                        qTb[s][:, c, c0 + qoff:c0 + w], start=True, stop=True))
                            k.wait("act", [tS, pfree[r]])
                            tE = k.mark("act", act.activation(out=pT[r][:, qoff:w], in_=sbank[r][:, qoff:w], func=AF.Exp, scale=SCALE))
                            sfree[r] = tE
                            tP = tE
                            if kb in diags:
                                qd = diags.index(kb)
                                k.wait("dve", tE)
                                tP = k.mark("dve", dve.tensor_tensor(pT[r][:, qd * 128:(qd + 1) * 128], pT[r][:, qd * 128:(qd + 1) * 128],
                                                                     tri_b[:], ALU.mult))

                            def do_pv(kb=kb, qb0=qb0, r=r, tP=tP, tE=tE):
                                k.wait("pe", [tP, tE])
                                for qb in range(qb0, nqb):
                                    if kb == 0:
                                        k.wait("pe", ofree[qb])
                                    ins = pe.matmul(obank[qb][:, 0:257], pT[r][:, qb * 128:(qb + 1) * 128], va[s][:, kb, :],
                                                    start=(kb == 0), stop=(kb == diags[qb]))
                                    if kb == diags[qb]:
                                        to = k.mark("pe", ins)
                                        eng = "act" if qb % 2 == 0 else "dve"
                                        k.wait(eng, [to, osb_free[c][qb]])
                                        if eng == "act":
                                            te = k.mark("act", act.activation(out=Osb[c][qb][:], in_=obank[qb][:, 0:257], func=AF.Copy))
                                        else:
                                            te = k.mark("dve", dve.tensor_copy(Osb[c][qb][:], obank[qb][:, 0:257]))
                                        ofree[qb] = te
                                        osb_free[c][qb] = te
                                pfree[r] = k.mark("pe", ins) if kb != diags[nqb - 1] else (k.prog["pe"], k.prog["pe"].n)
                            pend.append(do_pv)
                            if len(pend) > 2:
                                pend.pop(0)()
                            bgstep[0] += 1
                            if bgstep[0] % 8 == 0:
                                ada_bg(1)
                            if kb == kmax // 2 and c == 0:
                                while st2_q:
                                    st2_q.pop(0)()
                            if kb == kmax and c == 0:
                                while st2_q:
                                    st2_q.pop(0)()
                                while epi_q:
                                    epi_q.pop(0)()
                        while pend:
                            pend.pop(0)()
                    k.wait("dve", ssall_free[0])
                    for qb in range(nqb):
                        O1, O2, m_ = Osb[0][qb], Osb[1][qb], sm[qb]
                        k.wait("dve", [osb_free[0][qb], osb_free[1][qb], comb_free[qb]])
                        t = k.mark("dve", dve.tensor_scalar(m_[:, 0:1], O1[:, 256:257], 1e-30, None, ALU.add))
                        t = k.mark("dve", dve.tensor_scalar(m_[:, 1:2], O2[:, 256:257], 1e-30, None, ALU.add))
                        k.wait("dve", t)
                        t = k.mark("dve", dve.reciprocal(m_[:, 2:4], m_[:, 0:2]))
                        k.wait("dve", t)
                        t = k.mark("dve", dve.tensor_tensor(m_[:, 3:4], m_[:, 3:4], nlam[:, 0:1], ALU.mult))
                        t1 = k.mark("dve", dve.tensor_scalar(ta_[qb][:], O1[:, 0:256], m_[:, 2:3], None, ALU.mult))
                        k.wait("dve", [t, t1])
                        t = k.mark("dve", dve.scalar_tensor_tensor(tb_[qb][:], O2[:, 0:256], m_[:, 3:4], ta_[qb][:], ALU.mult, ALU.add))
                        osb_free[0][qb] = t
                        osb_free[1][qb] = t
                        k.wait("dve", t)
                        t = k.mark("dve", dve.tensor_tensor(ta_[qb][:], tb_[qb][:], tb_[qb][:], ALU.mult))
                        k.wait("dve", t)
                        tss = k.mark("dve", dve.reduce_sum(ssall[:, qb:qb + 1], ta_[qb][:], AX.X))
                    fin = {}

                    def stage2(nqb=nqb, tss=tss, fin=fin):
                        k.wait("act", tss)
                        tl = k.mark("act", act.activation(out=ssall[:, 4:4 + nqb], in_=ssall[:, 0:nqb], func=AF.Ln, scale=1.0 / 256, bias=EPS))
                        k.wait("act", tl)
                        te_ = k.mark("act", act.activation(out=ssall[:, 8:8 + nqb], in_=ssall[:, 4:4 + nqb], func=AF.Exp, scale=-0.5))
                        k.wait("dve", te_)
                        for qb in range(nqb):
                            k.wait("dve", atb_free[qb])
                            t = k.mark("dve", dve.scalar_tensor_tensor(atb[qb][:], tb_[qb][:], ssall[:, 8 + qb:9 + qb], gsub[:], ALU.mult, ALU.mult))
                            comb_free[qb] = t
                            ssall_free[0] = t
                            fin[qb] = t
                    st2_q.append(stage2)
                    for qb in range(nqb):
                        def epi(qb=qb, fin=fin, c0=c0, s=s):
                            tb2 = 0
                            k.wait("pe", [fin[qb], tfree[tb2]])
                            for j in range(2):
                                ins = pe.transpose(tbank[tb2][:, j * 128:(j + 1) * 128], atb[qb][:, j * 128:(j + 1) * 128], ident_b[:])
                            tt = k.mark("pe", ins)
                            atb_free[qb] = tt
                            k.wait("act", [tt, mix_free[s]])
                            col = c0 + qb * 128
                            te = k.mark("act", act.activation(out=mixst[s][:, :, col:col + 128],
                                                              in_=tbank[tb2][:, 0:256].rearrange("p (j c) -> p j c", j=2), func=AF.Copy))
                            tfree[tb2] = te
                            mixtoks.append(te)
                        epi_q.append(epi)
                while st2_q:
                    st2_q.pop(0)()
                while epi_q:
                    epi_q.pop(0)()
                head_free[s] = (k.prog["pe"], k.prog["pe"].n)
                k.wait("sp", mixtoks)
                for g_, (gs_, gn_) in enumerate(OG):
                    mix_free[s] = k.dma("sp", mixT_s[g_][:, 2 * h:2 * h + 2, 0:gn_], mixst[s][:, :, gs_:gs_ + gn_], msem[s])
            while BGS["issued"] < NB_T:
                ada_bg(1)
            while BGS["done"] < BGS["issued"]:
                bg_compute()
            k.wait("dve", [BGS["last"], tfree[0]])
            tmz = k.mark("dve", dve.tensor_tensor(modT[:, 2 * KC:6 * KC], BGS["pm2"][:, 0:4 * KC], bada[:, 2 * KC:6 * KC], ALU.add))
            k.barrier(scratch)
        ada_finalize()

        def tm_phase(name, srcT_s, KCx, w_d, gsplit, ggi, resid_fn, final):
            GB = 4
            groups = [list(range(gs_ // 128, (gs_ + gn_) // 128)) for (gs_, gn_) in gsplit]
            nx = 1
            with ExitStack() as st:
                A = lambda nm, shape, dt=F32: sb(st, nc, f"{name}_{nm}", shape, dt)
                src = A("src", [128, KCx, GB * 128], BF16)
                yb = [A(f"yb{i}", [128, D]) for i in range(GB)]
                xb = [A(f"xb{i}", [128, D]) for i in range(nx)]
                ggrow = A("gg", [128, D])
                ssq = [A(f"ssq{i}", [128, NTD + 8]) for i in range(GB)]
                jk = A("jk", [128, 512], BF16)
                nbanks = 8 if final else 6
                banks = [psb(st, nc, f"{name}_b{i}", [128, 512]) for i in range(nbanks)]
                gsem = k.dsem(name + "_gg")
                tgg = k.dma("sp", ggrow[:], ggrow_s[ggi].partition_broadcast(128), gsem)
                ssem = k.dsem(name + "_src")
                xsem = [k.dsem(f"{name}_x{i}") for i in range(nx)]
                osem = [k.dsem(f"{name}_o{i}") for i in range(nx)]
                S_ = dict(yb_free=[None] * GB, xb_free=[None] * nx, xcnt=0, sq=[[] for _ in range(GB)])
                if not final:
                    pst = [psb(st, nc, f"{name}_pst{i}", [128, 1024], BF16) for i in range(2)]
                    h2st = [A(f"h2st{i}", [128, KC, 128], BF16) for i in range(2)]
                    hsem = [k.dsem(f"{name}_h{i}") for i in range(2)]
                    h2free = [None, None]
                    xnb = [A(f"xn{i}", [128, D], BF16) for i in range(GB)]
                    smb = [A(f"sm{i}", [128, 4]) for i in range(GB)]
                    xn_free = [None] * GB
                    pt_free = [None, None]
                    backs = []
                    gcnt = [0]

                def group_load(g):
                    n = len(groups[g]) * 128
                    k.wait("sp", (k.prog["pe"], k.prog["pe"].n))
                    return k.dma("sp", src[:, :, 0:n], srcT_s[g][:, :, 0:n], ssem)

                def cb(g, blk, bi, nt, bank, tp):
                    if nt == 0:
                        k.wait("dve", S_["yb_free"][bi])
                        tz = k.mark("dve", dve.memset(ssq[bi][:], 0.0))
                        k.wait("act", [tz, S_["yb_free"][bi]])
                    k.wait("dve", tp)
                    t1 = k.mark("dve", dve.tensor_copy(yb[bi][:, nt * 512:(nt + 1) * 512], bank[:, :]))
                    k.wait("act", t1)
                    t2 = k.mark("act", act.activation(out=jk[:], in_=yb[bi][:, nt * 512:(nt + 1) * 512], func=AF.Square,
                                                      accum_out=ssq[bi][:, nt:nt + 1]))
                    S_["sq"][bi] += [t1, t2]
                    return t1

                def group_done(g, blks):
                    if not final:
                        while backs:
                            for _ in backs.pop(0):
                                pass
                    for bi, blk in enumerate(blks):
                        m_ = ssq[bi]
                        xs = S_["xcnt"] % nx
                        S_["xcnt"] += 1
                        k.wait("sp", S_["xb_free"][xs])
                        tx = k.dma("sp", xb[xs][:], resid_fn(blk), xsem[xs])
                        k.wait("dve", S_["sq"][bi])
                        S_["sq"][bi] = []
                        t = k.mark("dve", dve.reduce_sum(m_[:, NTD:NTD + 1], m_[:, 0:NTD], AX.X))
                        k.wait("act", t)
                        t = k.mark("act", act.activation(out=m_[:, NTD + 1:NTD + 2], in_=m_[:, NTD:NTD + 1], func=AF.Sqrt, scale=1.0 / D, bias=EPS))
                        k.wait("dve", t)
                        t = k.mark("dve", dve.reciprocal(m_[:, NTD + 2:NTD + 3], m_[:, NTD + 1:NTD + 2]))
                        k.wait("dve", [t, tgg])
                        t = k.mark("dve", dve.scalar_tensor_tensor(yb[bi][:], yb[bi][:], m_[:, NTD + 2:NTD + 3], ggrow[:], ALU.mult, ALU.mult))
                        k.wait("dve", [t, tx])
                        t = k.mark("dve", dve.tensor_tensor(yb[bi][:], yb[bi][:], xb[xs][:], ALU.add))
                        S_["xb_free"][xs] = t
                        if final:
                            k.wait("sp", t)
                            S_["yb_free"][bi] = k.dma("sp", out_d[blk * 128:(blk + 1) * 128, :], yb[bi][:], osem[xs])
                        else:
                            stoks = []
                            if blk >= 1:
                                k.wait("sp", t)
                                stoks.append(k.dma("sp", xmid_s[(blk - 1) * 128:blk * 128, :], yb[bi][:], osem[xs]))

                            sm_ = smb[bi]
                            k.wait("dve", xn_free[bi])
                            tz = k.mark("dve", dve.memset(sm_[:], 0.0))
                            k.wait("act", [t, tz, xn_free[bi]])
                            t1 = k.mark("act", act.activation(out=xnb[bi][:], in_=yb[bi][:], func=AF.Square, accum_out=sm_[:, 0:1]))
                            k.wait("act", t1)
                            t2 = k.mark("act", act.activation(out=sm_[:, 1:2], in_=sm_[:, 0:1], func=AF.Sqrt, scale=1.0 / D, bias=EPS))
                            k.wait("dve", t2)
                            t3 = k.mark("dve", dve.reciprocal(sm_[:, 2:3], sm_[:, 1:2]))
                            k.wait("act", t3)
                            t4 = k.mark("act", act.activation(out=xnb[bi][:], in_=yb[bi][:], func=AF.Identity, scale=sm_[:, 2:3]))
                            S_["yb_free"][bi] = [t4] + stoks

                            def back(blk=blk, bi=bi, t4=t4):
                                hs = blk % 2
                                ev_all = []
                                for g0 in range(0, KC, 4):
                                    gi = gcnt[0]
                                    gcnt[0] += 1
                                    pt = pst[gi % 2]
                                    k.wait("pe", [pt_free[gi % 2], t4])
                                    n4 = min(4, KC - g0)
                                    for j in range(n4):
                                        kc = g0 + j
                                        ins = pe.transpose(pt[:, j * 128:(j + 1) * 128], xnb[bi][:, kc * 128:(kc + 1) * 128], ident_b[:])
                                    tp = k.mark("pe", ins)
                                    evs = []
                                    for j in range(n4):
                                        kc = g0 + j
                                        a_ap = modA[:, KC + kc:KC + kc + 1]
                                        b_ap = modB[:, KC + kc:KC + kc + 1]
                                        if g0 == 0 and j == 0:
                                            k.wait("dve", h2free[hs])
                                            k.wait("act", h2free[hs])
                                        o_ap = h2st[hs][:, kc, :]
                                        if gi % 2 == 0:
                                            k.wait("dve", tp)
                                            evs.append(k.mark("dve", dve.tensor_scalar(o_ap, pt[:, j * 128:(j + 1) * 128], a_ap, b_ap, ALU.mult, ALU.add)))
                                        else:
                                            k.wait("act", tp)
                                            evs.append(k.mark("act", act.activation(out=o_ap, in_=pt[:, j * 128:(j + 1) * 128], func=AF.Identity,
                                                                                    scale=a_ap, bias=b_ap)))
                                    pt_free[gi % 2] = evs[-1]
                                    ev_all += evs
                                    if g0 + 4 < KC:
                                        yield
                                xn_free[bi] = tp
                                k.wait("sp", ev_all)
                                h2free[hs] = k.dma("sp", h2T_s[:, :, blk * 128:(blk + 1) * 128].rearrange("kc p t -> p kc t"),
                                                   h2st[hs][:], hsem[hs])
                            backs.append(back())

                def filler(nt):
                    if final or nt < min(NTD // 2, NTD - 1):
                        return
                    while backs:
                        try:
                            next(backs[0])
                            return
                        except StopIteration:
                            backs.pop(0)

                tm_gemm(st, name, w_d, KCx, groups, lambda kc, blk, bi: src[:, kc, bi * 128:(bi + 1) * 128], group_load, banks, cb,
                        group_done=group_done, ksub=(2 if final else 4), nring=4, NT=NTD, filler=filler)
                if not final:
                    while backs:
                        for _ in backs.pop(0):
                            pass
                k.barrier(scratch)

        if STOP < 4:
            return nc
        tm_phase("op", mixT_s, KC, wout_c, OG, 0,
                 lambda blk: x_own[blk * 128:(blk + 1) * 128, :], False)

        if STOP < 5:
            return nc
        with ExitStack() as st:
            A = lambda name, shape, dt=F32: sb(st, nc, name, shape, dt)
            h2T = A("h2T", [128, KC, TOKL], BF16)
            U = A("up_U", [128, TOKL + 2])
            CG = A("up_CG", [128, TOKL])
            CV = A("up_CV", [128, TOKL])
            gst_ = [A(f"up_g{i}", [128, TOK], BF16) for i in range(2)]
            banks = [psb(st, nc, f"up_b{i}", [128, 512]) for i in range(8)]
            hs_ = k.dsem("up_h2")
            th = k.dma("sp", h2T[:], h2T_s.rearrange("kc p t -> p kc t"), hs_)
            k.wait("pe", th)
            gsem = [k.dsem(f"up_g{i}") for i in range(2)]
            tiles_l = [(HW - HN, HN)] + tok_tiles(TOKL, first=HW)[1:]
            for nt_ in range(NTD):
                for k0 in range(0, FC, 22):
                    k1 = min(FC, k0 + 22)
                    cache_jobs.append((wdn_c[nt_][:, k0:k1, :], wdn_d[nt_][:, k0:k1, :]))
            tz = k.mark("dve", dve.memset(U[:], 0.0))
            S_ = dict(U_free=None, CG_free=None, CV_free=None, g_free=[None, None], ev=[])

            def cb(ci, e, ti, c0, w, bank, tp):
                j, kind = divmod(e, 2)
                k.wait("act", [tp, tz] + ([S_["U_free"]] if ti == 0 else []))
                if ti == 0:
                    t = k.mark("act", act.activation(out=U[:, 2 + c0:2 + c0 + w], in_=bank[:, 0:w], func=AF.Identity, scale=flag[:, 0:1]))
                else:
                    t = k.mark("act", act.activation(out=U[:, 2 + c0:2 + c0 + w], in_=bank[:, 0:w], func=AF.Copy))
                S_["ev"].append(t)
                if ti == len(tiles_l) - 1:
                    dst = CG if kind == 0 else CV
                    wj = lambda tap: wcf[:, e * 3 + tap:e * 3 + tap + 1]
                    k.wait("dve", S_["ev"] + [S_["CG_free"] if kind == 0 else S_["CV_free"]])
                    S_["ev"] = []
                    t1 = k.mark("dve", dve.tensor_scalar(dst[:, :], U[:, 2:2 + TOKL], wj(2), None, ALU.mult))
                    k.wait("dve", t1)
                    t2 = k.mark("dve", dve.scalar_tensor_tensor(dst[:, :], U[:, 1:1 + TOKL], wj(1), dst[:, :], ALU.mult, ALU.add))
                    k.wait("dve", t2)
                    t3 = k.mark("dve", dve.scalar_tensor_tensor(dst[:, :], U[:, 0:TOKL], wj(0), dst[:, :], ALU.mult, ALU.add))
                    S_["U_free"] = t3
                    if kind == 0:
                        k.wait("act", t3)
                        S_["sil"] = k.mark("act", act.activation(out=CG[:, HW:], in_=CG[:, HW:], func=AF.Silu))
                    else:
                        s = j % 2
                        k.wait("dve", [t3, S_["sil"], S_["g_free"][s]])
                        tg = k.mark("dve", dve.tensor_tensor(gst_[s][:], CG[:, HW:], CV[:, HW:], ALU.mult))
                        S_["CG_free"] = tg
                        S_["CV_free"] = tg
                        k.wait("sp", tg)
                        for g_, (gs_, gn_) in enumerate(DG):
                            S_["g_free"][s] = k.dma("sp", gT_s[g_][:, j, 0:gn_], gst_[s][:, gs_:gs_ + gn_], gsem[s])
                return t, None
            fm_gemm(st, "up", [(wup_d[e], e) for e in range(2 * FC)], KC, lambda kc, c0, w: h2T[:, kc, c0:c0 + w], tiles_l, banks, cb, nslots=2, bg=1)
            while cache_jobs:
                cache_step()
            k.barrier(scratch)

        if STOP < 6:
            return nc
        tm_phase("dn", gT_s, FC, wdn_c, DG, 1,
                 lambda blk: xmid_s[blk * 128:(blk + 1) * 128, :], True)
        k.barrier(scratch)
    return nc


def _fm(W, cols):
    K = W.shape[0]
    sub = W[:, cols]
    ne = sub.shape[1] // 128
    return np.ascontiguousarray(sub.reshape(K // 128, 128, ne, 128).transpose(2, 1, 0, 3))


def _tm(W):
    K, N = W.shape
    return np.ascontiguousarray(W.reshape(K // 128, 128, N // 512, 512).transpose(2, 1, 0, 3))


def _featT(v):
    return np.ascontiguousarray(v.reshape(-1, 128).T)


def prepare(cfg, inp):
    D, S, H, DFF, B = cfg["D"], cfg["S"], cfg["H"], cfg["DFF"], cfg["B"]
    KC = D // 128
    TOK = S // 2
    HW = 128
    AW = H * 256
    CW = D - AW
    NCC = CW // 128
    FC = DFF // 128
    f32 = np.float32
    x = np.asarray(inp["x"], f32)
    c = np.asarray(inp["c"], f32)
    pos = np.asarray(inp["positions"], np.int32)
    w_in = np.asarray(inp["w_in"][0], f32)
    ar = np.arange
    shared = {}
    shared["ident"] = np.eye(128, dtype=f32)
    rot = np.zeros((128, 128), f32)
    for do in range(64):
        rot[do + 64, do] = -1.0
    for do in range(64, 128):
        rot[do - 64, do] = 1.0
    shared["rotm"] = rot
    invf = (ROPE_THETA ** (-(np.arange(0, 128, 2, dtype=np.float32)) / np.float32(128))).astype(f32)
    shared["invf"] = np.concatenate([invf, invf]).reshape(128, 1).astype(f32)
    shared["tri"] = (ar(128)[:, None] <= ar(128)[None, :]).astype(f32)
    w_ada = np.asarray(inp["w_ada"][0], f32)
    shared["wada"] = _tm(w_ada[:, :2 * D])
    shared["wadab"] = _fm(w_ada, ar(2 * D, 6 * D))
    shared["badaT"] = _featT(np.asarray(inp["b_ada"][0], f32))
    shared["gvec"] = np.concatenate([_featT(np.asarray(inp[n][0], f32)) for n in ("g_pre_mix", "g_post_mix", "g_pre_ffn", "g_post_ffn")], axis=1)
    shared["lamv"] = np.stack([np.asarray(inp[n][0], f32) for n in ("lambda_q1", "lambda_k1", "lambda_q2", "lambda_k2")])
    shared["gsub"] = np.asarray(inp["g_subln"][0], f32)
    shared["wq"] = _fm(w_in, ar(0, AW))
    shared["wk"] = _fm(w_in, ar(AW, 2 * AW))
    shared["wv"] = _tm(w_in[:, 2 * AW:3 * AW])
    oB, oC, oH = 3 * AW, 3 * AW + CW, 3 * AW + 2 * CW
    cols = np.concatenate([np.concatenate([ar(oC + j * 128, oC + (j + 1) * 128), ar(oH + j * 128, oH + (j + 1) * 128),
                                           ar(oB + j * 128, oB + (j + 1) * 128)]) for j in range(NCC)])
    shared["wcv"] = _fm(w_in, cols)
    wcm = np.asarray(inp["w_conv_mix"][0], f32)
    shared["wcm"] = np.ascontiguousarray(wcm.reshape(3, NCC, 128).transpose(2, 1, 0).reshape(128, NCC * 3))
    shared["wout"] = _tm(np.asarray(inp["w_out"][0], f32))
    w_up = np.asarray(inp["w_up"][0], f32)
    cols = np.concatenate([np.concatenate([ar(j * 128, (j + 1) * 128), ar(DFF + j * 128, DFF + (j + 1) * 128)]) for j in range(FC)])
    shared["wup"] = _fm(w_up, cols)
    wcf = np.asarray(inp["w_conv_ffn"][0], f32)[:, cols]
    shared["wcf"] = np.ascontiguousarray(wcf.reshape(3, 2 * FC, 128).transpose(2, 1, 0).reshape(128, 2 * FC * 3))
    shared["wdn"] = _tm(np.asarray(inp["w_down"][0], f32))
    in_maps = []
    for b in range(B):
        for h in range(2):
            m = dict(shared)
            t0 = h * TOK
            if h == 0:
                xo = np.concatenate([x[b, 0:HW], x[b, 0:TOK]], axis=0)
                po = np.concatenate([pos[b, 0:HW], pos[b, 0:TOK]])
            else:
                xo = x[b, t0 - HW:t0 + TOK]
                po = pos[b, t0 - HW:t0 + TOK]
            m["x_own"] = np.ascontiguousarray(xo)
            m["x_pre"] = np.ascontiguousarray(x[b, 0:TOK])
            m["pos_own"] = np.ascontiguousarray(po)
            m["pos_pre"] = np.ascontiguousarray(pos[b, 0:TOK])
            m["cT"] = _featT(c[b])
            m["flag"] = np.full((128, 1), float(h), f32)
            in_maps.append(m)
    return in_maps


def run(cfg, inputs, trace=False):
    _UID[0] = 0
    nc = build(cfg)
    in_maps = prepare(cfg, inputs)
    n = len(in_maps)
    res = run_bass_kernel_spmd(nc, in_maps, core_ids=list(range(n)), **({"trace": True} if trace else {}))
    B, S, D = cfg["B"], cfg["S"], cfg["D"]
    TOK = S // 2
    out = np.empty((B, S, D), np.float32)
    for b in range(B):
        for h in range(2):
            out[b, h * TOK:(h + 1) * TOK] = res.results[b * 2 + h]["out"]
    return out, res


def kernel(**inputs):
    out, _ = run(FULL_CFG, inputs)
    return out
```

```python
import math
from contextlib import ExitStack
import numpy as np
import concourse.bass as bass
import concourse.mybir as mybir
from concourse.bass_utils import run_bass_kernel_spmd

F32 = mybir.dt.float32
BF16 = mybir.dt.bfloat16
I32 = mybir.dt.int32
AF = mybir.ActivationFunctionType
ALU = mybir.AluOpType
AX = mybir.AxisListType

EPS = 1e-6
ROPE_THETA = 10000.0
LAM_INIT = 0.8 - 0.6 * math.exp(-0.3 * 0)
FULL_CFG = dict(D=4096, S=4096, H=8, DFF=11008, B=4)


class Sem:
    def __init__(self, nc, name):
        self.h = nc.alloc_semaphore(name)
        self.n = 0
        self.name = name


class KB:
    def __init__(self, nc):
        self.nc = nc
        self.eng = {"pe": nc.tensor, "act": nc.scalar, "dve": nc.vector, "pool": nc.gpsimd, "sp": nc.sync}
        self.prog = {e: Sem(nc, "p_" + e) for e in ("pe", "act", "dve", "pool")}
        self.waited = {e: {} for e in self.eng}
        self.dsems = []
        self.nsem = 0

    def dsem(self, name):
        s = Sem(self.nc, f"d{self.nsem}_{name}")
        self.nsem += 1
        self.dsems.append(s)
        return s

    def mark(self, e, ins):
        s = self.prog[e]
        ins.then_inc(s.h, 1)
        s.n += 1
        return (s, s.n)

    def wait(self, e, tok):
        if tok is None:
            return
        if isinstance(tok, list):
            for t in tok:
                self.wait(e, t)
            return
        s, v = tok
        w = self.waited[e]
        if w.get(s.name, 0) >= v:
            return
        w[s.name] = v
        self.eng[e].wait_ge(s.h, v)

    def dma(self, q, out, in_, sem):
        ins = self.eng[q].dma_start(out=out, in_=in_)
        ins.then_inc(sem.h, 16)
        sem.n += 16
        return (sem, sem.n)

    def barrier(self, scratch):
        nc = self.nc
        toks = []
        toks.append(self.mark("act", nc.scalar.activation(out=scratch[:, 0:1], in_=scratch[:, 4:5], func=AF.Copy)))
        toks.append(self.mark("dve", nc.vector.memset(scratch[:, 1:2], 0.0)))
        toks.append(self.mark("pool", nc.gpsimd.memset(scratch[:, 2:3], 0.0)))
        toks.append((self.prog["pe"], self.prog["pe"].n))
        for s in self.dsems:
            if s.n:
                toks.append((s, s.n))
        for e in self.eng:
            self.wait(e, toks)


_UID = [0]


def sb(st, nc, name, shape, dt):
    _UID[0] += 1
    return st.enter_context(nc.sbuf_tensor(f"sb{_UID[0]}_{name}", list(shape), dt))


def psb(st, nc, name, shape, dt=F32):
    _UID[0] += 1
    return st.enter_context(nc.psum_tensor(f"ps{_UID[0]}_{name}", list(shape), dt))


def tok_tiles(n, first=None):
    out = []
    c = 0
    if first:
        out.append((0, first))
        c = first
    while c < n:
        w = min(512, n - c)
        out.append((c, w))
        c += w
    return out


def build(cfg):
    D, S, H, DFF = cfg["D"], cfg["S"], cfg["H"], cfg["DFF"]
    KC = D // 128
    TOK = S // 2
    NB = TOK // 128
    HW = 128
    HN = 32
    TOKL = HW + TOK
    NBL = NB + 1
    AW = H * 256
    CW = D - AW
    NCC = CW // 128
    FC = DFF // 128
    NTD = D // 512
    NTV = AW // 512
    NADA = 2 * D // 512
    NB_T = 4 * D // 128
    SCALE = 128.0 ** -0.5

    nc = bass.Bass("TRN2", target_bir_lowering=False)
    k = KB(nc)
    pe, act, dve, pool = nc.tensor, nc.scalar, nc.vector, nc.gpsimd

    def din(name, shape, dt=F32):
        return nc.dram_tensor(name, list(shape), dt, kind="ExternalInput").ap()

    DEBUG = cfg.get("debug", False)
    STOP = cfg.get("stop", 99)

    def dscr(name, shape, dt):
        return nc.dram_tensor(name, list(shape), dt, kind=("ExternalOutput" if DEBUG else "Internal")).ap()

    x_own = din("x_own", [TOKL, D])
    x_pre = din("x_pre", [TOK, D])
    pos_own = din("pos_own", [TOKL], I32)
    pos_pre = din("pos_pre", [TOK], I32)
    cT_d = din("cT", [128, KC])
    flag_d = din("flag", [128, 1])
    ident_d = din("ident", [128, 128])
    rotm_d = din("rotm", [128, 128])
    invf_d = din("invf", [128, 1])
    tri_d = din("tri", [128, 128])
    wada_d = din("wada", [NADA, 128, KC, 512])
    wadab_d = din("wadab", [NB_T, 128, KC, 128])
    badaT_d = din("badaT", [128, 6 * KC])
    gvec_d = din("gvec", [128, 4 * KC])
    lam_d = din("lamv", [4, 128])
    gsub_d = din("gsub", [256])
    wq_d = din("wq", [2 * H, 128, KC, 128])
    wk_d = din("wk", [2 * H, 128, KC, 128])
    wv_d = din("wv", [NTV, 128, KC, 512])
    wcv_d = din("wcv", [3 * NCC, 128, KC, 128])
    wcm_d = din("wcm", [128, NCC * 3])
    wout_d = din("wout", [NTD, 128, KC, 512])
    wup_d = din("wup", [2 * FC, 128, KC, 128])
    wcf_d = din("wcf", [128, 2 * FC * 3])
    wdn_d = din("wdn", [NTD, 128, FC, 512])
    out_d = nc.dram_tensor("out", [TOK, D], F32, kind="ExternalOutput").ap()

    kT_s = dscr("kT_s", [2 * H, 128, 2 * TOK], BF16)
    qT_s = dscr("qT_s", [2 * H, 128, TOKL], BF16)
    v_s = dscr("v_s", [2 * TOK, AW], BF16)
    def split_groups(nblk):
        ng = (nblk + 3) // 4
        base, rem = divmod(nblk, ng)
        sizes = [base + (1 if i < rem else 0) for i in range(ng)]
        out, b0 = [], 0
        for n_ in sizes:
            out.append((b0 * 128, n_ * 128))
            b0 += n_
        return out
    OG = split_groups(NBL)
    DG = split_groups(NB)
    mixT_s = dscr("mixT_s", [len(OG), 128, KC, 512], BF16)
    xmid_s = dscr("xmid_s", [TOK, D], F32)
    h2T_s = dscr("h2T_s", [KC, 128, TOKL], BF16)
    gT_s = dscr("gT_s", [len(DG), 128, FC, 512], BF16)
    ggrow_s = dscr("ggrow_s", [2, D], F32)
    cs_own_s = dscr("cs_own_s", [2, 128, TOKL], F32)
    cs_pre_s = dscr("cs_pre_s", [2, 128, TOK], F32)
    wout_c = dscr("wout_c", [NTD, 128, KC, 512], BF16)
    wdn_c = dscr("wdn_c", [NTD, 128, FC, 512], BF16)
    cache_sem = k.dsem("wcache")
    cache_jobs = []
    for nt_ in range(NTD):
        for k0 in range(0, KC, 8):
            cache_jobs.append((wout_c[nt_][:, k0:k0 + 8, :], wout_d[nt_][:, k0:k0 + 8, :]))

    def cache_step(n=1):
        for _ in range(n):
            if cache_jobs:
                dst_, src_ = cache_jobs.pop(0)
                k.dma("pool", dst_, src_, cache_sem)

    with ExitStack() as gst:
        P = lambda name, shape, dt=F32: sb(gst, nc, name, shape, dt)
        scratch = P("scratch", [128, 8])
        ident_f = P("ident_f", [128, 128])
        ident_b = P("ident_b", [128, 128], BF16)
        rotm_f = P("rotm_f", [128, 128])
        ones_f = P("ones_f", [128, 128])
        tri_b = P("tri_b", [128, 128], BF16)
        flag = P("flag_sb", [128, 1])
        invf = P("invf_sb", [128, 1])
        modA = P("modA", [128, 2 * KC])
        modB = P("modB", [128, 2 * KC])
        nlam = P("nlam", [128, 1])
        gsub = P("gsub_sb", [128, 256])
        wcm = P("wcm_sb", [128, NCC * 3])
        wcf = P("wcf_sb", [128, 2 * FC * 3])
        cond = P("cond", [128, KC], BF16)
        bada = P("bada", [128, 6 * KC])
        gvec = P("gvec", [128, 4 * KC])
        modT = P("modT", [128, 6 * KC])

        cs = k.dsem("const")
        toks = []
        toks.append(k.dma("sp", ident_f[:], ident_d, cs))
        toks.append(k.dma("sp", rotm_f[:], rotm_d, cs))
        toks.append(k.dma("sp", flag[:], flag_d, cs))
        toks.append(k.dma("sp", invf[:], invf_d, cs))
        toks.append(k.dma("sp", gsub[:], gsub_d.partition_broadcast(128), cs))
        toks.append(k.dma("sp", wcm[:], wcm_d, cs))
        toks.append(k.dma("sp", wcf[:], wcf_d, cs))
        ctok = toks[-1]
        k.wait("dve", ctok)
        k.mark("dve", dve.memset(scratch[:], 0.0))
        k.mark("dve", dve.tensor_copy(ident_b[:], ident_f[:]))
        k.mark("dve", dve.memset(ones_f[:], 1.0))
        k.mark("dve", dve.tensor_scalar(gsub[:], gsub[:], 1.0 - LAM_INIT, None, ALU.mult))

        with ExitStack() as st:
            A = lambda name, shape, dt=F32: sb(st, nc, name, shape, dt)
            tri_f = A("tri_f", [128, 128])
            lamb = A("lamb", [128, 4, 128])
            lprod = A("lprod", [128, 2, 128])
            lsum = A("lsum", [128, 4])
            c_sb = A("c_sb", [128, KC])
            wsl = [A(f"wada{i}", [128, KC, 512], BF16) for i in range(2)]
            pm = psb(st, nc, "pm", [128, 512])
            s0 = k.dsem("p0")
            t1 = k.dma("sp", tri_f[:], tri_d, s0)
            t2 = k.dma("sp", lamb[:], lam_d.partition_broadcast(128), s0)
            t3 = k.dma("sp", c_sb[:], cT_d, s0)
            t4 = k.dma("sp", bada[:], badaT_d, s0)
            t5 = k.dma("sp", gvec[:], gvec_d, s0)
            k.wait("dve", [t1, t2, t3, t4, t5])
            k.mark("dve", dve.tensor_copy(tri_b[:], tri_f[:]))
            k.mark("dve", dve.tensor_tensor(lprod[:, 0, :], lamb[:, 0, :], lamb[:, 1, :], ALU.mult))
            tl = k.mark("dve", dve.tensor_tensor(lprod[:, 1, :], lamb[:, 2, :], lamb[:, 3, :], ALU.mult))
            k.wait("dve", tl)
            k.mark("dve", dve.reduce_sum(lsum[:, 0:1], lprod[:, 0, :], AX.X))
            tl = k.mark("dve", dve.reduce_sum(lsum[:, 1:2], lprod[:, 1, :], AX.X))
            k.wait("act", tl)
            tl = k.mark("act", act.activation(out=lsum[:, 2:4], in_=lsum[:, 0:2], func=AF.Exp))
            k.wait("dve", tl)
            tl = k.mark("dve", dve.tensor_tensor(lsum[:, 0:1], lsum[:, 3:4], lsum[:, 2:3], ALU.subtract))
            k.wait("dve", tl)
            k.mark("dve", dve.tensor_scalar(nlam[:], lsum[:, 0:1], -LAM_INIT, None, ALU.add))
            k.wait("act", t3)
            tcond = k.mark("act", act.activation(out=cond[:], in_=c_sb[:], func=AF.Silu))
            wsem = [k.dsem(f"wada{i}") for i in range(2)]
            wfree = [None, None]
            wtok = {}

            def issue_ada(et):
                s = et % 2
                k.wait("pool", wfree[s])
                wtok[et] = k.dma("pool", wsl[s][:], wada_d[et], wsem[s])

            issue_ada(0)
            for ri, (pos_d, n, dst) in enumerate(((pos_own, TOKL, cs_own_s), (pos_pre, TOK, cs_pre_s))):
                pi_ = A(f"pi_{ri}", [128, n], I32)
                ang = A(f"ang{ri}", [128, n], F32)
                tmp = A(f"tmp{ri}", [128, n], F32)
                tabs = [A(f"tab{ri}_{i}", [128, n], F32) for i in range(2)]
                rs = k.dsem("rope")
                tpz = k.dma("sp", pi_[:], pos_d.partition_broadcast(128), rs)
                k.wait("dve", tpz)
                t = k.mark("dve", dve.tensor_copy(ang[:], pi_[:]))
                k.wait("dve", t)
                t = k.mark("dve", dve.tensor_scalar(ang[:], ang[:], invf[:, 0:1], None, ALU.mult))
                for ti, shift in enumerate((math.pi / 2, 0.0)):
                    tab = tabs[ti]
                    k.wait("dve", t)
                    t = k.mark("dve", dve.tensor_scalar(tab[:], ang[:], float(shift), None, ALU.add))
                    k.wait("dve", t)
                    t = k.mark("dve", dve.tensor_scalar(tmp[:], tab[:], float(1.0 / (2 * math.pi)), None, ALU.mult))
                    k.wait("dve", t)
                    t = k.mark("dve", dve.tensor_copy(pi_[:], tmp[:]))
                    k.wait("dve", t)
                    t = k.mark("dve", dve.tensor_copy(tmp[:], pi_[:]))
                    k.wait("dve", t)
                    t = k.mark("dve", dve.scalar_tensor_tensor(tab[:], tmp[:], -float(2 * math.pi), tab[:], ALU.mult, ALU.add))
                    k.wait("dve", t)
                    t = k.mark("dve", dve.tensor_scalar(tmp[:], tab[:], float(math.pi), float(2 * math.pi), ALU.is_gt, ALU.mult))
                    k.wait("dve", t)
                    t = k.mark("dve", dve.tensor_tensor(tab[:], tab[:], tmp[:], ALU.subtract))
                    k.wait("dve", t)
                    t = k.mark("dve", dve.tensor_scalar(tmp[:], tab[:], -float(math.pi), float(2 * math.pi), ALU.is_lt, ALU.mult))
                    k.wait("dve", t)
                    t = k.mark("dve", dve.tensor_tensor(tab[:], tab[:], tmp[:], ALU.add))
                    k.wait("act", t)
                    t2_ = k.mark("act", act.activation(out=tab[:], in_=tab[:], func=AF.Sin))
                    k.wait("sp", t2_)
                    k.dma("sp", dst[ti], tab[:], rs)
            k.wait("pe", tcond)
            lastp = None
            for et in range(NADA):
                if et + 1 < NADA:
                    issue_ada(et + 1)
                s = et % 2
                k.wait("pe", wtok[et])
                for jj in range(4):
                    j = et * 4 + jj
                    for kc in range(KC):
                        ins = pe.matmul(pm[:, j:j + 1], wsl[s][:, kc, jj * 128:(jj + 1) * 128], cond[:, kc:kc + 1],
                                        start=(kc == 0), stop=(kc == KC - 1))
                lastp = k.mark("pe", ins)
                wfree[s] = lastp
            k.wait("dve", lastp)
            tm = k.mark("dve", dve.tensor_tensor(modT[:, 0:2 * KC], pm[:, 0:2 * KC], bada[:, 0:2 * KC], ALU.add))
            k.wait("dve", tm)
            k.mark("dve", dve.tensor_copy(modB[:, 0:KC], modT[:, 0:KC]))
            k.mark("dve", dve.scalar_tensor_tensor(modA[:, 0:KC], modT[:, KC:2 * KC], 1.0, gvec[:, 0:KC], ALU.add, ALU.mult))
            k.barrier(scratch)

        BGS = dict(issued=0, done=0, free=[None, None], tok={}, last=None, slots=None, sems=None, pm2=None, guard=None)

        def bg_compute():
            j = BGS["done"]
            s_ = j % 2
            k.wait("pe", [BGS["tok"][j], BGS["guard"]()])
            for kc in range(KC):
                ins = pe.matmul(BGS["pm2"][:, j:j + 1], BGS["slots"][s_][:, kc, :], cond[:, kc:kc + 1], start=(kc == 0), stop=(kc == KC - 1))
            t = k.mark("pe", ins)
            BGS["free"][s_] = t
            BGS["last"] = t
            BGS["done"] += 1

        def ada_bg(n=1):
            for _ in range(n):
                i = BGS["issued"]
                if i < NB_T:
                    s_ = i % 2
                    k.wait("pool", BGS["free"][s_])
                    BGS["tok"][i] = k.dma("pool", BGS["slots"][s_][:], wadab_d[i], BGS["sems"][s_])
                    BGS["issued"] += 1
                if BGS["done"] < BGS["issued"] - 1:
                    bg_compute()

        def ada_finalize():
            with ExitStack() as st:
                A = lambda name, shape, dt=F32: sb(st, nc, name, shape, dt)
                ggT = A("ggT", [128, 2 * KC])
                diag = [A(f"diag{i}", [128, 128]) for i in range(2)]
                ggrow = A("ggrow", [1, D])
                pr = [psb(st, nc, f"pr{i}", [128, 512]) for i in range(2)]
                k.mark("dve", dve.tensor_copy(modB[:, KC:2 * KC], modT[:, 3 * KC:4 * KC]))
                k.mark("dve", dve.scalar_tensor_tensor(modA[:, KC:2 * KC], modT[:, 4 * KC:5 * KC], 1.0, gvec[:, 2 * KC:3 * KC], ALU.add, ALU.mult))
                k.mark("dve", dve.tensor_tensor(ggT[:, 0:KC], modT[:, 2 * KC:3 * KC], gvec[:, KC:2 * KC], ALU.mult))
                tg = k.mark("dve", dve.tensor_tensor(ggT[:, KC:2 * KC], modT[:, 5 * KC:6 * KC], gvec[:, 3 * KC:4 * KC], ALU.mult))
                k.wait("dve", tg)
                dfree = [None, None]
                pfree = [None, None]
                gs = k.dsem("ggrow")
                for i in range(2):
                    evs = []
                    for kc in range(KC):
                        s = kc % 2
                        k.wait("dve", dfree[s])
                        td = k.mark("dve", dve.tensor_scalar(diag[s][:], ident_f[:], ggT[:, i * KC + kc:i * KC + kc + 1], None, ALU.mult))
                        k.wait("pe", td)
                        k.wait("pe", pfree[s])
                        tp = k.mark("pe", pe.matmul(pr[s][0:1, 0:128], ones_f[:, 0:1], diag[s][:], start=True, stop=True))
                        dfree[s] = tp
                        k.wait("act", tp)
                        te = k.mark("act", act.activation(out=ggrow[0:1, kc * 128:(kc + 1) * 128], in_=pr[s][0:1, 0:128], func=AF.Copy))
                        pfree[s] = te
                        evs.append(te)
                    k.wait("sp", evs)
                    tdm = k.dma("sp", ggrow_s[i:i + 1, :], ggrow[0:1, :], gs)
                    k.wait("act", tdm)
                k.barrier(scratch)

        def nt_alloc(st, name):
            return dict(xn=[sb(st, nc, f"{name}_xn{i}", [128, D], BF16) for i in range(2)],
                        sm=[sb(st, nc, f"{name}_sm{i}", [128, 4], F32) for i in range(2)],
                        xn_free=[None, None], pt_free=[None, None], gi=0, cnt=0)

        def nt_block(N_, b, xap, rdy, done_cb, dstT, mi, pst, store_fn=None):
            s = N_["cnt"] % 2
            N_["cnt"] += 1
            xn, sm = N_["xn"], N_["sm"]
            k.wait("dve", N_["xn_free"][s])
            tz = k.mark("dve", dve.memset(sm[s][:], 0.0))
            k.wait("act", [rdy, tz, N_["xn_free"][s]])
            t1 = k.mark("act", act.activation(out=xn[s][:], in_=xap, func=AF.Square, accum_out=sm[s][:, 0:1]))
            k.wait("act", t1)
            t2 = k.mark("act", act.activation(out=sm[s][:, 1:2], in_=sm[s][:, 0:1], func=AF.Sqrt, scale=1.0 / D, bias=EPS))
            k.wait("dve", t2)
            t3 = k.mark("dve", dve.reciprocal(sm[s][:, 2:3], sm[s][:, 1:2]))
            k.wait("act", t3)
            t4 = k.mark("act", act.activation(out=xn[s][:], in_=xap, func=AF.Identity, scale=sm[s][:, 2:3]))
            done_cb(t4)
            ev_all = []
            for g0 in range(0, KC, 4):
                gi = N_["gi"]
                pt = pst[gi % 2]
                k.wait("pe", [N_["pt_free"][gi % 2], t4])
                n4 = min(4, KC - g0)
                for j in range(n4):
                    kc = g0 + j
                    ins = pe.transpose(pt[:, j * 128:(j + 1) * 128], xn[s][:, kc * 128:(kc + 1) * 128], ident_b[:])
                tp = k.mark("pe", ins)
                evs = []
                for j in range(n4):
                    kc = g0 + j
                    a_ap = modA[:, mi * KC + kc:mi * KC + kc + 1]
                    b_ap = modB[:, mi * KC + kc:mi * KC + kc + 1]
                    o_ap = dstT(kc, b)
                    if gi % 2 == 0:
                        k.wait("dve", tp)
                        evs.append(k.mark("dve", dve.tensor_scalar(o_ap, pt[:, j * 128:(j + 1) * 128], a_ap, b_ap, ALU.mult, ALU.add)))
                    else:
                        k.wait("act", tp)
                        evs.append(k.mark("act", act.activation(out=o_ap, in_=pt[:, j * 128:(j + 1) * 128], func=AF.Identity,
                                                                scale=a_ap, bias=b_ap)))
                N_["pt_free"][gi % 2] = evs
                ev_all += evs
                N_["gi"] += 1
            N_["xn_free"][s] = tp
            if store_fn is not None:
                store_fn(b, ev_all)

        def norm_transpose(st, name, nblk, x_dram, dstT, mi):
            NG = (KC + 3) // 4
            nbk = min(8, NG)
            pst = [psb(st, nc, f"{name}_pst{i}", [128, 1024], BF16) for i in range(nbk)]
            xb = [sb(st, nc, f"{name}_xb{i}", [128, D], F32) for i in range(2)]
            xn = [sb(st, nc, f"{name}_xn{i}", [128, D], BF16) for i in range(2)]
            sm = [sb(st, nc, f"{name}_sm{i}", [128, 4], F32) for i in range(2)]
            xsem = [k.dsem(f"{name}_x{i}") for i in range(2)]
            xfree = [None, None]
            xn_free = [None, None]
            pt_free = [None] * nbk
            ltok = {}

            def issue(b):
                s_ = b % 2
                k.wait("sp", xfree[s_])
                ltok[b] = k.dma("sp", xb[s_][:], x_dram[b * 128:(b + 1) * 128, :], xsem[s_])

            def front_a(b):
                s_ = b % 2
                k.wait("dve", [xn_free[s_], ltok[b]])
                tz = k.mark("dve", dve.memset(sm[s_][:], 0.0))
                k.wait("dve", tz)
                t1 = k.mark("dve", dve.scalar_tensor_tensor(xn[s_][:], xb[s_][:], 1.0, xb[s_][:], ALU.mult, ALU.mult,
                                                            accum_out=sm[s_][:, 0:1]))
                k.wait("act", t1)
                return k.mark("act", act.activation(out=sm[s_][:, 1:2], in_=sm[s_][:, 0:1], func=AF.Sqrt, scale=1.0 / D, bias=EPS))

            def front_b(b, t2):
                s_ = b % 2
                k.wait("dve", t2)
                return k.mark("dve", dve.reciprocal(sm[s_][:, 2:3], sm[s_][:, 1:2]))

            def front(b):
                return front_b(b, front_a(b))

            issue(0)
            if nblk > 1:
                issue(1)
            t3 = front(0)
            gcount = 0
            for b in range(nblk):
                s_ = b % 2
                k.wait("act", t3)
                t4 = k.mark("act", act.activation(out=xn[s_][:], in_=xb[s_][:], func=AF.Identity, scale=sm[s_][:, 2:3]))
                xfree[s_] = t4
                if b + 2 < nblk:
                    issue(b + 2)
                groups = []
                for g0 in range(0, KC, 4):
                    bk = gcount % nbk
                    gcount += 1
                    k.wait("pe", [pt_free[bk], t4])
                    n4 = min(4, KC - g0)
                    for j in range(n4):
                        kc = g0 + j
                        ins = pe.transpose(pst[bk][:, j * 128:(j + 1) * 128], xn[s_][:, kc * 128:(kc + 1) * 128], ident_b[:])
                    tp = k.mark("pe", ins)
                    groups.append((g0, n4, bk, tp))
                xn_free[s_] = tp
                if b + 1 < nblk:
                    t3 = front(b + 1)
                for gi_, (g0, n4, bk, tp) in enumerate(groups):
                    evs = []
                    for j in range(n4):
                        kc = g0 + j
                        a_ap = modA[:, mi * KC + kc:mi * KC + kc + 1]
                        b_ap = modB[:, mi * KC + kc:mi * KC + kc + 1]
                        o_ap = dstT(kc, b)
                        if gi_ % 2 == 0:
                            k.wait("dve", tp)
                            evs.append(k.mark("dve", dve.tensor_scalar(o_ap, pst[bk][:, j * 128:(j + 1) * 128], a_ap, b_ap, ALU.mult, ALU.add)))
                        else:
                            k.wait("act", tp)
                            evs.append(k.mark("act", act.activation(out=o_ap, in_=pst[bk][:, j * 128:(j + 1) * 128], func=AF.Identity,
                                                                    scale=a_ap, bias=b_ap)))
                    pt_free[bk] = evs[-1]

        def dram_loader(st, x_dram, nblk, name):
            xb = [sb(st, nc, f"{name}_xb{i}", [128, D], F32) for i in range(2)]
            sems = [k.dsem(f"{name}{i}") for i in range(2)]
            free = [None, None]
            toks = {}

            def issue(b):
                s = b % 2
                k.wait("sp", free[s])
                toks[b] = k.dma("sp", xb[s][:], x_dram[b * 128:(b + 1) * 128, :], sems[s])

            issue(0)

            def load_fn(b):
                if b + 1 < nblk:
                    issue(b + 1)
                s = b % 2

                def done(t):
                    free[s] = t
                return xb[s][:], toks[b], done
            return load_fn

        def fm_gemm(st, name, chunks, KCx, rhs_fn, tiles, banks, callback, nslots=3, flush=None, bg=0):
            wsl = [sb(st, nc, f"{name}_w{i}", [128, KCx, 128], BF16) for i in range(nslots)]
            wsem = [k.dsem(f"{name}_w{i}") for i in range(nslots)]
            wfree = [None] * nslots
            wtok = {}
            bfree = [None] * len(banks)

            def issue(ci):
                s = ci % nslots
                k.wait("pool", wfree[s])
                wtok[ci] = k.dma("pool", wsl[s][:], chunks[ci][0], wsem[s])
                if bg:
                    cache_step(bg)

            for ci in range(min(nslots - 1, len(chunks))):
                issue(ci)
            bi = 0
            pending = None
            for ci, (_, e) in enumerate(chunks):
                if ci + nslots - 1 < len(chunks):
                    issue(ci + nslots - 1)
                s = ci % nslots
                k.wait("pe", wtok[ci])
                for ti, (c0, w) in enumerate(tiles):
                    bk = bi % len(banks)
                    bi += 1
                    k.wait("pe", bfree[bk])
                    for kc in range(KCx):
                        ins = pe.matmul(banks[bk][:, 0:w], wsl[s][:, kc, :], rhs_fn(kc, c0, w), start=(kc == 0), stop=(kc == KCx - 1))
                    tp = k.mark("pe", ins)
                    if ti == len(tiles) - 1:
                        wfree[s] = tp
                    fr, defer = callback(ci, e, ti, c0, w, banks[bk], tp)
                    bfree[bk] = fr
                    if pending is not None:
                        pending()
                    pending = defer
            if pending is not None:
                pending()
            if flush is not None:
                flush()

        def tm_gemm(st, name, w_d, KCx, groups, lhs_fn, group_load, banks, callback, group_done=None, ksub=4, nring=4, NT=None, nt_hook=None, filler=None):
            ring = [sb(st, nc, f"{name}_r{i}", [128, ksub, 512], BF16) for i in range(nring)]
            rsem = [k.dsem(f"{name}_r{i}") for i in range(nring)]
            rfree = [None] * nring
            subs = [(k0, min(ksub, KCx - k0)) for k0 in range(0, KCx, ksub)]
            seq = [(g, nt, si) for g in range(len(groups)) for nt in range(NT) for si in range(len(subs))]
            rtok = {}

            def issue(i):
                g, nt, si = seq[i]
                k0, nk = subs[si]
                s = i % nring
                k.wait("pool", rfree[s])
                rtok[i] = k.dma("pool", ring[s][:, 0:nk, :], w_d[nt][:, k0:k0 + nk, :], rsem[s])

            for i in range(min(nring - 1, len(seq))):
                issue(i)
            bfree = [None] * len(banks)
            bnext = 0
            i = 0
            gtok = group_load(0) if group_load is not None else None
            for g, blks in enumerate(groups):
                k.wait("pe", gtok)
                for nt in range(NT):
                    mybanks = []
                    for _ in blks:
                        mybanks.append(bnext % len(banks))
                        bnext += 1
                    last = {}
                    for si, (k0, nk) in enumerate(subs):
                        if i + nring - 1 < len(seq):
                            issue(i + nring - 1)
                        s = i % nring
                        k.wait("pe", rtok[i])
                        for bi, blk in enumerate(blks):
                            bk = mybanks[bi]
                            if si == 0:
                                k.wait("pe", bfree[bk])
                            for kk in range(nk):
                                kc = k0 + kk
                                ins = pe.matmul(banks[bk][:, :], lhs_fn(kc, blk, bi), ring[s][:, kk, :], start=(kc == 0), stop=(kc == KCx - 1))
                            if si == len(subs) - 1:
                                last[bi] = k.mark("pe", ins)
                                if bi == len(blks) - 1:
                                    rfree[s] = last[bi]
                            else:
                                if bi == len(blks) - 1:
                                    rfree[s] = k.mark("pe", ins)
                                if filler is not None:
                                    filler(nt)
                        i += 1
                    for bi, blk in enumerate(blks):
                        bfree[mybanks[bi]] = callback(g, blk, bi, nt, banks[mybanks[bi]], last[bi])
                    if nt_hook is not None:
                        nt_hook(g, nt)
                if group_load is not None and g + 1 < len(groups):
                    gtok = group_load(g + 1)
                if group_done is not None:
                    group_done(g, blks)

        def qk_rope_pass(st, name, hT, ntok, tiles, cs_d, jobs):
            cosT = sb(st, nc, name + "_cos", [128, ntok], F32)
            sinT = sb(st, nc, name + "_sin", [128, ntok], F32)
            t32 = [sb(st, nc, f"{name}_t32{i}", [128, 512], F32) for i in range(2)]
            tA = [sb(st, nc, f"{name}_tA{i}", [128, 512], F32) for i in range(2)]
            tB = [sb(st, nc, f"{name}_tB{i}", [128, 512], F32) for i in range(2)]
            stage = [sb(st, nc, f"{name}_st{i}", [128, ntok], BF16) for i in range(2)]
            banks = [psb(st, nc, f"{name}_b{i}", [128, 512]) for i in range(4)]
            rb = [psb(st, nc, f"{name}_rb{i}", [128, 512]) for i in range(2)]
            csem = k.dsem(name + "_cs")
            tc1 = k.dma("sp", cosT[:], cs_d[0], csem)
            tc2 = k.dma("sp", sinT[:], cs_d[1], csem)
            ssem = [k.dsem(f"{name}_st{i}") for i in range(2)]
            state = dict(cnt=0, t32_free=[None, None], rb_free=[None, None], tA_free=[None, None], tB_free=[None, None],
                         st_free=[None, None], nch=0, sums=[])
            chunks = []
            for (w_d, cl, dst_fn) in jobs:
                chunks += [(w_d[e], dst_fn(e)) for e in cl]

            def cb(ci, dst, ti, c0, w, bank, tp):
                s = state["cnt"] % 2
                state["cnt"] += 1
                k.wait("act", [tp, state["t32_free"][s]])
                ta = k.mark("act", act.activation(out=t32[s][:, 0:w], in_=bank[:, 0:w], func=AF.Copy))

                def defer():
                    ss_ = state["nch"] % 2
                    k.wait("pe", [ta, state["rb_free"][s]])
                    tr = k.mark("pe", pe.matmul(rb[s][:, 0:w], rotm_f[:], t32[s][:, 0:w], start=True, stop=True))
                    k.wait("dve", [ta, tc1, state["tA_free"][s]])
                    t_a = k.mark("dve", dve.tensor_tensor(tA[s][:, 0:w], t32[s][:, 0:w], cosT[:, c0:c0 + w], ALU.mult))
                    k.wait("dve", [tr, tc2, state["tB_free"][s]])
                    t_b = k.mark("dve", dve.tensor_tensor(tB[s][:, 0:w], rb[s][:, 0:w], sinT[:, c0:c0 + w], ALU.mult))
                    state["rb_free"][s] = t_b
                    state["t32_free"][s] = [t_a, tr]
                    k.wait("dve", [t_a, t_b])
                    if ti == 0:
                        k.wait("dve", state["st_free"][ss_])
                    t_s = k.mark("dve", dve.tensor_tensor(stage[ss_][:, c0:c0 + w], tA[s][:, 0:w], tB[s][:, 0:w], ALU.add))
                    state["tA_free"][s] = t_s
                    state["tB_free"][s] = t_s
                    state["sums"].append(t_s)
                    if ti == len(tiles) - 1:
                        k.wait("sp", state["sums"])
                        state["sums"] = []
                        state["st_free"][ss_] = k.dma("sp", dst, stage[ss_][:], ssem[ss_])
                        state["nch"] += 1
                return ta, defer
            fm_gemm(st, name, chunks, KC, lambda kc, c0, w: hT[:, kc, c0:c0 + w], tiles, banks, cb, nslots=2)

        def v_pass(st, name, hT, blks_cols, dst_row0, use_flag):
            banks = [psb(st, nc, f"{name}_b{i}", [128, 512]) for i in range(8)]
            vst = [sb(st, nc, f"{name}_vs{i}", [128, 512], BF16) for i in range(4)]
            vsem = [k.dsem(f"{name}_vs{i}") for i in range(4)]
            vfree = [None] * 4
            cnt = [0]
            nb_ = len(blks_cols)
            groups = [list(range(g0, min(g0 + 6, nb_))) for g0 in range(0, nb_, 6)]

            def cb(g, blk, bi, nt, bank, tp):
                s = cnt[0] % 4
                cnt[0] += 1
                k.wait("act", [tp, vfree[s]])
                if use_flag:
                    te = k.mark("act", act.activation(out=vst[s][:], in_=bank[:, :], func=AF.Identity, scale=flag[:, 0:1]))
                else:
                    te = k.mark("act", act.activation(out=vst[s][:], in_=bank[:, :], func=AF.Copy))
                k.wait("sp", te)
                r0 = dst_row0 + blk * 128
                vfree[s] = k.dma("sp", v_s[r0:r0 + 128, nt * 512:(nt + 1) * 512], vst[s][:], vsem[s])
                return te
            tm_gemm(st, name, wv_d, KC, groups, lambda kc, blk, bi: hT[:, kc, blks_cols[blk]:blks_cols[blk] + 128], None, banks, cb,
                    ksub=8 if KC >= 8 else KC, nring=4, NT=NTV)

        if STOP < 1:
            return nc
        with ExitStack() as st1:
            hTp = sb(st1, nc, "hTp", [128, KC, TOK], BF16)
            with ExitStack() as st:
                norm_transpose(st, "p1n", NB, x_pre, lambda kc, b: hTp[:, kc, b * 128:(b + 1) * 128], 0)
                k.barrier(scratch)
            if STOP < 1.3:
                return nc
            with ExitStack() as st:
                qk_rope_pass(st, "p1k", hTp, TOK, tok_tiles(TOK), cs_pre_s,
                             [(wk_d, list(range(2 * H)), lambda e: kT_s[e][:, 0:TOK])])
                k.barrier(scratch)
            if STOP < 1.6:
                return nc
            with ExitStack() as st:
                v_pass(st, "p1v", hTp, [b * 128 for b in range(NB)], 0, True)
                k.barrier(scratch)

        if STOP < 2:
            return nc
        with ExitStack() as st2:
            hT = sb(st2, nc, "hT", [128, KC, TOKL], BF16)
            with ExitStack() as st:
                norm_transpose(st, "p2n", NBL, x_own, lambda kc, b: hT[:, kc, b * 128:(b + 1) * 128], 0)
                k.barrier(scratch)
            tiles_l = tok_tiles(TOKL, first=HW)
            with ExitStack() as st:
                qk_rope_pass(st, "p2qk", hT, TOKL, tiles_l, cs_own_s,
                             [(wq_d, list(range(2 * H)), lambda e: qT_s[e]),
                              (wk_d, list(range(2 * H)), lambda e: kT_s[e][:, TOK - HW:2 * TOK])])
                k.barrier(scratch)
            with ExitStack() as st:
                Cb = sb(st, nc, "cv_C", [128, TOKL], F32)
                mb = sb(st, nc, "cv_m", [128, TOKL + 2], F32)
                stg = [sb(st, nc, f"cv_st{i}", [128, TOKL], BF16) for i in range(2)]
                banks = [psb(st, nc, f"cv_b{i}", [128, 512]) for i in range(6)]
                ssem = [k.dsem(f"cv_st{i}") for i in range(2)]
                S_ = dict(C_free=None, m_free=None, st_free=[None, None], Ctoks=[], mtoks=[], Btoks=[], convtok=None)
                tiles_c = [(HW - HN, HN)] + tiles_l[1:]
                k.mark("dve", dve.memset(mb[:], 0.0))
                k.mark("dve", dve.memset(Cb[:], 0.0))
                k.mark("dve", dve.memset(stg[0][:], 0.0))
                tz = k.mark("dve", dve.memset(stg[1][:], 0.0))
                k.wait("act", tz)

                def cb(ci, e, ti, c0, w, bank, tp):
                    j, kind = divmod(e, 3)
                    if kind == 0:
                        k.wait("act", [tp, S_["C_free"]] if ti == 0 else tp)
                        t = k.mark("act", act.activation(out=Cb[:, c0:c0 + w], in_=bank[:, 0:w], func=AF.Copy))
                        S_["Ctoks"].append(t)
                        return t, None
                    if kind == 1:
                        k.wait("dve", [tp, tz] + S_["Ctoks"] + ([S_["m_free"]] if ti == 0 else []))
                        t = k.mark("dve", dve.tensor_tensor(mb[:, 2 + c0:2 + c0 + w], bank[:, 0:w], Cb[:, c0:c0 + w], ALU.mult))
                        S_["mtoks"].append(t)
                        if ti == len(tiles_l) - 1:
                            k.wait("dve", S_["mtoks"])
                            t0 = k.mark("dve", dve.tensor_scalar(mb[:, 2:2 + HW], mb[:, 2:2 + HW], flag[:, 0:1], None, ALU.mult))
                            k.wait("dve", t0)
                            wj = lambda tap: wcm[:, j * 3 + tap:j * 3 + tap + 1]
                            t1 = k.mark("dve", dve.tensor_scalar(Cb[:, :], mb[:, 2:2 + TOKL], wj(2), None, ALU.mult))
                            k.wait("dve", t1)
                            t2 = k.mark("dve", dve.scalar_tensor_tensor(Cb[:, :], mb[:, 1:1 + TOKL], wj(1), Cb[:, :], ALU.mult, ALU.add))
                            k.wait("dve", t2)
                            t3 = k.mark("dve", dve.scalar_tensor_tensor(Cb[:, :], mb[:, 0:TOKL], wj(0), Cb[:, :], ALU.mult, ALU.add))
                            S_["convtok"] = t3
                            S_["m_free"] = t3
                            S_["Ctoks"] = []
                            S_["mtoks"] = []
                        return t, None
                    s = j % 2
                    k.wait("dve", [tp, S_["convtok"]] + ([S_["st_free"][s]] if ti == 0 else []))
                    t = k.mark("dve", dve.tensor_tensor(stg[s][:, c0:c0 + w], bank[:, 0:w], Cb[:, c0:c0 + w], ALU.mult))
                    S_["Btoks"].append(t)
                    if ti == len(tiles_l) - 1:
                        k.wait("sp", S_["Btoks"])
                        for g_, (gs_, gn_) in enumerate(OG):
                            S_["st_free"][s] = k.dma("sp", mixT_s[g_][:, AW // 128 + j, 0:gn_], stg[s][:, gs_:gs_ + gn_], ssem[s])
                        S_["C_free"] = S_["Btoks"][-1]
                        S_["C_free"] = list(S_["Btoks"])
                        S_["Btoks"] = []
                    return t, None
                fm_gemm(st, "cv", [(wcv_d[e], e) for e in range(3 * NCC)], KC, lambda kc, c0, w: hT[:, kc, c0:c0 + w], tiles_c, banks, cb, bg=2)
                while cache_jobs:
                    cache_step()
                k.barrier(scratch)
            with ExitStack() as st:
                v_pass(st, "p2v", hT, [HW + b * 128 for b in range(NB)], TOK, False)
                k.barrier(scratch)

        if STOP < 3:
            return nc
        with ExitStack() as st:
            A = lambda name, shape, dt=F32: sb(st, nc, name, shape, dt)
            NKB = 2 * NB
            kTb = [A(f"at_k{i}", [128, 2, 2 * TOK], BF16) for i in range(2)]
            qTb = [A(f"at_q{i}", [128, 2, TOKL], BF16) for i in range(2)]
            va = [A(f"at_v{i}", [128, NKB, 257], BF16) for i in range(2)]
            mixst = [A(f"at_m{i}", [128, 2, TOKL], BF16) for i in range(2)]
            pT = [A(f"at_p{i}", [128, 512], BF16) for i in range(3)]
            Osb = [[A(f"at_o{c}{q}", [128, 257]) for q in range(4)] for c in range(2)]
            sm = [A(f"at_sm{q}", [128, 8]) for q in range(4)]
            ta_ = [A(f"at_ta{q}", [128, 256]) for q in range(4)]
            tb_ = [A(f"at_tb{q}", [128, 256]) for q in range(4)]
            ssall = A("at_ssall", [128, 12])
            ssall_free = [None]
            atb = [A(f"at_ab{q}", [128, 256], BF16) for q in range(4)]
            sbank = [psb(st, nc, f"at_sb{i}", [128, 512]) for i in range(3)]
            obank = [psb(st, nc, f"at_ob{i}", [128, 512]) for i in range(4)]
            tb_all = psb(st, nc, "at_tball", [128, 512], F32)
            tbank = [tb_all[:, 0:128].bitcast(BF16)]
            BGS["pm2"] = tb_all[:, 256:512]
            BGS["slots"] = [A(f"adab{i}", [128, KC, 128], BF16) for i in range(2)]
            BGS["sems"] = [k.dsem("adab0"), k.dsem("adab1")]
            BGS["guard"] = lambda: tfree[0]
            bgstep = [0]
            lsem = [k.dsem(f"at_ld{i}") for i in range(2)]
            msem = [k.dsem(f"at_ms{i}") for i in range(2)]
            for i in range(2):
                k.mark("pool", pool.memset(va[i][:, :, 256:257], 1.0))
                tv = k.mark("pool", pool.tensor_scalar(va[i][:, 0:NB, 256:257], va[i][:, 0:NB, 256:257], flag[:, 0:1], None, ALU.mult))
            head_free = [None, None]
            mix_free = [None, None]
            ltok = {}

            def issue_head(h):
                s = h % 2
                k.wait("sp", head_free[s])
                k.dma("sp", kTb[s][:], kT_s[2 * h:2 * h + 2].rearrange("c p t -> p c t"), lsem[s])
                k.dma("sp", qTb[s][:], qT_s[2 * h:2 * h + 2].rearrange("c p t -> p c t"), lsem[s])
                ltok[h] = k.dma("sp", va[s][:, :, 0:256], v_s[:, h * 256:(h + 1) * 256].rearrange("(kb p) e -> p kb e", p=128), lsem[s])

            issue_head(0)
            qtiles = [(0, HW, [NB - 1])]
            for (c0, w) in tok_tiles(TOK):
                qtiles.append((HW + c0, w, [NB + (c0 + i * 128) // 128 for i in range(w // 128)]))
            sfree = [None] * 3
            pfree = [None] * 3
            ofree = [None] * 4
            osb_free = [[None] * 4 for _ in range(2)]
            tfree = [None] * 1
            atb_free = [None] * 4
            comb_free = [None] * 4
            sidx = [0]
            epi_q = []
            st2_q = []
            for h in range(H):
                s = h % 2
                if h + 1 < H:
                    issue_head(h + 1)
                k.wait("pe", [ltok[h], tv])
                lastpe = None
                mixtoks = []
                for (c0, w, diags) in qtiles:
                    nqb = len(diags)
                    kmax = diags[-1]
                    for c in range(2):
                        pend = []
                        for kb in range(kmax + 1):
                            qb0 = 0
                            while diags[qb0] < kb:
                                qb0 += 1
                            qoff = qb0 * 128
                            r = sidx[0] % 3
                            sidx[0] += 1
                            k.wait("pe", [sfree[r]])
                            tS = k.mark("pe", pe.matmul(sbank[r][:, qoff:w], kTb[s][:, c, kb * 128:(kb + 1) * 128],
                                                        qTb[s][:, c, c0 + qoff:c0 + w], start=True, stop=True))
                            k.wait("act", [tS, pfree[r]])
                            tE = k.mark("act", act.activation(out=pT[r][:, qoff:w], in_=sbank[r][:, qoff:w], func=AF.Exp, scale=SCALE))
                            sfree[r] = tE
                            tP = tE
                            if kb in diags:
                                qd = diags.index(kb)
                                k.wait("dve", tE)
                                tP = k.mark("dve", dve.tensor_tensor(pT[r][:, qd * 128:(qd + 1) * 128], pT[r][:, qd * 128:(qd + 1) * 128],
                                                                     tri_b[:], ALU.mult))

                            def do_pv(kb=kb, qb0=qb0, r=r, tP=tP, tE=tE):
                                k.wait("pe", [tP, tE])
                                for qb in range(qb0, nqb):
                                    if kb == 0:
                                        k.wait("pe", ofree[qb])
                                    ins = pe.matmul(obank[qb][:, 0:257], pT[r][:, qb * 128:(qb + 1) * 128], va[s][:, kb, :],
                                                    start=(kb == 0), stop=(kb == diags[qb]))
                                    if kb == diags[qb]:
                                        to = k.mark("pe", ins)
                                        eng = "act" if qb % 2 == 0 else "dve"
                                        k.wait(eng, [to, osb_free[c][qb]])
                                        if eng == "act":
                                            te = k.mark("act", act.activation(out=Osb[c][qb][:], in_=obank[qb][:, 0:257], func=AF.Copy))
                                        else:
                                            te = k.mark("dve", dve.tensor_copy(Osb[c][qb][:], obank[qb][:, 0:257]))
                                        ofree[qb] = te
                                        osb_free[c][qb] = te
                                pfree[r] = k.mark("pe", ins) if kb != diags[nqb - 1] else (k.prog["pe"], k.prog["pe"].n)
                            pend.append(do_pv)
                            if len(pend) > 2:
                                pend.pop(0)()
                            bgstep[0] += 1
                            if bgstep[0] % 8 == 0:
                                ada_bg(1)
                            if kb == kmax // 2 and c == 0:
                                while st2_q:
                                    st2_q.pop(0)()
                            if kb == kmax and c == 0:
                                while st2_q:
                                    st2_q.pop(0)()
                                while epi_q:
                                    epi_q.pop(0)()
                        while pend:
                            pend.pop(0)()
                    k.wait("dve", ssall_free[0])
                    for qb in range(nqb):
                        O1, O2, m_ = Osb[0][qb], Osb[1][qb], sm[qb]
                        k.wait("dve", [osb_free[0][qb], osb_free[1][qb], comb_free[qb]])
                        t = k.mark("dve", dve.tensor_scalar(m_[:, 0:1], O1[:, 256:257], 1e-30, None, ALU.add))
                        t = k.mark("dve", dve.tensor_scalar(m_[:, 1:2], O2[:, 256:257], 1e-30, None, ALU.add))
                        k.wait("dve", t)
                        t = k.mark("dve", dve.reciprocal(m_[:, 2:4], m_[:, 0:2]))
                        k.wait("dve", t)
                        t = k.mark("dve", dve.tensor_tensor(m_[:, 3:4], m_[:, 3:4], nlam[:, 0:1], ALU.mult))
                        t1 = k.mark("dve", dve.tensor_scalar(ta_[qb][:], O1[:, 0:256], m_[:, 2:3], None, ALU.mult))
                        k.wait("dve", [t, t1])
                        t = k.mark("dve", dve.scalar_tensor_tensor(tb_[qb][:], O2[:, 0:256], m_[:, 3:4], ta_[qb][:], ALU.mult, ALU.add))
                        osb_free[0][qb] = t
                        osb_free[1][qb] = t
                        k.wait("dve", t)
                        t = k.mark("dve", dve.tensor_tensor(ta_[qb][:], tb_[qb][:], tb_[qb][:], ALU.mult))
                        k.wait("dve", t)
                        tss = k.mark("dve", dve.reduce_sum(ssall[:, qb:qb + 1], ta_[qb][:], AX.X))
                    fin = {}

                    def stage2(nqb=nqb, tss=tss, fin=fin):
                        k.wait("act", tss)
                        tl = k.mark("act", act.activation(out=ssall[:, 4:4 + nqb], in_=ssall[:, 0:nqb], func=AF.Ln, scale=1.0 / 256, bias=EPS))
                        k.wait("act", tl)
                        te_ = k.mark("act", act.activation(out=ssall[:, 8:8 + nqb], in_=ssall[:, 4:4 + nqb], func=AF.Exp, scale=-0.5))
                        k.wait("dve", te_)
                        for qb in range(nqb):
                            k.wait("dve", atb_free[qb])
                            t = k.mark("dve", dve.scalar_tensor_tensor(atb[qb][:], tb_[qb][:], ssall[:, 8 + qb:9 + qb], gsub[:], ALU.mult, ALU.mult))
                            comb_free[qb] = t
                            ssall_free[0] = t
                            fin[qb] = t
                    st2_q.append(stage2)
                    for qb in range(nqb):
                        def epi(qb=qb, fin=fin, c0=c0, s=s, mixtoks=mixtoks):
                            tb2 = 0
                            k.wait("pe", [fin[qb], tfree[tb2]])
                            for j in range(2):
                                ins = pe.transpose(tbank[tb2][:, j * 128:(j + 1) * 128], atb[qb][:, j * 128:(j + 1) * 128], ident_b[:])
                            tt = k.mark("pe", ins)
                            atb_free[qb] = tt
                            k.wait("act", [tt, mix_free[s]])
                            col = c0 + qb * 128
                            te = k.mark("act", act.activation(out=mixst[s][:, :, col:col + 128],
                                                              in_=tbank[tb2][:, 0:256].rearrange("p (j c) -> p j c", j=2), func=AF.Copy))
                            tfree[tb2] = te
                            mixtoks.append(te)
                        epi_q.append(epi)
                head_free[s] = (k.prog["pe"], k.prog["pe"].n)

                def store(h=h, s=s, mixtoks=mixtoks):
                    k.wait("sp", mixtoks)
                    for g_, (gs_, gn_) in enumerate(OG):
                        mix_free[s] = k.dma("sp", mixT_s[g_][:, 2 * h:2 * h + 2, 0:gn_], mixst[s][:, :, gs_:gs_ + gn_], msem[s])
                epi_q.append(store)
            while st2_q:
                st2_q.pop(0)()
            while epi_q:
                epi_q.pop(0)()
            while BGS["issued"] < NB_T:
                ada_bg(1)
            while BGS["done"] < BGS["issued"]:
                bg_compute()
            k.wait("dve", [BGS["last"], tfree[0]])
            tmz = k.mark("dve", dve.tensor_tensor(modT[:, 2 * KC:6 * KC], BGS["pm2"][:, 0:4 * KC], bada[:, 2 * KC:6 * KC], ALU.add))
            k.barrier(scratch)
        ada_finalize()

        def tm_phase(name, srcT_s, KCx, w_d, gsplit, ggi, resid_fn, final):
            GB = 4
            groups = [list(range(gs_ // 128, (gs_ + gn_) // 128)) for (gs_, gn_) in gsplit]
            nx = 1
            with ExitStack() as st:
                A = lambda nm, shape, dt=F32: sb(st, nc, f"{name}_{nm}", shape, dt)
                src = A("src", [128, KCx, GB * 128], BF16)
                yb = [A(f"yb{i}", [128, D]) for i in range(GB)]
                xb = [A(f"xb{i}", [128, D]) for i in range(nx)]
                ggrow = A("gg", [128, D])
                ssq = [A(f"ssq{i}", [128, NTD + 8]) for i in range(GB)]
                jk = A("jk", [128, 512], BF16)
                nbanks = 8 if final else 6
                banks = [psb(st, nc, f"{name}_b{i}", [128, 512]) for i in range(nbanks)]
                gsem = k.dsem(name + "_gg")
                tgg = k.dma("sp", ggrow[:], ggrow_s[ggi].partition_broadcast(128), gsem)
                ssem = k.dsem(name + "_src")
                xsem = [k.dsem(f"{name}_x{i}") for i in range(nx)]
                osem = [k.dsem(f"{name}_o{i}") for i in range(nx)]
                S_ = dict(yb_free=[None] * GB, xb_free=[None] * nx, xcnt=0, sq=[[] for _ in range(GB)])
                if not final:
                    pst = [psb(st, nc, f"{name}_pst{i}", [128, 1024], BF16) for i in range(2)]
                    h2st = [A(f"h2st{i}", [128, KC, 128], BF16) for i in range(2)]
                    hsem = [k.dsem(f"{name}_h{i}") for i in range(2)]
                    h2free = [None, None]
                    xnb = [A(f"xn{i}", [128, D], BF16) for i in range(GB)]
                    smb = [A(f"sm{i}", [128, 4]) for i in range(GB)]
                    xn_free = [None] * GB
                    pt_free = [None, None]
                    backs = []
                    gcnt = [0]

                def group_load(g):
                    n = len(groups[g]) * 128
                    k.wait("sp", (k.prog["pe"], k.prog["pe"].n))
                    return k.dma("sp", src[:, :, 0:n], srcT_s[g][:, :, 0:n], ssem)

                def cb(g, blk, bi, nt, bank, tp):
                    if nt == 0:
                        k.wait("dve", S_["yb_free"][bi])
                        tz = k.mark("dve", dve.memset(ssq[bi][:], 0.0))
                        k.wait("act", [tz, S_["yb_free"][bi]])
                    k.wait("dve", tp)
                    t1 = k.mark("dve", dve.tensor_copy(yb[bi][:, nt * 512:(nt + 1) * 512], bank[:, :]))
                    k.wait("act", t1)
                    t2 = k.mark("act", act.activation(out=jk[:], in_=yb[bi][:, nt * 512:(nt + 1) * 512], func=AF.Square,
                                                      accum_out=ssq[bi][:, nt:nt + 1]))
                    S_["sq"][bi] += [t1, t2]
                    return t1

                def group_done(g, blks):
                    if not final:
                        while backs:
                            for _ in backs.pop(0):
                                pass
                    for bi, blk in enumerate(blks):
                        m_ = ssq[bi]
                        xs = S_["xcnt"] % nx
                        S_["xcnt"] += 1
                        k.wait("sp", S_["xb_free"][xs])
                        tx = k.dma("sp", xb[xs][:], resid_fn(blk), xsem[xs])
                        k.wait("dve", S_["sq"][bi])
                        S_["sq"][bi] = []
                        t = k.mark("dve", dve.reduce_sum(m_[:, NTD:NTD + 1], m_[:, 0:NTD], AX.X))
                        k.wait("act", t)
                        t = k.mark("act", act.activation(out=m_[:, NTD + 1:NTD + 2], in_=m_[:, NTD:NTD + 1], func=AF.Sqrt, scale=1.0 / D, bias=EPS))
                        k.wait("dve", t)
                        t = k.mark("dve", dve.reciprocal(m_[:, NTD + 2:NTD + 3], m_[:, NTD + 1:NTD + 2]))
                        k.wait("dve", [t, tgg])
                        t = k.mark("dve", dve.scalar_tensor_tensor(yb[bi][:], yb[bi][:], m_[:, NTD + 2:NTD + 3], ggrow[:], ALU.mult, ALU.mult))
                        k.wait("dve", [t, tx])
                        t = k.mark("dve", dve.tensor_tensor(yb[bi][:], yb[bi][:], xb[xs][:], ALU.add))
                        S_["xb_free"][xs] = t
                        if final:
                            k.wait("sp", t)
                            S_["yb_free"][bi] = k.dma("sp", out_d[blk * 128:(blk + 1) * 128, :], yb[bi][:], osem[xs])
                        else:
                            stoks = []
                            if blk >= 1:
                                k.wait("sp", t)
                                stoks.append(k.dma("sp", xmid_s[(blk - 1) * 128:blk * 128, :], yb[bi][:], osem[xs]))

                            sm_ = smb[bi]
                            k.wait("dve", xn_free[bi])
                            tz = k.mark("dve", dve.memset(sm_[:], 0.0))
                            k.wait("act", [t, tz, xn_free[bi]])
                            t1 = k.mark("act", act.activation(out=xnb[bi][:], in_=yb[bi][:], func=AF.Square, accum_out=sm_[:, 0:1]))
                            k.wait("act", t1)
                            t2 = k.mark("act", act.activation(out=sm_[:, 1:2], in_=sm_[:, 0:1], func=AF.Sqrt, scale=1.0 / D, bias=EPS))
                            k.wait("dve", t2)
                            t3 = k.mark("dve", dve.reciprocal(sm_[:, 2:3], sm_[:, 1:2]))
                            k.wait("act", t3)
                            t4 = k.mark("act", act.activation(out=xnb[bi][:], in_=yb[bi][:], func=AF.Identity, scale=sm_[:, 2:3]))
                            S_["yb_free"][bi] = [t4] + stoks

                            def back(blk=blk, bi=bi, t4=t4):
                                hs = blk % 2
                                ev_all = []
                                for g0 in range(0, KC, 4):
                                    gi = gcnt[0]
                                    gcnt[0] += 1
                                    pt = pst[gi % 2]
                                    k.wait("pe", [pt_free[gi % 2], t4])
                                    n4 = min(4, KC - g0)
                                    for j in range(n4):
                                        kc = g0 + j
                                        ins = pe.transpose(pt[:, j * 128:(j + 1) * 128], xnb[bi][:, kc * 128:(kc + 1) * 128], ident_b[:])
                                    tp = k.mark("pe", ins)
                                    evs = []
                                    for j in range(n4):
                                        kc = g0 + j
                                        a_ap = modA[:, KC + kc:KC + kc + 1]
                                        b_ap = modB[:, KC + kc:KC + kc + 1]
                                        if g0 == 0 and j == 0:
                                            k.wait("dve", h2free[hs])
                                            k.wait("act", h2free[hs])
                                        o_ap = h2st[hs][:, kc, :]
                                        if gi % 2 == 0:
                                            k.wait("dve", tp)
                                            evs.append(k.mark("dve", dve.tensor_scalar(o_ap, pt[:, j * 128:(j + 1) * 128], a_ap, b_ap, ALU.mult, ALU.add)))
                                        else:
                                            k.wait("act", tp)
                                            evs.append(k.mark("act", act.activation(out=o_ap, in_=pt[:, j * 128:(j + 1) * 128], func=AF.Identity,
                                                                                    scale=a_ap, bias=b_ap)))
                                    pt_free[gi % 2] = evs[-1]
                                    ev_all += evs
                                    if g0 + 4 < KC:
                                        yield
                                xn_free[bi] = tp
                                k.wait("sp", ev_all)
                                h2free[hs] = k.dma("sp", h2T_s[:, :, blk * 128:(blk + 1) * 128].rearrange("kc p t -> p kc t"),
                                                   h2st[hs][:], hsem[hs])
                            backs.append(back())

                def filler(nt):
                    if final or nt < min(NTD // 2, NTD - 1):
                        return
                    while backs:
                        try:
                            next(backs[0])
                            return
                        except StopIteration:
                            backs.pop(0)

                tm_gemm(st, name, w_d, KCx, groups, lambda kc, blk, bi: src[:, kc, bi * 128:(bi + 1) * 128], group_load, banks, cb,
                        group_done=group_done, ksub=(2 if final else 4), nring=(6 if final else 4), NT=NTD, filler=filler)
                if not final:
                    while backs:
                        for _ in backs.pop(0):
                            pass
                k.barrier(scratch)

        if STOP < 4:
            return nc
        tm_phase("op", mixT_s, KC, wout_c, OG, 0,
                 lambda blk: x_own[blk * 128:(blk + 1) * 128, :], False)

        if STOP < 5:
            return nc
        with ExitStack() as st:
            A = lambda name, shape, dt=F32: sb(st, nc, name, shape, dt)
            h2T = A("h2T", [128, KC, TOKL], BF16)
            U = A("up_U", [128, TOKL + 2])
            CG = A("up_CG", [128, TOKL])
            CV = A("up_CV", [128, TOKL])
            gst_ = [A(f"up_g{i}", [128, TOK], BF16) for i in range(2)]
            banks = [psb(st, nc, f"up_b{i}", [128, 512]) for i in range(8)]
            hs_ = k.dsem("up_h2")
            th = k.dma("sp", h2T[:], h2T_s.rearrange("kc p t -> p kc t"), hs_)
            k.wait("pe", th)
            gsem = [k.dsem(f"up_g{i}") for i in range(2)]
            tiles_l = [(HW - HN, HN)] + tok_tiles(TOKL, first=HW)[1:]
            for nt_ in range(NTD):
                for k0 in range(0, FC, 22):
                    k1 = min(FC, k0 + 22)
                    cache_jobs.append((wdn_c[nt_][:, k0:k1, :], wdn_d[nt_][:, k0:k1, :]))
            tz = k.mark("dve", dve.memset(U[:], 0.0))
            S_ = dict(U_free=None, CG_free=None, CV_free=None, g_free=[None, None], ev=[])

            def cb(ci, e, ti, c0, w, bank, tp):
                j, kind = divmod(e, 2)
                k.wait("act", [tp, tz] + ([S_["U_free"]] if ti == 0 else []))
                if ti == 0:
                    t = k.mark("act", act.activation(out=U[:, 2 + c0:2 + c0 + w], in_=bank[:, 0:w], func=AF.Identity, scale=flag[:, 0:1]))
                else:
                    t = k.mark("act", act.activation(out=U[:, 2 + c0:2 + c0 + w], in_=bank[:, 0:w], func=AF.Copy))
                S_["ev"].append(t)
                if ti == len(tiles_l) - 1:
                    dst = CG if kind == 0 else CV
                    wj = lambda tap: wcf[:, e * 3 + tap:e * 3 + tap + 1]
                    k.wait("dve", S_["ev"] + [S_["CG_free"] if kind == 0 else S_["CV_free"]])
                    S_["ev"] = []
                    t1 = k.mark("dve", dve.tensor_scalar(dst[:, :], U[:, 2:2 + TOKL], wj(2), None, ALU.mult))
                    k.wait("dve", t1)
                    t2 = k.mark("dve", dve.scalar_tensor_tensor(dst[:, :], U[:, 1:1 + TOKL], wj(1), dst[:, :], ALU.mult, ALU.add))
                    k.wait("dve", t2)
                    t3 = k.mark("dve", dve.scalar_tensor_tensor(dst[:, :], U[:, 0:TOKL], wj(0), dst[:, :], ALU.mult, ALU.add))
                    S_["U_free"] = t3
                    if kind == 0:
                        k.wait("act", t3)
                        S_["sil"] = k.mark("act", act.activation(out=CG[:, HW:], in_=CG[:, HW:], func=AF.Silu))
                    else:
                        s = j % 2
                        k.wait("dve", [t3, S_["sil"], S_["g_free"][s]])
                        tg = k.mark("dve", dve.tensor_tensor(gst_[s][:], CG[:, HW:], CV[:, HW:], ALU.mult))
                        S_["CG_free"] = tg
                        S_["CV_free"] = tg
                        k.wait("sp", tg)
                        for g_, (gs_, gn_) in enumerate(DG):
                            S_["g_free"][s] = k.dma("sp", gT_s[g_][:, j, 0:gn_], gst_[s][:, gs_:gs_ + gn_], gsem[s])
                return t, None
            fm_gemm(st, "up", [(wup_d[e], e) for e in range(2 * FC)], KC, lambda kc, c0, w: h2T[:, kc, c0:c0 + w], tiles_l, banks, cb, nslots=2, bg=1)
            while cache_jobs:
                cache_step()
            k.barrier(scratch)

        if STOP < 6:
            return nc
        tm_phase("dn", gT_s, FC, wdn_c, DG, 1,
                 lambda blk: xmid_s[blk * 128:(blk + 1) * 128, :], True)
        k.barrier(scratch)
    return nc


def _fm(W, cols):
    K = W.shape[0]
    sub = W[:, cols]
    ne = sub.shape[1] // 128
    return np.ascontiguousarray(sub.reshape(K // 128, 128, ne, 128).transpose(2, 1, 0, 3))


def _tm(W):
    K, N = W.shape
    return np.ascontiguousarray(W.reshape(K // 128, 128, N // 512, 512).transpose(2, 1, 0, 3))


def _featT(v):
    return np.ascontiguousarray(v.reshape(-1, 128).T)


def prepare(cfg, inp):
    D, S, H, DFF, B = cfg["D"], cfg["S"], cfg["H"], cfg["DFF"], cfg["B"]
    KC = D // 128
    TOK = S // 2
    HW = 128
    AW = H * 256
    CW = D - AW
    NCC = CW // 128
    FC = DFF // 128
    f32 = np.float32
    x = np.asarray(inp["x"], f32)
    c = np.asarray(inp["c"], f32)
    pos = np.asarray(inp["positions"], np.int32)
    w_in = np.asarray(inp["w_in"][0], f32)
    ar = np.arange
    shared = {}
    shared["ident"] = np.eye(128, dtype=f32)
    rot = np.zeros((128, 128), f32)
    for do in range(64):
        rot[do + 64, do] = -1.0
    for do in range(64, 128):
        rot[do - 64, do] = 1.0
    shared["rotm"] = rot
    invf = (ROPE_THETA ** (-(np.arange(0, 128, 2, dtype=np.float32)) / np.float32(128))).astype(f32)
    shared["invf"] = np.concatenate([invf, invf]).reshape(128, 1).astype(f32)
    shared["tri"] = (ar(128)[:, None] <= ar(128)[None, :]).astype(f32)
    w_ada = np.asarray(inp["w_ada"][0], f32)
    shared["wada"] = _tm(w_ada[:, :2 * D])
    shared["wadab"] = _fm(w_ada, ar(2 * D, 6 * D))
    shared["badaT"] = _featT(np.asarray(inp["b_ada"][0], f32))
    shared["gvec"] = np.concatenate([_featT(np.asarray(inp[n][0], f32)) for n in ("g_pre_mix", "g_post_mix", "g_pre_ffn", "g_post_ffn")], axis=1)
    shared["lamv"] = np.stack([np.asarray(inp[n][0], f32) for n in ("lambda_q1", "lambda_k1", "lambda_q2", "lambda_k2")])
    shared["gsub"] = np.asarray(inp["g_subln"][0], f32)
    shared["wq"] = _fm(w_in, ar(0, AW))
    shared["wk"] = _fm(w_in, ar(AW, 2 * AW))
    shared["wv"] = _tm(w_in[:, 2 * AW:3 * AW])
    oB, oC, oH = 3 * AW, 3 * AW + CW, 3 * AW + 2 * CW
    cols = np.concatenate([np.concatenate([ar(oC + j * 128, oC + (j + 1) * 128), ar(oH + j * 128, oH + (j + 1) * 128),
                                           ar(oB + j * 128, oB + (j + 1) * 128)]) for j in range(NCC)])
    shared["wcv"] = _fm(w_in, cols)
    wcm = np.asarray(inp["w_conv_mix"][0], f32)
    shared["wcm"] = np.ascontiguousarray(wcm.reshape(3, NCC, 128).transpose(2, 1, 0).reshape(128, NCC * 3))
    shared["wout"] = _tm(np.asarray(inp["w_out"][0], f32))
    w_up = np.asarray(inp["w_up"][0], f32)
    cols = np.concatenate([np.concatenate([ar(j * 128, (j + 1) * 128), ar(DFF + j * 128, DFF + (j + 1) * 128)]) for j in range(FC)])
    shared["wup"] = _fm(w_up, cols)
    wcf = np.asarray(inp["w_conv_ffn"][0], f32)[:, cols]
    shared["wcf"] = np.ascontiguousarray(wcf.reshape(3, 2 * FC, 128).transpose(2, 1, 0).reshape(128, 2 * FC * 3))
    shared["wdn"] = _tm(np.asarray(inp["w_down"][0], f32))
    in_maps = []
    for b in range(B):
        for h in range(2):
            m = dict(shared)
            t0 = h * TOK
            if h == 0:
                xo = np.concatenate([x[b, 0:HW], x[b, 0:TOK]], axis=0)
                po = np.concatenate([pos[b, 0:HW], pos[b, 0:TOK]])
            else:
                xo = x[b, t0 - HW:t0 + TOK]
                po = pos[b, t0 - HW:t0 + TOK]
            m["x_own"] = np.ascontiguousarray(xo)
            m["x_pre"] = np.ascontiguousarray(x[b, 0:TOK])
            m["pos_own"] = np.ascontiguousarray(po)
            m["pos_pre"] = np.ascontiguousarray(pos[b, 0:TOK])
            m["cT"] = _featT(c[b])
            m["flag"] = np.full((128, 1), float(h), f32)
            in_maps.append(m)
    return in_maps


def run(cfg, inputs, trace=False):
    _UID[0] = 0
    nc = build(cfg)
    in_maps = prepare(cfg, inputs)
    n = len(in_maps)
    res = run_bass_kernel_spmd(nc, in_maps, core_ids=list(range(n)), **({"trace": True} if trace else {}))
    B, S, D = cfg["B"], cfg["S"], cfg["D"]
    TOK = S // 2
    out = np.empty((B, S, D), np.float32)
    for b in range(B):
        for h in range(2):
            out[b, h * TOK:(h + 1) * TOK] = res.results[b * 2 + h]["out"]
    return out, res


def kernel(**inputs):
    out, _ = run(FULL_CFG, inputs)
    return out
```
